# Optimizing a Trainium2 kernel written in Bass

```python
import jax, jax.numpy as jnp
from jax import lax
import numpy as np

D_MODEL = 1024
BATCH = 16
SEQ = 256
DEPTH = 1
DEC_BATCH = 4
DEC_SEQ = 4096
PAST_LEN = 256

GRID_W = 64
N_FOURIER_GROUPS = 4
FOURIER_GROUP_W = 128
D_FOURIER = N_FOURIER_GROUPS * FOURIER_GROUP_W
N_HEADS = 4
DK_HEAD = 128
DV_HEAD = 256
DK_TOT = N_HEADS * DK_HEAD
DV_TOT = N_HEADS * DV_HEAD
GATE_RANK = 16
GATE_TEMP = 16.0
CHUNK = 32
D_FF = 2816
N_MOD = 9
RMS_EPS = 1e-6
POS_BASE = 10000.0
IN_COLS = D_FOURIER + 2 * DK_TOT + 2 * DV_TOT + 2 * GATE_RANK + 2 * D_MODEL

kernel_name = "hybrid_fnet_gla_macaron_dit_step"


def rms_norm(x, gain):
    x32 = x.astype(jnp.float32)
    y = x32 * lax.rsqrt(jnp.mean(x32 * x32, axis=-1, keepdims=True) + RMS_EPS)
    return (y * gain.astype(jnp.float32)).astype(x.dtype)


def modulate(h, shift, scale):
    return h * (1.0 + scale) + shift


def swiglu(h, w_gate, w_up, w_down):
    return (jax.nn.silu(h @ w_gate) * (h @ w_up)) @ w_down


def grid_pos_embed(n_tokens, dtype):
    rows = n_tokens // GRID_W
    row = jnp.repeat(jnp.arange(rows, dtype=jnp.float32), GRID_W)
    col = jnp.tile(jnp.arange(GRID_W, dtype=jnp.float32), rows)
    n_freq = D_MODEL // 4
    omega = POS_BASE ** (-jnp.arange(n_freq, dtype=jnp.float32) / n_freq)
    ra = row[:, None] * omega
    ca = col[:, None] * omega
    return jnp.concatenate([jnp.sin(ra), jnp.cos(ra), jnp.sin(ca), jnp.cos(ca)], axis=-1).astype(dtype)


def fourier_mix(f):
    B, T, _ = f.shape
    g = f.astype(jnp.float32).reshape(B, T, N_FOURIER_GROUPS, FOURIER_GROUP_W).transpose(0, 2, 1, 3)
    m = jnp.fft.fft2(g, norm="ortho").real
    return m.transpose(0, 2, 1, 3).reshape(B, T, D_FOURIER)


def _to_chunks(x):
    B, T, H, d = x.shape
    return x.reshape(B, T // CHUNK, CHUNK, H, d).transpose(1, 0, 3, 2, 4)


def gla_chunked(q, k, v, log_a, s0):
    B, T = q.shape[0], q.shape[1]
    mask = jnp.tril(jnp.ones((CHUNK, CHUNK), dtype=bool))[:, :, None]

    def step(S, inp):
        qc, kc, vc, ac = inp
        b = jnp.cumsum(ac, axis=2)
        o_inter = jnp.einsum('bhid,bhdv->bhiv', qc * jnp.exp(b), S)
        diff = b[:, :, :, None, :] - b[:, :, None, :, :]
        decay = jnp.where(mask, jnp.exp(jnp.minimum(diff, 0.0)), 0.0)
        scores = jnp.einsum('bhid,bhjd,bhijd->bhij', qc, kc, decay)
        o_intra = jnp.einsum('bhij,bhjv->bhiv', scores, vc)
        b_last = b[:, :, -1, :]
        S_new = jnp.exp(b_last)[..., None] * S + jnp.einsum(
            'bhjd,bhjv->bhdv', kc * jnp.exp(b_last[:, :, None, :] - b), vc)
        return S_new, o_inter + o_intra

    s_final, o = lax.scan(step, s0.astype(jnp.float32),
                          (_to_chunks(q), _to_chunks(k), _to_chunks(v), _to_chunks(log_a)))
    o = o.transpose(1, 0, 3, 2, 4).reshape(B, T, N_HEADS, DV_HEAD)
    return o, s_final


def mixer(h, s_fwd, s_bwd, w_in, w_alpha_fwd, b_alpha_fwd, w_alpha_bwd, b_alpha_bwd,
          gla_norm, w_proj_fourier, w_proj_gla, w_out):
    B, T, _ = h.shape
    z = h @ w_in
    cuts = [D_FOURIER, D_FOURIER + DK_TOT, D_FOURIER + 2 * DK_TOT, D_FOURIER + 2 * DK_TOT + DV_TOT,
            D_FOURIER + 2 * DK_TOT + 2 * DV_TOT, D_FOURIER + 2 * DK_TOT + 2 * DV_TOT + 2 * GATE_RANK]
    f, q, k, v, r, a_lr, g = jnp.split(z, cuts, axis=-1)

    branch_a = fourier_mix(f).astype(h.dtype) @ w_proj_fourier

    qh = q.astype(jnp.float32).reshape(B, T, N_HEADS, DK_HEAD) * (DK_HEAD ** -0.5)
    kh = k.astype(jnp.float32).reshape(B, T, N_HEADS, DK_HEAD)
    vh = v.astype(jnp.float32).reshape(B, T, N_HEADS, DV_HEAD)
    a32 = a_lr.astype(jnp.float32)
    la_f = jax.nn.log_sigmoid(a32[..., :GATE_RANK] @ w_alpha_fwd.astype(jnp.float32)
                              + b_alpha_fwd.astype(jnp.float32)) / GATE_TEMP
    la_b = jax.nn.log_sigmoid(a32[..., GATE_RANK:] @ w_alpha_bwd.astype(jnp.float32)
                              + b_alpha_bwd.astype(jnp.float32)) / GATE_TEMP
    la_f = la_f.reshape(B, T, N_HEADS, DK_HEAD)
    la_b = la_b.reshape(B, T, N_HEADS, DK_HEAD)
    o_f, sf = gla_chunked(qh, kh, vh, la_f, s_fwd)
    o_b, sb = gla_chunked(jnp.flip(qh, 1), jnp.flip(kh, 1), jnp.flip(vh, 1), jnp.flip(la_b, 1), s_bwd)
    o = o_f + jnp.flip(o_b, 1)
    o = o * lax.rsqrt(jnp.mean(o * o, axis=-1, keepdims=True) + RMS_EPS)
    o = o.reshape(B, T, DV_TOT) * gla_norm.astype(jnp.float32)
    o = (o * jax.nn.silu(r.astype(jnp.float32))).astype(h.dtype)
    branch_b = o @ w_proj_gla

    g_a, g_b = jnp.split(jax.nn.sigmoid(g), 2, axis=-1)
    y = (g_a * branch_a + g_b * branch_b) @ w_out
    return y, sf, sb


def trunk_layer(x, mod, s_fwd, s_bwd, norm_ffn1, w_ffn1_gate, w_ffn1_up, w_ffn1_down,
                norm_mix, w_in, w_alpha_fwd, b_alpha_fwd, w_alpha_bwd, b_alpha_bwd, gla_norm,
                w_proj_fourier, w_proj_gla, w_out, norm_ffn2, w_ffn2_gate, w_ffn2_up, w_ffn2_down):
    sh1, sc1, gt1, sh2, sc2, gt2, sh3, sc3, gt3 = jnp.split(mod, N_MOD, axis=-1)
    h = modulate(rms_norm(x, norm_ffn1), sh1, sc1)
    x = x + (0.5 * gt1 * swiglu(h, w_ffn1_gate, w_ffn1_up, w_ffn1_down)).astype(x.dtype)
    h = modulate(rms_norm(x, norm_mix), sh2, sc2).astype(x.dtype)
    y, sf, sb = mixer(h, s_fwd, s_bwd, w_in, w_alpha_fwd, b_alpha_fwd, w_alpha_bwd, b_alpha_bwd,
                      gla_norm, w_proj_fourier, w_proj_gla, w_out)
    x = x + (gt2 * y).astype(x.dtype)
    h = modulate(rms_norm(x, norm_ffn2), sh3, sc3)
    x = x + (0.5 * gt3 * swiglu(h, w_ffn2_gate, w_ffn2_up, w_ffn2_down)).astype(x.dtype)
    return x, sf, sb


def setup_inputs(seed: int = 0) -> dict:
    key = jax.random.key(seed)
    ks = jax.random.split(key, 32)
    L, D = DEPTH, D_MODEL

    def nrm(k, shape, scale):
        return jax.random.normal(k, shape, jnp.float32) * scale

    def gain(k, shape):
        return 1.0 + 0.02 * jax.random.normal(k, shape, jnp.float32)

    st_shape = (DEC_BATCH, L, N_HEADS, DK_HEAD, DV_HEAD)
    return {
        "x_prompt": nrm(ks[0], (BATCH, SEQ, D), 1.0),
        "x_sample": nrm(ks[1], (DEC_BATCH, DEC_SEQ, D), 1.0),
        "state_gla_fwd": nrm(ks[2], st_shape, 1.0),
        "state_gla_bwd": nrm(ks[3], st_shape, 1.0),
        "c": nrm(ks[4], (DEC_BATCH, D), 1.0),
        "c_ctx": nrm(ks[5], (D,), 1.0),
        "w_ada": nrm(ks[6], (L, D, N_MOD * D), D ** -0.5),
        "b_ada": nrm(ks[7], (L, N_MOD * D), 0.02),
        "norm_ffn1": gain(ks[8], (L, D)),
        "w_ffn1_gate": nrm(ks[9], (L, D, D_FF), D ** -0.5),
        "w_ffn1_up": nrm(ks[10], (L, D, D_FF), D ** -0.5),
        "w_ffn1_down": nrm(ks[11], (L, D_FF, D), D_FF ** -0.5),
        "norm_mix": gain(ks[12], (L, D)),
        "w_in": nrm(ks[13], (L, D, IN_COLS), D ** -0.5),
        "w_alpha_fwd": nrm(ks[14], (L, GATE_RANK, DK_TOT), GATE_RANK ** -0.5),
        "b_alpha_fwd": nrm(ks[15], (L, DK_TOT), 0.1),
        "w_alpha_bwd": nrm(ks[16], (L, GATE_RANK, DK_TOT), GATE_RANK ** -0.5),
        "b_alpha_bwd": nrm(ks[17], (L, DK_TOT), 0.1),
        "gla_norm": gain(ks[18], (L, DV_TOT)),
        "w_proj_fourier": nrm(ks[19], (L, D_FOURIER, D), D_FOURIER ** -0.5),
        "w_proj_gla": nrm(ks[20], (L, DV_TOT, D), DV_TOT ** -0.5),
        "w_out": nrm(ks[21], (L, D, D), D ** -0.5),
        "norm_ffn2": gain(ks[22], (L, D)),
        "w_ffn2_gate": nrm(ks[23], (L, D, D_FF), D ** -0.5),
        "w_ffn2_up": nrm(ks[24], (L, D, D_FF), D ** -0.5),
        "w_ffn2_down": nrm(ks[25], (L, D_FF, D), D_FF ** -0.5),
        "final_norm": gain(ks[26], (D,)),
    }


def reference(x_prompt, x_sample, state_gla_fwd, state_gla_bwd, c, c_ctx, w_ada, b_ada,
              norm_ffn1, w_ffn1_gate, w_ffn1_up, w_ffn1_down, norm_mix, w_in,
              w_alpha_fwd, b_alpha_fwd, w_alpha_bwd, b_alpha_bwd, gla_norm,
              w_proj_fourier, w_proj_gla, w_out, norm_ffn2, w_ffn2_gate, w_ffn2_up, w_ffn2_down,
              final_norm):
    xc = x_prompt
    xl = x_sample + grid_pos_embed(x_sample.shape[1], x_sample.dtype)
    zero_state = jnp.zeros((x_prompt.shape[0], N_HEADS, DK_HEAD, DV_HEAD), jnp.float32)
    new_fwd, new_bwd = [], []
    for l in range(DEPTH):
        prm = (norm_ffn1[l], w_ffn1_gate[l], w_ffn1_up[l], w_ffn1_down[l], norm_mix[l], w_in[l],
               w_alpha_fwd[l], b_alpha_fwd[l], w_alpha_bwd[l], b_alpha_bwd[l], gla_norm[l],
               w_proj_fourier[l], w_proj_gla[l], w_out[l], norm_ffn2[l], w_ffn2_gate[l],
               w_ffn2_up[l], w_ffn2_down[l])
        mod_ctx = (jax.nn.silu(c_ctx) @ w_ada[l] + b_ada[l])[None, None, :]
        mod_lat = (jax.nn.silu(c) @ w_ada[l] + b_ada[l])[:, None, :]
        xc, sf, sb = trunk_layer(xc, mod_ctx, zero_state, zero_state, *prm)
        new_fwd.append(sf)
        new_bwd.append(sb)
        xl, _, _ = trunk_layer(xl, mod_lat, state_gla_fwd[:, l], state_gla_bwd[:, l], *prm)
    y_prompt = rms_norm(xc, final_norm)
    y_sample = rms_norm(xl, final_norm)
    new_state_fwd = jnp.stack(new_fwd, axis=1)
    new_state_bwd = jnp.stack(new_bwd, axis=1)
    return (y_prompt, y_sample, new_state_fwd, new_state_bwd)
```

```python
import math
from contextlib import ExitStack

import ml_dtypes
import numpy as np

import concourse.bass as bass
import concourse.mybir as mybir
from concourse.bass_utils import run_bass_kernel_spmd

F32 = mybir.dt.float32
BF16 = mybir.dt.bfloat16
AF = mybir.ActivationFunctionType
ALU = mybir.AluOpType

D = 1024
KC = 8
DFF = 2816
NFF = 22
T = 2560
NT = 5
NB = 20
TS = 2048
NCOLS_IN = 5664
EPS = 1e-6
NMODV = 72

FV_BADA = 0
FV_N1 = 144
FV_N2 = 152
FV_N3 = 160
FV_NF = 168
FV_GN = 176
FV_SEL = 184
FV_N = 186


class _Stop(Exception):
    pass


class Buf:
    __slots__ = ("w", "r")

    def __init__(self, seed=None):
        self.w = {}
        self.r = dict(seed) if seed else {}


class DSem:
    def __init__(self, h):
        self.h = h
        self.n = 0


class KB:
    ENG = ("pe", "act", "dve", "pool", "sp")

    def __init__(self, nc, es):
        self.nc = nc
        self.es = es
        self.q = {e: [] for e in self.ENG}
        self.sem = {e: es.enter_context(nc.semaphore("s_" + e)) for e in self.ENG}
        self.cnt = {e: 0 for e in self.ENG}
        self.waited = {e: {} for e in self.ENG}
        self.pend_r = {e: [] for e in self.ENG}
        self.pend_w = {e: [] for e in self.ENG}
        self.dsems = []
        self.semname = {}
        for e in self.ENG:
            self.semname[id(self.sem[e])] = e

    def dsem(self, name):
        self.nds = getattr(self, "nds", 0) + 1
        d = DSem(self.es.enter_context(self.nc.semaphore(f"d{self.nds}_{name}")))
        self.dsems.append(d)
        return d

    def snapshot(self):
        s = {}
        for e in self.ENG:
            if self.cnt[e]:
                s[id(self.sem[e])] = (self.sem[e], self.cnt[e])
        for d in self.dsems:
            if d.n:
                s[id(d.h)] = (d.h, d.n)
        return s

    def _need(self, eng, waits, ev):
        sem, val = ev
        k = id(sem)
        if eng == "pe" and sem is self.sem["pe"]:
            return
        if self.waited[eng].get(k, 0) >= val:
            return
        if k in waits and waits[k][1] >= val:
            return
        waits[k] = (sem, val)

    def _deps(self, eng, reads, writes):
        waits = {}
        for b in reads:
            for ev in b.w.values():
                self._need(eng, waits, ev)
        for b in writes:
            for ev in b.w.values():
                self._need(eng, waits, ev)
            for ev in b.r.values():
                self._need(eng, waits, ev)
        for k, (sem, val) in waits.items():
            self.waited[eng][k] = val
        return list(waits.values())

    def _commit(self, ev, reads, writes):
        k = id(ev[0])
        for b in reads:
            b.r[k] = ev
        for b in writes:
            b.w[k] = ev

    def op(self, eng, fn, reads=(), writes=(), inc=True):
        reads = list(reads)
        writes = list(writes)
        waits = self._deps(eng, reads, writes)
        if inc:
            self.cnt[eng] += 1
            ev = (self.sem[eng], self.cnt[eng])
            self._commit(ev, reads + self.pend_r[eng], writes + self.pend_w[eng])
            self.pend_r[eng] = []
            self.pend_w[eng] = []
            self.q[eng].append((waits, fn, (self.sem[eng], 1)))
        else:
            self.pend_r[eng] += reads
            self.pend_w[eng] += writes
            self.q[eng].append((waits, fn, None))

    def dma(self, queue, out, in_, dsem, reads=(), writes=(), **kw):
        reads = list(reads)
        writes = list(writes)
        waits = self._deps(queue, reads, writes)
        dsem.n += 16
        ev = (dsem.h, dsem.n)
        self._commit(ev, reads, writes)
        self.q[queue].append((waits, lambda e: e.dma_start(out=out, in_=in_, **kw), (dsem.h, 16)))

    def raw(self, queue, fn, dsem_inc, reads=(), writes=()):
        reads = list(reads)
        writes = list(writes)
        waits = self._deps(queue, reads, writes)
        d, n = dsem_inc
        d.n += n
        ev = (d.h, d.n)
        self._commit(ev, reads, writes)
        self.q[queue].append((waits, fn, (d.h, n)))

    def wait_all(self, eng):
        waits = []
        for k, (sem, val) in self.snapshot().items():
            if sem is self.sem[eng]:
                continue
            if self.waited[eng].get(k, 0) >= val:
                continue
            self.waited[eng][k] = val
            waits.append((sem, val))
        self.q[eng].append((waits, None, None))

    def replay(self, block):
        def run(eng):
            def f(e):
                for waits, fn, inc in self.q[eng]:
                    for sem, val in waits:
                        e.wait_ge(sem, val)
                    if fn is None:
                        continue
                    ins = fn(e)
                    if inc is not None:
                        ins.then_inc(inc[0], inc[1])
            return f

        block.tensor(run("pe"))
        block.scalar(run("act"))
        block.vector(run("dve"))
        block.gpsimd(run("pool"))
        block.sync(run("sp"))


class Arena:
    def __init__(self, kb, tens, nbytes):
        self.kb = kb
        self.t = tens
        self.top = 0
        self.cap = nbytes
        self.seed = None

    def mark(self):
        return self.top

    def release(self, mark):
        self.top = mark
        self.seed = self.kb.snapshot()

    def alloc(self, nbytes, dtype, pat=None, parts=128, **kw):
        assert nbytes % 4 == 0
        off = self.top
        self.top += (nbytes + 31) // 32 * 32
        assert self.top <= self.cap, f"SBUF arena overflow {self.top} > {self.cap}"
        ap = self.t[0:parts, off // 4:(off + nbytes) // 4]
        if dtype != F32:
            ap = ap.bitcast(dtype)
        if pat:
            ap = ap.rearrange(pat, **kw)
        return ap

    def buf(self):
        return Buf(self.seed)


class Ring:
    def __init__(self, ar, kb, name, n, nbytes, dtype, pat=None, parts=128, dma=True, **kw):
        self.aps = [ar.alloc(nbytes, dtype, pat, parts, **kw) for _ in range(n)]
        self.bufs = [ar.buf() for _ in range(n)]
        self.ds = [kb.dsem(f"{name}{i}") for i in range(n)] if dma else [None] * n
        self.i = -1
        self.n = n

    def next(self):
        self.i = (self.i + 1) % self.n
        return self.aps[self.i], self.bufs[self.i], self.ds[self.i]


def build_nc(debug=None):
    nc = bass.Bass("TRN2", target_bir_lowering=False)

    def din(name, shape, dt=F32):
        return nc.dram_tensor(name, list(shape), dt, kind="ExternalInput").ap()

    def dout(name, shape, dt=F32):
        return nc.dram_tensor(name, list(shape), dt, kind="ExternalOutput").ap()

    xin = din("xin", [T, D])
    posT = din("posT", [16, 128, KC, 128])
    sinit = din("sinit", [128, 4, 256])
    cT = din("cT", [128, KC, 2])
    fvec = din("fvec", [128, FV_N])
    w_ada = din("w_ada", [D, 9216])
    wg1 = din("w_ffn1_gate", [D, DFF]); wu1 = din("w_ffn1_up", [D, DFF]); wd1 = din("w_ffn1_down", [DFF, D])
    wg2 = din("w_ffn2_gate", [D, DFF]); wu2 = din("w_ffn2_up", [D, DFF]); wd2 = din("w_ffn2_down", [DFF, D])
    w_in = din("w_in", [D, NCOLS_IN])
    w_alr = din("w_alr", [D, 32])
    walpha = din("walpha", [2, 17, 512])
    w_pf = din("w_proj_fourier", [512, D])
    w_pg = din("w_proj_gla", [D, D])
    w_out = din("w_out", [D, D])
    ident_d = din("ident", [128, 128])
    tri_d = din("tri", [128, 4, 128])
    mask_d = din("maskc", [128, 2, 512], BF16)
    cs128_d = din("cs128", [128, 256], BF16)
    tabs_d = din("tabs", [32, 4, 128, 2, 512], BF16)
    tabp_d = din("tabp", [2, 128, 2, 256], BF16)

    yout = dout("yout", [T, D])
    st1 = dout("st1", [2, 128, 4, 256])
    st2 = dout("st2", [2, 128, 4, 256])
    dbg = dout("dbg", [128, 8 * T]) if debug else None
    dbgh = dout("dbgh", [128, 8 * T], BF16) if debug == "h" else None

    px_in = [nc.dram_tensor(f"px_in{i}", [512, 1024], BF16) for i in range(4)]
    px_out = [nc.dram_tensor(f"px_out{i}", [1024, 1024], BF16) for i in range(4)]
    pp = nc.dram_tensor("pp", [512, 1024], BF16)
    sx_in = nc.dram_tensor("sx_in", [128, 1024], F32)
    sx_out = nc.dram_tensor("sx_out", [256, 1024], F32)
    qkT = nc.dram_tensor("qkT", [NB, 128, 8, 128], BF16)
    kvt = nc.dram_tensor("kvt", [NB, 128, 1536], BF16)
    gsp = nc.dram_tensor("gsp", [24, 128, T], BF16)
    o1sp = nc.dram_tensor("o1sp", [NB, 128, 8, 128], F32)
    ogsp = nc.dram_tensor("ogsp", [NB, 128, 8, 128], BF16)

    es = ExitStack()
    with es:
        ARENA_BYTES = 212000
        arena_t = es.enter_context(nc.sbuf_tensor("arena", [128, ARENA_BYTES // 4], F32))
        ps = [es.enter_context(nc.psum_tensor(f"ps{i}", [128, 512], F32)) for i in range(8)]
        kb = KB(nc, es)
        ar = Arena(kb, arena_t, ARENA_BYTES)
        psb = [Buf() for _ in range(8)]
        block = es.enter_context(nc.Block())

        x = ar.alloc(KC * T * 4, F32, "p (m t) -> p m t", m=KC)
        xb = [[Buf() for _ in range(NB)] for _ in range(KC)]
        ident = ar.alloc(512, F32)
        ones = ar.alloc(256, BF16)
        tri = ar.alloc(2048, F32, "p (a b) -> p a b", a=4)
        trib = ar.alloc(1024, BF16, "p (a b) -> p a b", a=4)
        maskc = ar.alloc(2048, BF16, "p (a b) -> p a b", a=2)
        cs128 = ar.alloc(512, BF16)
        epsc = ar.alloc(32, F32)[:, 0:1]
        fv = ar.alloc(FV_N * 4, F32)
        modfm = ar.alloc(NMODV * 2 * 4, F32, "p (c v) -> p c v", v=2)
        sc = ar.alloc(9 * 16 * 4, F32)

        def scal(k, v, m):
            c = (k * 2 + v) * 8 + m
            return sc[:, c:c + 1]
        cbuf = Buf()
        d_const = kb.dsem("const")
        wslab = Ring(ar, kb, "ws", 4, 4096, BF16)

        def xbufs(m, n):
            return [xb[m][4 * n + i] for i in range(4)]

        def tsel(n):
            return 0 if n < 4 else 1

        for dst, src in ((ident, ident_d), (tri, tri_d), (maskc, mask_d), (cs128, cs128_d), (fv, fvec)):
            kb.dma("sp", dst, src, d_const, writes=[cbuf])
        kb.op("dve", lambda e: e.memset(ones, 1.0), writes=[cbuf])
        kb.op("dve", lambda e: e.memset(epsc, EPS), writes=[cbuf])
        kb.op("dve", lambda e: e.tensor_copy(out=trib, in_=tri), reads=[cbuf], writes=[cbuf])

        mkA = ar.mark()
        ctf = ar.alloc(KC * 2 * 4, F32, "p (k v) -> p k v", v=2)
        ctb = ar.alloc(KC * 2 * 2, BF16, "p (k v) -> p k v", v=2)
        ctbuf = ar.buf()
        d_ct = kb.dsem("ct")
        kb.dma("sp", ctf, cT, d_ct, writes=[ctbuf])
        kb.op("act", lambda e: e.activation(out=ctb, in_=ctf, func=AF.Silu), reads=[ctbuf], writes=[ctbuf])
        ADABANK = 7
        adar = Ring(ar, kb, "ada", 3, 4096, BF16)
        mstr = Ring(ar, kb, "mst", 2, 1024, F32, parts=2, dma=False)
        scb = Buf()
        ada_pending = []

        def ada_load(cb):
            slab, slb, sld = adar.next()
            sv = slab.rearrange("p (k c) -> p k c", k=KC)
            kb.dma("pool", sv, w_ada[:, cb * 256:(cb + 1) * 256].rearrange("(k p) c -> p k c", p=128), sld, writes=[slb])
            ada_pending.append((cb, sv, slb))

        def ada_compute():
            cb, sv, slb = ada_pending.pop(0)
            for kc in range(KC):
                kb.op("pe", lambda e, a=ctb[:, kc, :], r=sv[:, kc, :], kc=kc:
                      e.matmul(ps[ADABANK][0:2, 0:256], a, r, start=(kc == 0), stop=(kc == KC - 1)),
                      reads=[ctbuf, slb], writes=[psb[ADABANK]], inc=(kc == KC - 1))
            ms, msb, _ = mstr.next()
            kb.op("act", lambda e, o=ms: e.activation(out=o, in_=ps[ADABANK][0:2, 0:256], func=AF.Copy), reads=[psb[ADABANK]], writes=[msb])
            for j in range(2):
                kb.op("pe", lambda e, j=j, ms=ms: e.matmul(ps[ADABANK][:, 256 + 2 * j:258 + 2 * j], ms[:, j * 128:(j + 1) * 128], ident[0:2, 0:2], start=True, stop=True),
                      reads=[msb, cbuf], writes=[psb[ADABANK]], inc=(j == 1))
            kb.op("dve", lambda e, cb=cb: e.tensor_tensor(out=modfm[:, 2 * cb:2 * cb + 2, :], in0=ps[ADABANK][:, 256:260].rearrange("p (c v) -> p c v", v=2),
                                                         in1=fv[:, FV_BADA + 4 * cb:FV_BADA + 4 * cb + 4].rearrange("p (c v) -> p c v", v=2), op=ALU.add),
                  reads=[psb[ADABANK], cbuf], writes=[scb])

        def derive_scalars(li, norm_part=True, gate_part=True):
            fvn = (FV_N1, FV_N2, FV_N3)[li]
            base = li * 24
            for v in range(2):
                c_a = ((3 * li) * 2 + v) * 8
                c_s = ((3 * li + 1) * 2 + v) * 8
                c_g = ((3 * li + 2) * 2 + v) * 8
                if norm_part:
                    kb.op("dve", lambda e, o=sc[:, c_a:c_a + 8], a=modfm[:, base + 8:base + 16, v], g=fv[:, fvn:fvn + 8]:
                          e.scalar_tensor_tensor(out=o, in0=a, scalar=1.0, in1=g, op0=ALU.add, op1=ALU.mult),
                          reads=[scb, cbuf], writes=[scb])
                    kb.op("dve", lambda e, o=sc[:, c_s:c_s + 8], a=modfm[:, base:base + 8, v]:
                          e.tensor_copy(out=o, in_=a), reads=[scb], writes=[scb])
                if gate_part:
                    gsc = 1.0 if li == 1 else 0.5
                    kb.op("dve", lambda e, o=sc[:, c_g:c_g + 8], a=modfm[:, base + 16:base + 24, v], gsc=gsc:
                          e.tensor_scalar(out=o, in0=a, scalar1=gsc, scalar2=None, op0=ALU.mult),
                          reads=[scb], writes=[scb])

        NPRE = 8
        for cb in range(3):
            ada_load(cb)
        ada_ld = [3]

        mk = ar.mark()
        tokr = Ring(ar, kb, "tok", 3, 4096, F32)
        posr = Ring(ar, kb, "pos", 2, 4096, F32, "p (m t) -> p m t", m=KC)
        for b in range(NB):
            tok, tokb, tokd = tokr.next()
            kb.dma("sp", tok, xin[b * 128:(b + 1) * 128, :], tokd, writes=[tokb])
            if b < 16:
                pos, posb, posd = posr.next()
                kb.dma("sp", pos, posT[b], posd, writes=[posb])
            for hh in range(2):
                bank = hh
                for i in range(4):
                    m = hh * 4 + i
                    kb.op("pe", lambda e, o=ps[bank][:, i * 128:(i + 1) * 128], a=tok[:, m * 128:(m + 1) * 128]:
                          e.transpose(o, a, ident),
                          reads=[tokb, cbuf], writes=[psb[bank]], inc=(i == 3))
                pv = ps[bank][:, :].rearrange("p (a b) -> p a b", a=4)
                xo = x[:, hh * 4:hh * 4 + 4, b * 128:(b + 1) * 128]
                wr = [xb[hh * 4 + i][b] for i in range(4)]
                if b < 16:
                    kb.op("dve", lambda e, o=xo, a=pv, c=pos[:, hh * 4:hh * 4 + 4, :]:
                          e.tensor_tensor(out=o, in0=a, in1=c, op=ALU.add),
                          reads=[psb[bank], posb], writes=wr)
                else:
                    kb.op("act", lambda e, o=xo, a=pv: e.activation(out=o, in_=a, func=AF.Copy),
                          reads=[psb[bank]], writes=wr)
            if b < NPRE:
                ada_compute()
                if ada_ld[0] < NPRE + 3:
                    ada_load(ada_ld[0])
                    ada_ld[0] += 1
        ar.release(mk)

        derive_scalars(0, gate_part=False)
        ada_next = [NPRE + 3]
        gate1_done = [False]

        def ada_hook(k=2):
            if not gate1_done[0]:
                while ada_pending:
                    ada_compute()
                ada_load(ada_next[0])
                ada_next[0] += 1
                ada_compute()
                derive_scalars(0, norm_part=False)
                gate1_done[0] = True
            for _ in range(3):
                if ada_pending:
                    ada_compute()
            for _ in range(k):
                if ada_next[0] < 36:
                    ada_load(ada_next[0])
                    ada_next[0] += 1

        def rms_stats(n, sqr, rsr, ssbank):
            for m in range(KC):
                sq, sqb, _ = sqr.next()
                kb.op("act", lambda e, o=sq, a=x[:, m, n * 512:(n + 1) * 512]: e.activation(out=o, in_=a, func=AF.Square),
                      reads=xbufs(m, n), writes=[sqb])
                kb.op("pe", lambda e, a=sq, m=m: e.matmul(ps[ssbank][:, :], ones, a,
                                                          start=(m == 0), stop=(m == KC - 1)),
                      reads=[sqb, cbuf], writes=[psb[ssbank]], inc=True)
            rs, rsb, _ = rsr.next()
            kb.op("act", lambda e, o=rs: e.activation(out=o, in_=ps[ssbank][:, :], func=AF.Ln, scale=1.0 / D, bias=epsc),
                  reads=[psb[ssbank], cbuf], writes=[rsb])
            kb.op("act", lambda e, o=rs: e.activation(out=o, in_=o, func=AF.Exp, scale=-0.5), reads=[rsb], writes=[rsb])
            return rs, rsb

        def norm_mod(li, h, hb, sqr, rsr, tmr, ssbank, tiles=None):
            for n in (range(NT) if tiles is None else tiles):
                v = tsel(n)
                rs, rsb = rms_stats(n, sqr, rsr, ssbank)
                for m in range(KC):
                    tm, tmb, _ = tmr.next()
                    kb.op("dve", lambda e, o=tm, a=x[:, m, n * 512:(n + 1) * 512], s=scal(3 * li, v, m), r=rs:
                          e.scalar_tensor_tensor(out=o, in0=a, scalar=s, in1=r, op0=ALU.mult, op1=ALU.mult),
                          reads=xbufs(m, n) + [rsb, scb], writes=[tmb])
                    if m % 2 == 0:
                        kb.op("act", lambda e, o=h[:, m, n * 512:(n + 1) * 512], a=tm, s=scal(3 * li + 1, v, m):
                              e.activation(out=o, in_=a, func=AF.Identity, bias=s, scale=1.0),
                              reads=[tmb, scb], writes=[hb[m][n]])
                    else:
                        kb.op("dve", lambda e, o=h[:, m, n * 512:(n + 1) * 512], a=tm, s=scal(3 * li + 1, v, m):
                              e.tensor_scalar(out=o, in0=a, scalar1=s, scalar2=None, op0=ALU.add),
                              reads=[tmb, scb], writes=[hb[m][n]])

        def ffn(li, wg, wu, wd):
            mk = ar.mark()
            h = ar.alloc(KC * T * 2, BF16, "p (m t) -> p m t", m=KC)
            hb = [[ar.buf() for _ in range(NT)] for _ in range(KC)]
            sqr = Ring(ar, kb, "sq", 2, 1024, BF16, dma=False)
            rsr = Ring(ar, kb, "rs", 2, 2048, F32, dma=False)
            tmr = Ring(ar, kb, "tm", 3, 2048, F32, dma=False)
            Ar = Ring(ar, kb, "A", 2, 2 * T * 2, BF16, "p (j t) -> p j t", j=2, dma=False)
            if debug == "h" and li == 0:
                norm_mod(li, h, hb, sqr, rsr, tmr, 7)
            if debug == "h" and li == 0:
                d_dh = kb.dsem("dbgh")
                kb.dma("sp", dbgh, h.rearrange("p m t -> p (m t)"), d_dh, reads=[hb[m][n] for m in range(KC) for n in range(NT)])
                raise _Stop()
            NG = NFF // 2
            hall = [hb[m][n] for m in range(KC) for n in range(NT)]
            gk = 3 * li + 2

            def load_gu(g):
                sg, sgb, sgd = wslab.next()
                su, sub, sud = wslab.next()
                sgv = sg.rearrange("p (k c) -> p k c", k=KC)
                suv = su.rearrange("p (k c) -> p k c", k=KC)
                kb.dma("pool", sgv, wg[:, g * 256:(g + 1) * 256].rearrange("(k p) c -> p k c", p=128), sgd, writes=[sgb])
                kb.dma("pool", suv, wu[:, g * 256:(g + 1) * 256].rearrange("(k p) c -> p k c", p=128), sud, writes=[sub])
                return sgv, sgb, suv, sub

            def load_d(g):
                sd, sdb, sdd = wdr.next()
                sdv = sd.rearrange("p (j c) -> p j c", j=2)
                kb.dma("pool", sdv, wd[g * 256:(g + 1) * 256, :].rearrange("(j p) c -> p j c", p=128), sdd, writes=[sdb])
                return sdv, sdb

            wdr = Ring(ar, kb, "wd", 2, 4096, BF16)
            gu_next = load_gu(0)
            prev = None
            pbank = 0
            ybank = [0]

            def y_group(prev, m, n):
                pA, pAb, (sdv, sdb) = prev
                bk = 4 + ybank[0]
                ybank[0] = (ybank[0] + 1) % (3 if li == 0 else 4)
                for j in range(2):
                    kb.op("pe", lambda e, o=ps[bk][:, :], a=sdv[:, j, m * 128:(m + 1) * 128], r=pA[:, j, n * 512:(n + 1) * 512], j=j:
                          e.matmul(o, a, r, start=(j == 0), stop=(j == 1)),
                          reads=[sdb, pAb], writes=[psb[bk]], inc=(j == 1))
                xs = x[:, m, n * 512:(n + 1) * 512]
                kb.op("dve", lambda e, o=xs, a=ps[bk][:, :], s=scal(gk, tsel(n), m):
                      e.scalar_tensor_tensor(out=o, in0=a, scalar=s, in1=o, op0=ALU.mult, op1=ALU.add),
                      reads=[psb[bk], scb], writes=xbufs(m, n))

            for g in range(NG + 1):
                ylist = [(m, n) for m in range(KC) for n in range(NT)] if prev is not None else []
                if g < NG:
                    sgv, sgb, suv, sub = gu_next
                    dcur = load_d(g)
                    Aap, Ab, _ = Ar.next()
                    for j in range(2):
                        for n in range(NT):
                            if g == 0 and j == 0 and not (debug == "h" and li == 0):
                                norm_mod(li, h, hb, sqr, rsr, tmr, 7, tiles=[n])
                            bg, bu = 2 * pbank, 2 * pbank + 1
                            pbank ^= 1
                            for (bk, sv, sb_) in ((bg, sgv, sgb), (bu, suv, sub)):
                                for kc in range(KC):
                                    kb.op("pe", lambda e, o=ps[bk][:, :], a=sv[:, kc, j * 128:(j + 1) * 128], r=h[:, kc, n * 512:(n + 1) * 512], kc=kc:
                                          e.matmul(o, a, r, start=(kc == 0), stop=(kc == KC - 1)),
                                          reads=[sb_] + [hb[kc][n]], writes=[psb[bk]], inc=(kc == KC - 1))
                                if bk == bg:
                                    for _ in range(2):
                                        if ylist:
                                            y_group(prev, *ylist.pop(0))
                            tm, tmb, _ = tmr.next()
                            kb.op("act", lambda e, o=tm, a=ps[bg][:, :]: e.activation(out=o, in_=a, func=AF.Silu),
                                  reads=[psb[bg]], writes=[tmb])
                            kb.op("dve", lambda e, o=Aap[:, j, n * 512:(n + 1) * 512], a=tm, b=ps[bu][:, :]:
                                  e.tensor_tensor(out=o, in0=a, in1=b, op=ALU.mult),
                                  reads=[tmb, psb[bu]], writes=[Ab])
                            for _ in range(2):
                                if ylist:
                                    y_group(prev, *ylist.pop(0))
                    if g + 1 < NG:
                        gu_next = load_gu(g + 1)
                    if li == 0:
                        ada_hook()
                    cur = (Aap, Ab, dcur)
                else:
                    cur = None
                while ylist:
                    y_group(prev, *ylist.pop(0))
                prev = cur
            ar.release(mk)

        def dump_x(tag):
            if debug == tag:
                d_dbg = kb.dsem("dbg")
                kb.dma("sp", dbg, x.rearrange("p m t -> p (m t)"), d_dbg,
                       reads=[xb[m][b] for m in range(KC) for b in range(NB)])

        try:
            ffn(0, wg1, wu1, wd1)
        except _Stop:
            pass
        while ada_pending or ada_next[0] < 36:
            ada_hook(3)
        derive_scalars(1)
        derive_scalars(2)
        ar.release(mkA)
        dump_x("ffn1")

        def _mixer_and_rest():
            mkM = ar.mark()
            alr = [ar.alloc(T * 2, BF16, parts=17) for _ in range(2)]
            alrb = [ar.buf() for _ in range(2)]
            SQ = 1.0 / math.sqrt(128.0)

            if debug != "ffn1":
                mk = ar.mark()
                h2 = ar.alloc(KC * T * 2, BF16, "p (m t) -> p m t", m=KC)
                h2b = [[ar.buf() for _ in range(NT)] for _ in range(KC)]
                stg = Ring(ar, kb, "stg", 3, 1024, BF16)
                pstg = Ring(ar, kb, "pstg", 3, 2048, BF16, "p (b c) -> p b c", b=4)
                kvst = Ring(ar, kb, "kvst", 2, 3072, BF16)
                pre_slabs = []

                def slab_prefetch(wsrc, ncol):
                    slab, slb, sld = wslab.next()
                    sv = slab.rearrange("p (k c) -> p k c", k=KC)[:, :, 0:ncol]
                    kb.dma("pool", sv, wsrc.rearrange("(k p) c -> p k c", p=128), sld, writes=[slb])
                    return sv, slb

                pre_slabs.append(slab_prefetch(w_in[:, 0:256], 256))
                pre_slabs.append(slab_prefetch(w_in[:, 256:512], 256))
                mk3 = ar.mark()
                wkv = ar.alloc(KC * 1536 * 2, BF16, "p (k c) -> p k c", k=KC)
                wkvb = ar.buf()
                d_wkv = kb.dsem("wkv")
                for i in range(6):
                    c0 = 1024 + i * 256
                    kb.dma("pool", wkv[:, :, i * 256:(i + 1) * 256], w_in[:, c0:c0 + 256].rearrange("(k p) c -> p k c", p=128),
                           d_wkv, writes=[wkvb])
                mk2 = ar.mark()
                sqr = Ring(ar, kb, "sq", 2, 1024, BF16, dma=False)
                rsr = Ring(ar, kb, "rs", 2, 2048, F32, dma=False)
                tmr = Ring(ar, kb, "tm", 3, 2048, F32, dma=False)

                for d_ in range(2):
                    kb.op("dve", lambda e, o=alr[d_]: e.memset(o, 1.0), writes=[alrb[d_]])

                pp_b, qkT_b, kvt_b, gsp_b = Buf(), Buf(), Buf(), Buf()
                pxin_b = [Buf() for _ in range(4)]
                swb = [0]

                def fm_sweep(wsrc, ncol, epi, pre=None, pre_tile=None):
                    sv, slb = pre if pre is not None else slab_prefetch(wsrc, ncol)
                    nj = max(1, ncol // 128)
                    M = min(ncol, 128)
                    for j in range(nj):
                        for n in range(NT):
                            if pre_tile is not None and j == 0:
                                pre_tile(n)
                            bank = swb[0]
                            swb[0] = (swb[0] + 1) % 4
                            for kc in range(KC):
                                kb.op("pe", lambda e, o=ps[bank][0:M, :], a=sv[:, kc, j * M:(j + 1) * M], r=h2[:, kc, n * 512:(n + 1) * 512], kc=kc:
                                      e.matmul(o, a, r, start=(kc == 0), stop=(kc == KC - 1)),
                                      reads=[slb, h2b[kc][n]], writes=[psb[bank]], inc=(kc == KC - 1))
                            epi(j, n, bank, M)

                s1b = [0]

                pend_f = []

                def flush_f():
                    while pend_f:
                        g, n, fs, fsb = pend_f.pop(0)
                        pst, pstb, pstd = pstg.next()
                        for bl in range(4):
                            b2 = 4 + s1b[0]
                            s1b[0] = (s1b[0] + 1) % 4
                            kb.op("pe", lambda e, o=ps[b2][:, 0:256], a=fs[:, bl * 128:(bl + 1) * 128]:
                                  e.matmul(o, a, cs128, start=True, stop=True),
                                  reads=[fsb, cbuf], writes=[psb[b2]])
                            kb.op("dve", lambda e, o=pst[:, bl, :], a=ps[b2][:, 0:256]: e.tensor_copy(out=o, in_=a),
                                  reads=[psb[b2]], writes=[pstb])
                        dstt = px_in[n] if n < 4 else pp
                        kb.dma("sp", dstt[:, g * 256:(g + 1) * 256].rearrange("(b p) c -> p b c", p=128), pst, pstd,
                               reads=[pstb], writes=[pxin_b[n] if n < 4 else pp_b])

                def epi_f(gbase):
                    def epi(j, n, bank, M):
                        g = gbase + j
                        fs, fsb, _ = stg.next()
                        kb.op("act", lambda e, o=fs, a=ps[bank][:, :]: e.activation(out=o, in_=a, func=AF.Copy),
                              reads=[psb[bank]], writes=[fsb])
                        flush_f()
                        pend_f.append((g, n, fs, fsb))
                    return epi

                fm_sweep(w_in[:, 0:256], 256, epi_f(0), pre_slabs[0],
                         pre_tile=lambda n: norm_mod(1, h2, h2b, sqr, rsr, tmr, 7, tiles=[n]))
                ar.release(mk2)
                fm_sweep(w_in[:, 256:512], 256, epi_f(2), pre_slabs[1])
                flush_f()
                d_ccp = [kb.dsem(f"ccp{i}") for i in range(4)]
                pxout_b = [Buf() for _ in range(4)]

                def emit_cc(i):
                    kb.raw("pool", lambda e, i=i: e.collective_compute("AllGather", ALU.bypass,
                                                                  replica_groups=[[0, 1], [2, 3], [4, 5], [6, 7]],
                                                                  ins=[px_in[i].ap().opt()], outs=[px_out[i].ap().opt()]),
                           (d_ccp[i], 1), reads=[pxin_b[i]], writes=[pxout_b[i]])


                def epi_qk(idx0, scale):
                    def epi(j, n, bank, M):
                        st_, stb, std = stg.next()
                        kb.op("act", lambda e, o=st_, a=ps[bank][:, :]: e.activation(out=o, in_=a, func=AF.Identity, scale=scale),
                              reads=[psb[bank]], writes=[stb])
                        kb.dma("sp", qkT[n * 4:(n + 1) * 4, :, idx0 + j, :].rearrange("b p t -> p b t"),
                               st_.rearrange("p (b t) -> p b t", b=4), std, reads=[stb], writes=[qkT_b])
                    return epi

                fm_sweep(w_in[:, 512:768], 256, epi_qk(0, SQ))
                emit_cc(0)
                fm_sweep(w_in[:, 768:1024], 256, epi_qk(2, SQ))
                fm_sweep(w_in[:, 1024:1280], 256, epi_qk(4, 1.0))
                emit_cc(1)
                fm_sweep(w_in[:, 1280:1536], 256, epi_qk(6, 1.0))
                for b in range(NB):
                    kst, kstb, kstd = kvst.next()
                    for c3 in range(3):
                        bank = swb[0]
                        swb[0] = (swb[0] + 1) % 4
                        for kc in range(KC):
                            kb.op("pe", lambda e, o=ps[bank][:, :], a=h2[:, kc, b * 128:(b + 1) * 128], r=wkv[:, kc, c3 * 512:(c3 + 1) * 512], kc=kc:
                                  e.matmul(o, a, r, start=(kc == 0), stop=(kc == KC - 1)),
                                  reads=[wkvb, h2b[kc][b // 4]], writes=[psb[bank]], inc=(kc == KC - 1))
                        if c3 % 2 == 0:
                            kb.op("act", lambda e, o=kst[:, c3 * 512:(c3 + 1) * 512], a=ps[bank][:, :]: e.activation(out=o, in_=a, func=AF.Copy),
                                  reads=[psb[bank]], writes=[kstb])
                        else:
                            kb.op("dve", lambda e, o=kst[:, c3 * 512:(c3 + 1) * 512], a=ps[bank][:, :]: e.tensor_copy(out=o, in_=a),
                                  reads=[psb[bank]], writes=[kstb])
                    kb.dma("sp", kvt[b], kst, kstd, reads=[kstb], writes=[kvt_b])
                ar.release(mk3)

                def epi_alr(d_):
                    def epi(j, n, bank, M):
                        kb.op("act", lambda e, o=alr[d_][0:16, n * 512:(n + 1) * 512], a=ps[bank][0:16, :]:
                              e.activation(out=o, in_=a, func=AF.Copy), reads=[psb[bank]], writes=[alrb[d_]])
                    return epi

                fm_sweep(w_alr[:, 0:16], 16, epi_alr(0))
                fm_sweep(w_alr[:, 16:32], 16, epi_alr(1))

                def epi_gate(c0, func):
                    def epi(j, n, bank, M):
                        st_, stb, std = stg.next()
                        kb.op("act", lambda e, o=st_, a=ps[bank][:, :]: e.activation(out=o, in_=a, func=func),
                              reads=[psb[bank]], writes=[stb])
                        kb.dma("sp", gsp[c0 + j, :, n * 512:(n + 1) * 512], st_, std, reads=[stb], writes=[gsp_b])
                    return epi

                for i in range(4):
                    fm_sweep(w_in[:, 2560 + i * 256:2560 + (i + 1) * 256], 256, epi_gate(2 * i, AF.Silu))
                    if i in (0, 2):
                        emit_cc(2 + i // 2)
                for i in range(8):
                    fm_sweep(w_in[:, 3616 + i * 256:3616 + (i + 1) * 256], 256, epi_gate(8 + 2 * i, AF.Sigmoid))
                ar.release(mk)
                if debug == "E":
                    raise _Stop()

                mk = ar.mark()
                wal = ar.alloc(2 * 512 * 2, BF16, "p (a c) -> p a c", a=2, parts=17)
                walb = ar.buf()
                d_wal = kb.dsem("wal")
                kb.dma("pool", wal, walpha.rearrange("a p c -> p a c"), d_wal, writes=[walb])
                Sr = Ring(ar, kb, "S", 2, 4096, F32, "p (h v) -> p h v", dma=False, h=4)
                Sbfr = Ring(ar, kb, "sbf", 2, 2048, BF16, "p (h v) -> p h v", dma=False, h=4)
                qkr = Ring(ar, kb, "qk", 2, 2048, BF16, "p (a t) -> p a t", a=8)
                kvr = Ring(ar, kb, "kv", 4, 3072, BF16)
                ltr = Ring(ar, kb, "lt", 2, 1024, BF16, dma=False)
                e1r = Ring(ar, kb, "e1", 2, 2048, F32, dma=False)
                e2r = Ring(ar, kb, "e2", 2, 2048, F32, dma=False)
                e3r = Ring(ar, kb, "e3", 2, 2048, F32, dma=False)
                decr = Ring(ar, kb, "dec", 3, 32, F32, dma=False)
                qtr = Ring(ar, kb, "qt", 3, 1024, BF16, "p (h t) -> p h t", dma=False, h=4)
                ktr = Ring(ar, kb, "kt", 2, 1024, BF16, "p (h t) -> p h t", dma=False, h=4)
                khr = Ring(ar, kb, "kh", 2, 1024, BF16, dma=False)
                atr = Ring(ar, kb, "at", 2, 1024, BF16, dma=False)
                o1r = Ring(ar, kb, "o1", 2, 4096, F32, "p (c t) -> p c t", c=8)
                osr = Ring(ar, kb, "os", 2, 4096, F32, "p (c t) -> p c t", dma=False, c=8)
                oqr = Ring(ar, kb, "oq", 1, 2048, BF16, "p (c t) -> p c t", dma=False, c=8)
                rs2r = Ring(ar, kb, "rs2", 1, 2048, F32, dma=False)
                ogr = Ring(ar, kb, "og", 2, 2048, BF16, "p (c t) -> p c t", c=8)
                sx2 = ar.alloc(8192, F32, "p (r c) -> p r c", r=2)
                sx2b = ar.buf()
                o1sp_b, ogsp_b = Buf(), Buf()
                d_st = kb.dsem("st")
                st_b = Buf()
                sxin_b, sxout_b = Buf(), Buf()
                d_sx = kb.dsem("sx")
                d_ccs = kb.dsem("ccs")
                d_si = kb.dsem("sinit")
                d_sx2 = kb.dsem("sx2")
                cur = {}

                def set_state(S, Sb):
                    cur["S"], cur["Sb"] = S, Sb
                    sbf, sbfb, _ = Sbfr.next()
                    kb.op("act", lambda e, o=sbf, S=S: e.activation(out=o, in_=S, func=AF.Copy), reads=[Sb], writes=[sbfb])
                    cur["sbf"], cur["sbfb"] = sbf, sbfb

                steps = []
                for b in range(16):
                    steps.append(dict(b=b, dn=0, init="sinit" if b == 0 else None, save="xchg" if b == 15 else None))
                for sq_ in range(2):
                    b0 = 16 + 2 * sq_
                    steps.append(dict(b=b0, dn=0, init="zero", save=None))
                    steps.append(dict(b=b0 + 1, dn=0, init=None, save=st1[sq_]))
                for sq_ in range(2):
                    b0 = 16 + 2 * sq_
                    steps.append(dict(b=b0 + 1, dn=1, init="zero", save=None))
                    steps.append(dict(b=b0, dn=1, init=None, save=st2[sq_]))
                for b in range(15, -1, -1):
                    steps.append(dict(b=b, dn=1, init="exch" if b == 15 else None, save=None))
                NS = len(steps)

                def st_at(t, lag):
                    i = t - lag
                    return steps[i] if 0 <= i < NS else None

                for t in range(NS + 8):
                    c0, c1, c2, c3, c4, c5, c6, c7 = (st_at(t, k) for k in range(8))
                    if c6 is not None:
                        c = c6
                        b, dn = c["b"], c["dn"]
                        last = 127 if dn == 0 else 0
                        if c["init"] == "sinit":
                            S0, S0b, _ = Sr.next()
                            kb.dma("sp", S0, sinit, d_si, writes=[S0b])
                            set_state(S0, S0b)
                        elif c["init"] == "zero":
                            S0, S0b, _ = Sr.next()
                            kb.op("dve", lambda e, S0=S0: e.memset(S0, 0.0), writes=[S0b])
                            set_state(S0, S0b)
                        elif c["init"] == "exch":
                            kb.dma("sp", sx2, sx_out[:, :].rearrange("(r p) c -> p r c", p=128), d_sx2, reads=[sxout_b], writes=[sx2b])
                            S3, S3b, _ = Sr.next()
                            Sf = S3.rearrange("p h v -> p (h v)")
                            kb.op("dve", lambda e, Sf=Sf: e.tensor_scalar(out=Sf, in0=sx2[:, 0, :], scalar1=fv[:, FV_SEL:FV_SEL + 1], scalar2=None, op0=ALU.mult),
                                  reads=[sx2b, cbuf], writes=[S3b])
                            kb.op("dve", lambda e, Sf=Sf: e.scalar_tensor_tensor(out=Sf, in0=sx2[:, 1, :], scalar=fv[:, FV_SEL + 1:FV_SEL + 2], in1=Sf, op0=ALU.mult, op1=ALU.add),
                                  reads=[sx2b, cbuf], writes=[S3b])
                            set_state(S3, S3b)
                        S, Sb, sbf, sbfb = cur["S"], cur["Sb"], cur["sbf"], cur["sbfb"]
                        kv, kvb, qt, qtb, at, atb, dec, decb = (c[k] for k in ("kv", "kvb", "qt", "qtb", "at", "atb", "dec", "decb"))
                        S2, S2b, _ = Sr.next()
                        for h in range(4):
                            bank = 6 + h // 2
                            col = (h % 2) * 256
                            kb.op("dve", lambda e, bank=bank, col=col, h=h, S=S, S2=S2, dec=dec: e.scalar_tensor_tensor(out=S2[:, h, :], in0=S[:, h, :], scalar=dec[:, h:h + 1],
                                                                                              in1=ps[bank][:, col:col + 256], op0=ALU.mult, op1=ALU.add),
                                  reads=[Sb, decb, psb[bank]], writes=[S2b])
                        for h in range(4):
                            bank = 4 + h // 2
                            for cc_ in range(2):
                                col = ((h % 2) * 2 + cc_) * 128
                                v0 = 512 + h * 256 + cc_ * 128
                                kb.op("pe", lambda e, bank=bank, col=col, v0=v0, h=h, kv=kv, at=at: e.matmul(ps[bank][:, col:col + 128], kv[:, v0:v0 + 128], at[:, h * 128:(h + 1) * 128], start=True, stop=False),
                                      reads=[kvb, atb], writes=[psb[bank]], inc=False)
                                kb.op("pe", lambda e, bank=bank, col=col, cc_=cc_, h=h, sbf=sbf, qt=qt: e.matmul(ps[bank][:, col:col + 128], sbf[:, h, cc_ * 128:(cc_ + 1) * 128], qt[:, h, :], start=False, stop=True),
                                      reads=[sbfb, qtb], writes=[psb[bank]], inc=(h % 2 == 1 and cc_ == 1))
                    if c7 is not None and c7["dn"] == 1:
                        c = c7
                        oq, oqb, _ = oqr.next()
                        kb.op("act", lambda e, o=oq, a=c["os"]: e.activation(out=o, in_=a, func=AF.Square), reads=[c["osb"]], writes=[oqb])
                        c["oq"], c["oqb"] = oq, oqb
                    if c5 is not None:
                        c = c5
                        dn = c["dn"]
                        qt, qtb, kt, ktb, kh, khb, kv, kvb = (c[k] for k in ("qt", "qtb", "kt", "ktb", "kh", "khb", "kv", "kvb"))
                        for h in range(4):
                            kb.op("pe", lambda e, h=h, kt=kt, qt=qt: e.matmul(ps[3][:, h * 128:(h + 1) * 128], kt[:, h, :], qt[:, h, :], start=True, stop=True),
                                  reads=[ktb, qtb], writes=[psb[3]], inc=(h == 3))
                        at, atb, _ = atr.next()
                        kb.op("dve", lambda e, o=at, dn=dn: e.tensor_tensor(out=o, in0=ps[3][:, :], in1=maskc[:, dn, :], op=ALU.mult),
                              reads=[psb[3], cbuf], writes=[atb])
                        c["at"], c["atb"] = at, atb
                        for h in range(4):
                            bank = 6 + h // 2
                            col = (h % 2) * 256
                            kb.op("pe", lambda e, bank=bank, col=col, h=h, kh=kh, kv=kv: e.matmul(ps[bank][:, col:col + 256], kh[:, h * 128:(h + 1) * 128], kv[:, 512 + h * 256:512 + (h + 1) * 256], start=True, stop=True),
                                  reads=[khb, kvb], writes=[psb[bank]], inc=(h % 2 == 1))
                        o1, o1b, o1d = o1r.next()
                        c.update(o1=o1, o1b=o1b, o1d=o1d, o1_loaded=True)
                        if dn == 1:
                            kb.dma("sp", o1, o1sp[c["b"]], o1d, reads=[o1sp_b], writes=[o1b])
                    if c3 is not None:
                        c = c3
                        b = c["b"]
                        e1, e1b, _ = e1r.next()
                        e2, e2b, _ = e2r.next()
                        e3, e3b, _ = e3r.next()
                        kb.op("act", lambda e, o=e1: e.activation(out=o, in_=ps[1][:, :], func=AF.Exp), reads=[psb[1]], writes=[e1b])
                        kb.op("act", lambda e, o=e2: e.activation(out=o, in_=ps[1][:, :], func=AF.Exp, scale=-1.0), reads=[psb[1]], writes=[e2b])
                        kb.op("act", lambda e, o=e3: e.activation(out=o, in_=ps[2][:, :], func=AF.Exp), reads=[psb[2]], writes=[e3b])
                        qk, qkb, qkd = qkr.next()
                        kb.dma("sp", qk, qkT[b], qkd, reads=[qkT_b], writes=[qkb])
                        kv, kvb, kvd = kvr.next()
                        kb.dma("sp", kv, kvt[b], kvd, reads=[kvt_b], writes=[kvb])
                        c.update(e1=e1, e1b=e1b, e2=e2, e2b=e2b, e3=e3, e3b=e3b, qk=qk, qkb=qkb, kv=kv, kvb=kvb)
                    if c2 is not None:
                        c = c2
                        dn = c["dn"]
                        lt, ltb = c["lt"], c["ltb"]
                        for h in range(4):
                            kb.op("pe", lambda e, h=h, lt=lt, dn=dn: e.matmul(ps[1][:, h * 128:(h + 1) * 128], lt[:, h * 128:(h + 1) * 128], trib[:, 2 * dn, :], start=True, stop=True),
                                  reads=[ltb, cbuf], writes=[psb[1]], inc=(h == 3))
                        kb.op("pe", lambda e, lt=lt, dn=dn: e.matmul(ps[2][:, :], trib[:, 2 * dn + 1, :], lt, start=True, stop=True),
                              reads=[ltb, cbuf], writes=[psb[2]])
                    if c1 is not None:
                        c = c1
                        lt, ltb, _ = ltr.next()
                        kb.op("act", lambda e: e.activation(out=ps[0][:, :], in_=ps[0][:, :], func=AF.Exp, scale=-1.0), reads=[psb[0]], writes=[psb[0]])
                        kb.op("act", lambda e, o=lt: e.activation(out=o, in_=ps[0][:, :], func=AF.Ln, bias=1.0, scale=1.0), reads=[psb[0]], writes=[ltb])
                        c["lt"], c["ltb"] = lt, ltb
                    if c6 is not None:
                        c = c6
                        b, dn = c["b"], c["dn"]
                        set_state(S2, S2b)
                        if c["save"] == "xchg":
                            kb.dma("sp", sx_in[:, :], S2.rearrange("p h v -> p (h v)"), d_sx, reads=[S2b], writes=[sxin_b])
                            kb.raw("pool", lambda e: e.collective_compute("AllGather", ALU.bypass,
                                                                          replica_groups=[[0, 1], [2, 3], [4, 5], [6, 7]],
                                                                          ins=[sx_in.ap().opt()], outs=[sx_out.ap().opt()]),
                                   (d_ccs, 1), reads=[sxin_b], writes=[sxout_b])
                        elif c["save"] is not None:
                            kb.dma("sp", c["save"], S2, d_st, reads=[S2b], writes=[st_b])
                        if dn == 0:
                            o1, o1b, o1d = c["o1"], c["o1b"], c["o1d"]
                            kb.op("act", lambda e, o=o1[:, 0:4, :]: e.activation(out=o, in_=ps[4][:, :].rearrange("p (c t) -> p c t", c=4), func=AF.Copy),
                                  reads=[psb[4]], writes=[o1b])
                            kb.op("act", lambda e, o=o1[:, 4:8, :]: e.activation(out=o, in_=ps[5][:, :].rearrange("p (c t) -> p c t", c=4), func=AF.Copy),
                                  reads=[psb[5]], writes=[o1b])
                            kb.dma("sp", o1sp[b], o1, o1d, reads=[o1b], writes=[o1sp_b])
                        else:
                            os_, osb, _ = osr.next()
                            o1, o1b = c["o1"], c["o1b"]
                            if not c["o1_loaded"]:
                                kb.dma("sp", o1, o1sp[b], c["o1d"], reads=[o1sp_b], writes=[o1b])
                            for hh in range(2):
                                kb.op("dve", lambda e, hh=hh, o=os_[:, hh * 4:hh * 4 + 4, :], o1=o1: e.tensor_tensor(out=o, in0=ps[4 + hh][:, :].rearrange("p (c t) -> p c t", c=4),
                                                                                                         in1=o1[:, hh * 4:hh * 4 + 4, :], op=ALU.add),
                                      reads=[psb[4 + hh], o1b], writes=[osb])
                            c["os"], c["osb"] = os_, osb
                    if c4 is not None:
                        c = c4
                        dn = c["dn"]
                        last = 127 if dn == 0 else 0
                        qk, qkb, kv, kvb, e1, e1b, e2, e2b, e3, e3b = (c[k] for k in ("qk", "qkb", "kv", "kvb", "e1", "e1b", "e2", "e2b", "e3", "e3b"))
                        qt, qtb, _ = qtr.next()
                        kt, ktb, _ = ktr.next()
                        kh, khb, _ = khr.next()
                        dec, decb, _ = decr.next()
                        kb.op("dve", lambda e, o=qt, qk=qk, e1=e1: e.tensor_tensor(out=o, in0=qk[:, 0:4, :], in1=e1.rearrange("p (h t) -> p h t", h=4), op=ALU.mult),
                              reads=[qkb, e1b], writes=[qtb])
                        kb.op("dve", lambda e, o=kt, qk=qk, e2=e2: e.tensor_tensor(out=o, in0=qk[:, 4:8, :], in1=e2.rearrange("p (h t) -> p h t", h=4), op=ALU.mult),
                              reads=[qkb, e2b], writes=[ktb])
                        kb.op("dve", lambda e, o=kh, kv=kv, e3=e3: e.tensor_tensor(out=o, in0=kv[:, 0:512], in1=e3, op=ALU.mult),
                              reads=[kvb, e3b], writes=[khb])
                        kb.op("dve", lambda e, o=dec[:, 0:4], e1=e1, last=last: e.tensor_copy(out=o, in_=e1.rearrange("p (h t) -> p h t", h=4)[:, :, last]),
                              reads=[e1b], writes=[decb])
                        c.update(qt=qt, qtb=qtb, kt=kt, ktb=ktb, kh=kh, khb=khb, dec=dec, decb=decb)
                    if c0 is not None:
                        c = c0
                        b, dn = c["b"], c["dn"]
                        kb.op("pe", lambda e, b=b, dn=dn: e.matmul(ps[0][:, :], alr[dn][0:17, b * 128:(b + 1) * 128], wal[:, dn, :], start=True, stop=True),
                              reads=[alrb[dn], walb], writes=[psb[0]])
                    if c7 is not None and c7["dn"] == 1:
                        c = c7
                        oq, oqb, os_, osb = c["oq"], c["oqb"], c["os"], c["osb"]
                        for h in range(4):
                            for cc_ in range(2):
                                kb.op("pe", lambda e, h=h, cc_=cc_, oq=oq: e.matmul(ps[3][:, h * 128:(h + 1) * 128], ones, oq[:, 2 * h + cc_, :], start=(cc_ == 0), stop=(cc_ == 1)),
                                      reads=[oqb, cbuf], writes=[psb[3]], inc=(h == 3 and cc_ == 1))
                        rs2, rs2b, _ = rs2r.next()
                        kb.op("act", lambda e, o=rs2: e.activation(out=o, in_=ps[3][:, :], func=AF.Ln, scale=1.0 / 256.0, bias=epsc),
                              reads=[psb[3], cbuf], writes=[rs2b])
                        kb.op("act", lambda e, o=rs2: e.activation(out=o, in_=o, func=AF.Exp, scale=-0.5), reads=[rs2b], writes=[rs2b])
                        os4 = os_.rearrange("p (h c) t -> p h c t", h=4)
                        rsb4 = rs2.rearrange("p (h t) -> p h t", h=4).unsqueeze(2).broadcast_to([128, 4, 2, 128])
                        og, ogb, ogd = ogr.next()
                        kb.op("dve", lambda e, o=og.rearrange("p (h c) t -> p h c t", h=4), a=os4, r=rsb4: e.tensor_tensor(out=o, in0=a, in1=r, op=ALU.mult),
                              reads=[osb, rs2b], writes=[ogb])
                        kb.dma("sp", ogsp[c["b"]], og, ogd, reads=[ogb], writes=[ogsp_b])
                ar.release(mkM)
                if debug == "F":
                    raise _Stop()

                mk = ar.mark()
                mT = ar.alloc(4 * T * 2, BF16, "p (g t) -> p g t", g=4)
                mTb = ar.buf()
                wpf = ar.alloc(4 * 1024 * 2, BF16, "p (g c) -> p g c", g=4)
                wpg = ar.alloc(8 * 1024 * 2, BF16, "p (k c) -> p k c", k=8)
                wo = ar.alloc(8 * 1024 * 2, BF16, "p (k c) -> p k c", k=8)
                wpb = ar.buf()
                d_wp = kb.dsem("wp")
                for i in range(2):
                    kb.dma("pool", wpf[:, :, i * 512:(i + 1) * 512], w_pf[:, i * 512:(i + 1) * 512].rearrange("(g p) c -> p g c", p=128), d_wp, writes=[wpb])
                for i in range(4):
                    kb.dma("pool", wpg[:, :, i * 256:(i + 1) * 256], w_pg[:, i * 256:(i + 1) * 256].rearrange("(k p) c -> p k c", p=128), d_wp, writes=[wpb])
                for i in range(4):
                    kb.dma("pool", wo[:, :, i * 256:(i + 1) * 256], w_out[:, i * 256:(i + 1) * 256].rearrange("(k p) c -> p k c", p=128), d_wp, writes=[wpb])
                for c8 in range(8):
                    kb.op("act", lambda e, c8=c8: e.activation(out=wpg[:, c8, :], in_=wpg[:, c8, :], func=AF.Identity, scale=fv[:, FV_GN + c8:FV_GN + c8 + 1]),
                          reads=[wpb, cbuf], writes=[wpb])
                mkg = ar.mark()
                pcr = Ring(ar, kb, "pc", 8, 2048, BF16)
                tbr = Ring(ar, kb, "tb", 8, 2048, BF16, "p (a t) -> p a t", a=2)
                SC_S = 1.0 / math.sqrt(4096.0 * 128.0)
                SC_P = 1.0 / math.sqrt(256.0 * 128.0)
                for n in range(4):
                    for rc in range(32):
                        pc, pcb, pcd = pcr.next()
                        ti, rk, cc_ = rc // 8, (rc % 8) // 4, rc % 4
                        r0 = rk * 512 + cc_ * 128
                        kb.dma("sp", pc, px_out[ti][r0:r0 + 128, :], pcd, reads=[pxout_b[ti]], writes=[pcb])
                        tb, tbb, tbd = tbr.next()
                        kb.dma("sp", tb, tabs_d[rc][n], tbd, writes=[tbb])
                        for g in range(4):
                            kb.op("pe", lambda e, g=g, pc=pc, tb=tb, rc=rc: e.matmul(ps[g][:, :], pc[:, g * 256:g * 256 + 128], tb[:, 0, :], start=(rc == 0), stop=False),
                                  reads=[pcb, tbb], writes=[psb[g]], inc=False)
                            kb.op("pe", lambda e, g=g, pc=pc, tb=tb, rc=rc: e.matmul(ps[g][:, :], pc[:, g * 256 + 128:g * 256 + 256], tb[:, 1, :], start=False, stop=(rc == 31)),
                                  reads=[pcb, tbb], writes=[psb[g]], inc=(g == 3))
                    for g in range(4):
                        kb.op("act", lambda e, g=g, n=n: e.activation(out=mT[:, g, n * 512:(n + 1) * 512], in_=ps[g][:, :], func=AF.Identity, scale=SC_S),
                              reads=[psb[g]], writes=[mTb])
                for sq_ in range(2):
                    for rc in range(2):
                        pc, pcb, pcd = pcr.next()
                        r0 = sq_ * 256 + rc * 128
                        kb.dma("sp", pc, pp[r0:r0 + 128, :], pcd, reads=[pp_b], writes=[pcb])
                        tb, tbb, tbd = tbr.next()
                        kb.dma("sp", tb[:, :, 0:256], tabp_d[rc], tbd, writes=[tbb])
                        for g in range(4):
                            kb.op("pe", lambda e, g=g, pc=pc, tb=tb, rc=rc: e.matmul(ps[4 + g][:, 0:256], pc[:, g * 256:g * 256 + 128], tb[:, 0, 0:256], start=(rc == 0), stop=False),
                                  reads=[pcb, tbb], writes=[psb[4 + g]], inc=False)
                            kb.op("pe", lambda e, g=g, pc=pc, tb=tb, rc=rc: e.matmul(ps[4 + g][:, 0:256], pc[:, g * 256 + 128:g * 256 + 256], tb[:, 1, 0:256], start=False, stop=(rc == 1)),
                                  reads=[pcb, tbb], writes=[psb[4 + g]], inc=(g == 3))
                    for g in range(4):
                        t0 = TS + sq_ * 256
                        kb.op("act", lambda e, g=g, t0=t0: e.activation(out=mT[:, g, t0:t0 + 256], in_=ps[4 + g][:, 0:256], func=AF.Identity, scale=SC_P),
                              reads=[psb[4 + g]], writes=[mTb])
                ar.release(mkg)
                if debug == "G":
                    raise _Stop()

                srtr = Ring(ar, kb, "srt", 1, 8192, BF16, "p (c t) -> p c t", c=8)
                ogt = ar.alloc(8 * 512 * 2, BF16, "p (c b t) -> p c b t", c=8, b=4)
                ogtb = ar.buf()
                d_ogt = kb.dsem("ogt")
                ypre = ar.alloc(8 * 512 * 2, BF16, "p (m t) -> p m t", m=8)
                ypb = [ar.buf() for _ in range(8)]
                gtr = Ring(ar, kb, "gt", 8, 1024, BF16)
                t1r = Ring(ar, kb, "t1", 2, 2048, F32, dma=False)
                t2r = Ring(ar, kb, "t2", 2, 2048, F32, dma=False)
                hb_ = [0, 0, 0]
                for n in range(NT):
                    for bl in range(4):
                        kb.dma("sp", ogt[:, :, bl, :], ogsp[n * 4 + bl], d_ogt, reads=[ogsp_b], writes=[ogtb])
                    ogv = ogt.rearrange("p c b t -> p c (b t)")
                    srt, srtb, srtd = srtr.next()
                    kb.dma("sp", srt, gsp[0:8, :, n * 512:(n + 1) * 512].rearrange("c p t -> p c t"), srtd, reads=[gsp_b], writes=[srtb])
                    kb.op("dve", lambda e, o=ogv, g_=srt: e.tensor_tensor(out=o, in0=o, in1=g_, op=ALU.mult), reads=[ogtb, srtb], writes=[ogtb])
                    for m in range(8):
                        ba = hb_[0] % 2
                        hb_[0] += 1
                        bb = 2 + hb_[1] % 2
                        hb_[1] += 1
                        for g in range(4):
                            kb.op("pe", lambda e, ba=ba, g=g, m=m, n=n: e.matmul(ps[ba][:, :], wpf[:, g, m * 128:(m + 1) * 128], mT[:, g, n * 512:(n + 1) * 512], start=(g == 0), stop=(g == 3)),
                                  reads=[wpb, mTb], writes=[psb[ba]], inc=(g == 3))
                        for c8 in range(8):
                            kb.op("pe", lambda e, bb=bb, c8=c8, m=m: e.matmul(ps[bb][:, :], wpg[:, c8, m * 128:(m + 1) * 128], ogv[:, c8, :], start=(c8 == 0), stop=(c8 == 7)),
                                  reads=[wpb, ogtb], writes=[psb[bb]], inc=(c8 == 7))
                        ga, gab, gad = gtr.next()
                        kb.dma("sp", ga, gsp[8 + m, :, n * 512:(n + 1) * 512], gad, reads=[gsp_b], writes=[gab])
                        gb_, gbb, gbd = gtr.next()
                        kb.dma("sp", gb_, gsp[16 + m, :, n * 512:(n + 1) * 512], gbd, reads=[gsp_b], writes=[gbb])
                        t1, t1b, _ = t1r.next()
                        t2, t2b, _ = t2r.next()
                        kb.op("dve", lambda e, o=t1, ba=ba, ga=ga: e.tensor_tensor(out=o, in0=ps[ba][:, :], in1=ga, op=ALU.mult),
                              reads=[psb[ba], gab], writes=[t1b])
                        kb.op("dve", lambda e, o=t2, bb=bb, gb_=gb_: e.tensor_tensor(out=o, in0=ps[bb][:, :], in1=gb_, op=ALU.mult),
                              reads=[psb[bb], gbb], writes=[t2b])
                        kb.op("dve", lambda e, m=m, t1=t1, t2=t2: e.tensor_tensor(out=ypre[:, m, :], in0=t1, in1=t2, op=ALU.add),
                              reads=[t1b, t2b], writes=[ypb[m]])
                    for m2 in range(8):
                        bk = 4 + hb_[2] % 3
                        hb_[2] += 1
                        for m in range(8):
                            kb.op("pe", lambda e, bk=bk, m=m, m2=m2: e.matmul(ps[bk][:, :], wo[:, m, m2 * 128:(m2 + 1) * 128], ypre[:, m, :], start=(m == 0), stop=(m == 7)),
                                  reads=[wpb, ypb[m]], writes=[psb[bk]], inc=(m == 7))
                        xs = x[:, m2, n * 512:(n + 1) * 512]
                        kb.op("dve", lambda e, o=xs, bk=bk, s=scal(5, tsel(n), m2): e.scalar_tensor_tensor(out=o, in0=ps[bk][:, :], scalar=s, in1=o, op0=ALU.mult, op1=ALU.add),
                              reads=[psb[bk], scb], writes=xbufs(m2, n))
                ar.release(mk)
                dump_x("mix")

                if debug != "mix":
                    ffn(2, wg2, wu2, wd2)
                    dump_x("ffn2")

                    mk = ar.mark()
                    sqr = Ring(ar, kb, "sq", 2, 1024, BF16, dma=False)
                    rsr = Ring(ar, kb, "rs", 2, 2048, F32, dma=False)
                    xnr = Ring(ar, kb, "xn", 2, 8 * 2048, F32, "p (m t) -> p m t", dma=False, m=8)
                    osg = Ring(ar, kb, "osg", 3, 4096, F32)
                    yb_ = Buf()
                    tb_ = [0]
                    for n in range(NT):
                        rs, rsb = rms_stats(n, sqr, rsr, 7)
                        xn, xnb, _ = xnr.next()
                        for m in range(KC):
                            kb.op("dve", lambda e, o=xn[:, m, :], a=x[:, m, n * 512:(n + 1) * 512], s=fv[:, FV_NF + m:FV_NF + m + 1], r=rs:
                                  e.scalar_tensor_tensor(out=o, in0=a, scalar=s, in1=r, op0=ALU.mult, op1=ALU.mult),
                                  reads=xbufs(m, n) + [rsb, cbuf], writes=[xnb])
                        for bl in range(4):
                            og_, ogb_, ogd_ = osg.next()
                            for hh in range(2):
                                bank = tb_[0] % 4
                                tb_[0] += 1
                                for i in range(4):
                                    m = hh * 4 + i
                                    kb.op("pe", lambda e, bank=bank, i=i, m=m, xn=xn, bl=bl: e.transpose(ps[bank][:, i * 128:(i + 1) * 128], xn[:, m, bl * 128:(bl + 1) * 128], ident),
                                          reads=[xnb, cbuf], writes=[psb[bank]], inc=(i == 3))
                                if hh == 0:
                                    kb.op("act", lambda e, o=og_[:, 0:512], bank=bank: e.activation(out=o, in_=ps[bank][:, :], func=AF.Copy),
                                          reads=[psb[bank]], writes=[ogb_])
                                else:
                                    kb.op("dve", lambda e, o=og_[:, 512:1024], bank=bank: e.tensor_copy(out=o, in_=ps[bank][:, :]),
                                          reads=[psb[bank]], writes=[ogb_])
                            r0 = (n * 4 + bl) * 128
                            kb.dma("sp", yout[r0:r0 + 128, :], og_, ogd_, reads=[ogb_], writes=[yb_])
                    ar.release(mk)

        try:
            _mixer_and_rest()
        except _Stop:
            dump_x(debug)

        kb.wait_all("sp")
        kb.replay(block)
    return nc


def _bf16(a):
    return np.asarray(a, dtype=np.float32).astype(ml_dtypes.bfloat16)


def _grid_pos():
    rows = 4096 // 64
    row = np.repeat(np.arange(rows, dtype=np.float32), 64)
    col = np.tile(np.arange(64, dtype=np.float32), rows)
    n_freq = D // 4
    omega = (np.float32(10000.0) ** (-np.arange(n_freq, dtype=np.float32) / np.float32(n_freq))).astype(np.float32)
    ra = row[:, None] * omega
    ca = col[:, None] * omega
    return np.concatenate([np.sin(ra), np.cos(ra), np.sin(ca), np.cos(ca)], axis=-1).astype(np.float32)


def _consts():
    ident = np.eye(128, dtype=np.float32)
    i = np.arange(128)
    L1 = (i[:, None] <= i[None, :]).astype(np.float32)
    U1 = (i[:, None] > i[None, :]).astype(np.float32)
    L2 = (i[:, None] >= i[None, :]).astype(np.float32)
    U2 = (i[:, None] < i[None, :]).astype(np.float32)
    tri = np.stack([L1, U1, L2, U2], 1) * np.float32(-1.0 / 16.0)
    mask = np.stack([np.tile(L1, (1, 4)), np.tile(L2, (1, 4))], 1)
    c = np.arange(128)
    ang = 2 * np.pi * ((c[:, None] * c[None, :]) % 128) / 128.0
    cs128 = np.concatenate([np.cos(ang), -np.sin(ang)], 1)
    return ident, tri.astype(np.float32), _bf16(mask), _bf16(cs128)


def _tables(flip):
    rc = np.arange(32)[:, None]
    p = np.arange(128)[None, :]
    local = (rc // 8) * 512 + (rc % 4) * 128 + p
    rpos = np.where(((rc % 8) // 4) == 0, local, 4095 - local).reshape(4096)
    j = np.arange(2048)
    cpos = (4095 - j) if flip else j
    ang = 2 * np.pi * ((rpos[:, None].astype(np.int64) * cpos[None, :]) % 4096) / 4096.0
    tabs = np.stack([np.cos(ang), np.sin(ang)], 1)
    tabs = np.ascontiguousarray(tabs.reshape(32, 128, 2, 4, 512).transpose(0, 3, 1, 2, 4))
    rp = np.arange(256)
    ppos = (255 - rp) if flip else rp
    angp = 2 * np.pi * ((ppos[:, None] * ppos[None, :]) % 256) / 256.0
    tabp = np.stack([np.cos(angp), np.sin(angp)], 1).reshape(2, 128, 2, 256)
    return _bf16(tabs), _bf16(tabp)


def _fm(vec):
    return np.ascontiguousarray(np.asarray(vec, np.float32).reshape(-1, 128).T)


def make_in_maps(inp):
    f32 = lambda a: np.ascontiguousarray(np.asarray(a, dtype=np.float32))
    pos = _grid_pos()
    ident, tri, mask, cs128 = _consts()
    tabs = [_tables(False), _tables(True)]
    shared = {
        "w_ada": f32(inp["w_ada"][0]),
        "w_ffn1_gate": f32(inp["w_ffn1_gate"][0]), "w_ffn1_up": f32(inp["w_ffn1_up"][0]), "w_ffn1_down": f32(inp["w_ffn1_down"][0]),
        "w_ffn2_gate": f32(inp["w_ffn2_gate"][0]), "w_ffn2_up": f32(inp["w_ffn2_up"][0]), "w_ffn2_down": f32(inp["w_ffn2_down"][0]),
        "w_in": f32(inp["w_in"][0]),
        "w_proj_fourier": f32(inp["w_proj_fourier"][0]), "w_proj_gla": f32(inp["w_proj_gla"][0]), "w_out": f32(inp["w_out"][0]),
        "ident": ident, "tri": tri, "maskc": mask, "cs128": cs128,
    }
    w_in = shared["w_in"]
    alr_f, alr_b = w_in[:, 3584:3600], w_in[:, 3600:3616]
    wa_f = np.concatenate([f32(inp["w_alpha_fwd"][0]), f32(inp["b_alpha_fwd"][0])[None]], 0)
    wa_b = np.concatenate([f32(inp["w_alpha_bwd"][0]), f32(inp["b_alpha_bwd"][0])[None]], 0)
    b_ada = _fm(inp["b_ada"][0])
    maps = []
    for c in range(8):
        b, half = c // 2, c % 2
        flip = half == 1
        sl = slice(half * TS, (half + 1) * TS)
        xs = f32(inp["x_sample"][b, sl])
        ps_ = pos[sl]
        xp = [f32(inp["x_prompt"][2 * c]), f32(inp["x_prompt"][2 * c + 1])]
        if flip:
            xs, ps_ = xs[::-1], ps_[::-1]
            xp = [a[::-1] for a in xp]
        xin = np.ascontiguousarray(np.concatenate([xs] + xp, 0))
        posT = np.ascontiguousarray(ps_.reshape(16, 128, KC, 128).transpose(0, 3, 2, 1))
        st = inp["state_gla_bwd"] if flip else inp["state_gla_fwd"]
        sinit = np.ascontiguousarray(f32(st[b, 0]).transpose(1, 0, 2))
        cc = np.stack([f32(inp["c"][b]), f32(inp["c_ctx"])], 0)
        cT = np.ascontiguousarray(cc.reshape(2, KC, 128).transpose(2, 1, 0))
        fvec = np.zeros((128, FV_N), np.float32)
        fvec[:, FV_BADA:FV_BADA + 144] = np.repeat(b_ada, 2, axis=1)
        fvec[:, FV_N1:FV_N1 + 8] = _fm(inp["norm_ffn1"][0])
        fvec[:, FV_N2:FV_N2 + 8] = _fm(inp["norm_mix"][0])
        fvec[:, FV_N3:FV_N3 + 8] = _fm(inp["norm_ffn2"][0])
        fvec[:, FV_NF:FV_NF + 8] = _fm(inp["final_norm"])
        fvec[:, FV_GN:FV_GN + 8] = _fm(inp["gla_norm"][0])
        fvec[:, FV_SEL:FV_SEL + 2] = np.array([1.0, 0.0] if flip else [0.0, 1.0], np.float32)
        m = dict(shared)
        m.update({
            "xin": xin, "posT": posT, "sinit": sinit, "cT": cT, "fvec": fvec,
            "w_alr": np.ascontiguousarray(np.concatenate([alr_b, alr_f] if flip else [alr_f, alr_b], 1)),
            "walpha": np.ascontiguousarray(np.stack([wa_b, wa_f] if flip else [wa_f, wa_b], 0)),
            "tabs": tabs[half][0], "tabp": tabs[half][1],
        })
        maps.append(m)
    return maps


def assemble(results):
    y_prompt = np.zeros((16, 256, D), np.float32)
    y_sample = np.zeros((4, 4096, D), np.float32)
    nsf = np.zeros((16, 1, 4, 128, 256), np.float32)
    nsb = np.zeros((16, 1, 4, 128, 256), np.float32)
    for c in range(8):
        r = results[c]
        b, half = c // 2, c % 2
        flip = half == 1
        y = r["yout"]
        ys, yp = y[:TS], [y[TS:TS + 256], y[TS + 256:]]
        if flip:
            ys = ys[::-1]
            yp = [a[::-1] for a in yp]
        y_sample[b, half * TS:(half + 1) * TS] = ys
        for s in range(2):
            y_prompt[2 * c + s] = yp[s]
            a1 = r["st1"][s].transpose(1, 0, 2)
            a2 = r["st2"][s].transpose(1, 0, 2)
            if flip:
                a1, a2 = a2, a1
            nsf[2 * c + s, 0] = a1
            nsb[2 * c + s, 0] = a2
    return y_prompt, y_sample, nsf, nsb


def kernel(**inputs):
    nc = build_nc()
    in_maps = make_in_maps(inputs)
    res = run_bass_kernel_spmd(nc, in_maps, core_ids=list(range(8)))
    return assemble(res.results)
```

```python
import math
from contextlib import ExitStack

import ml_dtypes
import numpy as np

import concourse.bass as bass
import concourse.mybir as mybir
from concourse.bass_utils import run_bass_kernel_spmd

F32 = mybir.dt.float32
BF16 = mybir.dt.bfloat16
AF = mybir.ActivationFunctionType
ALU = mybir.AluOpType

D = 1024
KC = 8
DFF = 2816
NFF = 22
T = 2560
NT = 5
NB = 20
TS = 2048
NCOLS_IN = 5664
EPS = 1e-6
NMODV = 72

FV_BADA = 0
FV_N1 = 144
FV_N2 = 152
FV_N3 = 160
FV_NF = 168
FV_GN = 176
FV_SEL = 184
FV_N = 186


class _Stop(Exception):
    pass


class Buf:
    __slots__ = ("w", "r")

    def __init__(self, seed=None):
        self.w = {}
        self.r = dict(seed) if seed else {}


class DSem:
    def __init__(self, h):
        self.h = h
        self.n = 0


class KB:
    ENG = ("pe", "act", "dve", "pool", "sp")

    def __init__(self, nc, es):
        self.nc = nc
        self.es = es
        self.q = {e: [] for e in self.ENG}
        self.sem = {e: es.enter_context(nc.semaphore("s_" + e)) for e in self.ENG}
        self.cnt = {e: 0 for e in self.ENG}
        self.waited = {e: {} for e in self.ENG}
        self.pend_r = {e: [] for e in self.ENG}
        self.pend_w = {e: [] for e in self.ENG}
        self.dsems = []
        self.semname = {}
        for e in self.ENG:
            self.semname[id(self.sem[e])] = e

    def dsem(self, name):
        self.nds = getattr(self, "nds", 0) + 1
        d = DSem(self.es.enter_context(self.nc.semaphore(f"d{self.nds}_{name}")))
        self.dsems.append(d)
        return d

    def snapshot(self):
        s = {}
        for e in self.ENG:
            if self.cnt[e]:
                s[id(self.sem[e])] = (self.sem[e], self.cnt[e])
        for d in self.dsems:
            if d.n:
                s[id(d.h)] = (d.h, d.n)
        return s

    def _need(self, eng, waits, ev):
        sem, val = ev
        k = id(sem)
        if eng == "pe" and sem is self.sem["pe"]:
            return
        if self.waited[eng].get(k, 0) >= val:
            return
        if k in waits and waits[k][1] >= val:
            return
        waits[k] = (sem, val)

    def _deps(self, eng, reads, writes):
        waits = {}
        for b in reads:
            for ev in b.w.values():
                self._need(eng, waits, ev)
        for b in writes:
            for ev in b.w.values():
                self._need(eng, waits, ev)
            for ev in b.r.values():
                self._need(eng, waits, ev)
        for k, (sem, val) in waits.items():
            self.waited[eng][k] = val
        return list(waits.values())

    def _commit(self, ev, reads, writes):
        k = id(ev[0])
        for b in reads:
            b.r[k] = ev
        for b in writes:
            b.w[k] = ev

    def op(self, eng, fn, reads=(), writes=(), inc=True):
        reads = list(reads)
        writes = list(writes)
        waits = self._deps(eng, reads, writes)
        if inc:
            self.cnt[eng] += 1
            ev = (self.sem[eng], self.cnt[eng])
            self._commit(ev, reads + self.pend_r[eng], writes + self.pend_w[eng])
            self.pend_r[eng] = []
            self.pend_w[eng] = []
            self.q[eng].append((waits, fn, (self.sem[eng], 1)))
        else:
            self.pend_r[eng] += reads
            self.pend_w[eng] += writes
            self.q[eng].append((waits, fn, None))

    def dma(self, queue, out, in_, dsem, reads=(), writes=(), **kw):
        reads = list(reads)
        writes = list(writes)
        waits = self._deps(queue, reads, writes)
        dsem.n += 16
        ev = (dsem.h, dsem.n)
        self._commit(ev, reads, writes)
        self.q[queue].append((waits, lambda e: e.dma_start(out=out, in_=in_, **kw), (dsem.h, 16)))

    def raw(self, queue, fn, dsem_inc, reads=(), writes=()):
        reads = list(reads)
        writes = list(writes)
        waits = self._deps(queue, reads, writes)
        d, n = dsem_inc
        d.n += n
        ev = (d.h, d.n)
        self._commit(ev, reads, writes)
        self.q[queue].append((waits, fn, (d.h, n)))

    def wait_all(self, eng):
        waits = []
        for k, (sem, val) in self.snapshot().items():
            if sem is self.sem[eng]:
                continue
            if self.waited[eng].get(k, 0) >= val:
                continue
            self.waited[eng][k] = val
            waits.append((sem, val))
        self.q[eng].append((waits, None, None))

    def replay(self, block):
        def run(eng):
            def f(e):
                for waits, fn, inc in self.q[eng]:
                    for sem, val in waits:
                        e.wait_ge(sem, val)
                    if fn is None:
                        continue
                    ins = fn(e)
                    if inc is not None:
                        ins.then_inc(inc[0], inc[1])
            return f

        block.tensor(run("pe"))
        block.scalar(run("act"))
        block.vector(run("dve"))
        block.gpsimd(run("pool"))
        block.sync(run("sp"))


class Arena:
    def __init__(self, kb, tens, nbytes):
        self.kb = kb
        self.t = tens
        self.top = 0
        self.cap = nbytes
        self.seed = None

    def mark(self):
        return self.top

    def release(self, mark):
        self.top = mark
        self.seed = self.kb.snapshot()

    def alloc(self, nbytes, dtype, pat=None, parts=128, **kw):
        assert nbytes % 4 == 0
        off = self.top
        self.top += (nbytes + 31) // 32 * 32
        assert self.top <= self.cap, f"SBUF arena overflow {self.top} > {self.cap}"
        ap = self.t[0:parts, off // 4:(off + nbytes) // 4]
        if dtype != F32:
            ap = ap.bitcast(dtype)
        if pat:
            ap = ap.rearrange(pat, **kw)
        return ap

    def buf(self):
        return Buf(self.seed)


class Ring:
    def __init__(self, ar, kb, name, n, nbytes, dtype, pat=None, parts=128, dma=True, **kw):
        self.aps = [ar.alloc(nbytes, dtype, pat, parts, **kw) for _ in range(n)]
        self.bufs = [ar.buf() for _ in range(n)]
        self.ds = [kb.dsem(f"{name}{i}") for i in range(n)] if dma else [None] * n
        self.i = -1
        self.n = n

    def next(self):
        self.i = (self.i + 1) % self.n
        return self.aps[self.i], self.bufs[self.i], self.ds[self.i]


def build_nc(debug=None):
    nc = bass.Bass("TRN2", target_bir_lowering=False)

    def din(name, shape, dt=F32):
        return nc.dram_tensor(name, list(shape), dt, kind="ExternalInput").ap()

    def dout(name, shape, dt=F32):
        return nc.dram_tensor(name, list(shape), dt, kind="ExternalOutput").ap()

    xin = din("xin", [T, D])
    posT = din("posT", [16, 128, KC, 128])
    sinit = din("sinit", [128, 4, 256])
    cT = din("cT", [128, KC, 2])
    fvec = din("fvec", [128, FV_N])
    w_ada = din("w_ada", [D, 9216])
    wg1 = din("w_ffn1_gate", [D, DFF]); wu1 = din("w_ffn1_up", [D, DFF]); wd1 = din("w_ffn1_down", [DFF, D])
    wg2 = din("w_ffn2_gate", [D, DFF]); wu2 = din("w_ffn2_up", [D, DFF]); wd2 = din("w_ffn2_down", [DFF, D])
    w_in = din("w_in", [D, NCOLS_IN])
    w_alr = din("w_alr", [D, 32])
    walpha = din("walpha", [2, 17, 512])
    w_pf = din("w_proj_fourier", [512, D])
    w_pg = din("w_proj_gla", [D, D])
    w_out = din("w_out", [D, D])
    ident_d = din("ident", [128, 128])
    tri_d = din("tri", [128, 4, 128])
    mask_d = din("maskc", [128, 2, 512], BF16)
    cs128_d = din("cs128", [128, 256], BF16)
    tabs_d = din("tabs", [32, 4, 128, 2, 512], BF16)
    tabp_d = din("tabp", [2, 128, 2, 256], BF16)

    yout = dout("yout", [T, D])
    st1 = dout("st1", [2, 128, 4, 256])
    st2 = dout("st2", [2, 128, 4, 256])
    dbg = dout("dbg", [128, 8 * T]) if debug else None
    dbgh = dout("dbgh", [128, 8 * T], BF16) if debug == "h" else None

    px_in = [nc.dram_tensor(f"px_in{i}", [512, 1024], BF16) for i in range(4)]
    px_out = [nc.dram_tensor(f"px_out{i}", [1024, 1024], BF16) for i in range(4)]
    pp = nc.dram_tensor("pp", [512, 1024], BF16)
    sx_in = nc.dram_tensor("sx_in", [128, 1024], F32)
    sx_out = nc.dram_tensor("sx_out", [256, 1024], F32)
    qkT = nc.dram_tensor("qkT", [NB, 128, 8, 128], BF16)
    kvt = nc.dram_tensor("kvt", [NB, 128, 1536], BF16)
    gsp = nc.dram_tensor("gsp", [24, 128, T], BF16)
    o1sp = nc.dram_tensor("o1sp", [NB, 128, 8, 128], F32)
    ogsp = nc.dram_tensor("ogsp", [NB, 128, 8, 128], BF16)

    es = ExitStack()
    with es:
        ARENA_BYTES = 212000
        arena_t = es.enter_context(nc.sbuf_tensor("arena", [128, ARENA_BYTES // 4], F32))
        ps = [es.enter_context(nc.psum_tensor(f"ps{i}", [128, 512], F32)) for i in range(8)]
        kb = KB(nc, es)
        ar = Arena(kb, arena_t, ARENA_BYTES)
        psb = [Buf() for _ in range(8)]
        block = es.enter_context(nc.Block())

        x = ar.alloc(KC * T * 4, F32, "p (m t) -> p m t", m=KC)
        xb = [[Buf() for _ in range(NB)] for _ in range(KC)]
        ident = ar.alloc(512, F32)
        ones = ar.alloc(256, BF16)
        tri = ar.alloc(2048, F32, "p (a b) -> p a b", a=4)
        trib = ar.alloc(1024, BF16, "p (a b) -> p a b", a=4)
        maskc = ar.alloc(2048, BF16, "p (a b) -> p a b", a=2)
        cs128 = ar.alloc(512, BF16)
        epsc = ar.alloc(32, F32)[:, 0:1]
        fv = ar.alloc(FV_N * 4, F32)
        modfm = ar.alloc(NMODV * 2 * 4, F32, "p (c v) -> p c v", v=2)
        sc = ar.alloc(9 * 16 * 4, F32)

        def scal(k, v, m):
            c = (k * 2 + v) * 8 + m
            return sc[:, c:c + 1]
        cbuf = Buf()
        d_const = kb.dsem("const")
        wslab = Ring(ar, kb, "ws", 4, 4096, BF16)

        def xbufs(m, n):
            return [xb[m][4 * n + i] for i in range(4)]

        def tsel(n):
            return 0 if n < 4 else 1

        for dst, src in ((ident, ident_d), (tri, tri_d), (maskc, mask_d), (cs128, cs128_d), (fv, fvec)):
            kb.dma("sp", dst, src, d_const, writes=[cbuf])
        kb.op("dve", lambda e: e.memset(ones, 1.0), writes=[cbuf])
        kb.op("dve", lambda e: e.memset(epsc, EPS), writes=[cbuf])
        kb.op("dve", lambda e: e.tensor_copy(out=trib, in_=tri), reads=[cbuf], writes=[cbuf])

        mkA = ar.mark()
        ctf = ar.alloc(KC * 2 * 4, F32, "p (k v) -> p k v", v=2)
        ctb = ar.alloc(KC * 2 * 2, BF16, "p (k v) -> p k v", v=2)
        ctbuf = ar.buf()
        d_ct = kb.dsem("ct")
        kb.dma("sp", ctf, cT, d_ct, writes=[ctbuf])
        kb.op("act", lambda e: e.activation(out=ctb, in_=ctf, func=AF.Silu), reads=[ctbuf], writes=[ctbuf])
        ADABANK = 7
        adar = Ring(ar, kb, "ada", 3, 4096, BF16)
        mstr = Ring(ar, kb, "mst", 2, 1024, F32, parts=2, dma=False)
        scb = Buf()
        ada_pending = []

        def ada_load(cb):
            slab, slb, sld = adar.next()
            sv = slab.rearrange("p (k c) -> p k c", k=KC)
            kb.dma("pool", sv, w_ada[:, cb * 256:(cb + 1) * 256].rearrange("(k p) c -> p k c", p=128), sld, writes=[slb])
            ada_pending.append((cb, sv, slb))

        def ada_compute():
            cb, sv, slb = ada_pending.pop(0)
            for kc in range(KC):
                kb.op("pe", lambda e, a=ctb[:, kc, :], r=sv[:, kc, :], kc=kc:
                      e.matmul(ps[ADABANK][0:2, 0:256], a, r, start=(kc == 0), stop=(kc == KC - 1)),
                      reads=[ctbuf, slb], writes=[psb[ADABANK]], inc=(kc == KC - 1))
            ms, msb, _ = mstr.next()
            kb.op("act", lambda e, o=ms: e.activation(out=o, in_=ps[ADABANK][0:2, 0:256], func=AF.Copy), reads=[psb[ADABANK]], writes=[msb])
            for j in range(2):
                kb.op("pe", lambda e, j=j, ms=ms: e.matmul(ps[ADABANK][:, 256 + 2 * j:258 + 2 * j], ms[:, j * 128:(j + 1) * 128], ident[0:2, 0:2], start=True, stop=True),
                      reads=[msb, cbuf], writes=[psb[ADABANK]], inc=(j == 1))
            kb.op("dve", lambda e, cb=cb: e.tensor_tensor(out=modfm[:, 2 * cb:2 * cb + 2, :], in0=ps[ADABANK][:, 256:260].rearrange("p (c v) -> p c v", v=2),
                                                         in1=fv[:, FV_BADA + 4 * cb:FV_BADA + 4 * cb + 4].rearrange("p (c v) -> p c v", v=2), op=ALU.add),
                  reads=[psb[ADABANK], cbuf], writes=[scb])

        def derive_scalars(li, norm_part=True, gate_part=True):
            fvn = (FV_N1, FV_N2, FV_N3)[li]
            base = li * 24
            for v in range(2):
                c_a = ((3 * li) * 2 + v) * 8
                c_s = ((3 * li + 1) * 2 + v) * 8
                c_g = ((3 * li + 2) * 2 + v) * 8
                if norm_part:
                    kb.op("dve", lambda e, o=sc[:, c_a:c_a + 8], a=modfm[:, base + 8:base + 16, v], g=fv[:, fvn:fvn + 8]:
                          e.scalar_tensor_tensor(out=o, in0=a, scalar=1.0, in1=g, op0=ALU.add, op1=ALU.mult),
                          reads=[scb, cbuf], writes=[scb])
                    kb.op("dve", lambda e, o=sc[:, c_s:c_s + 8], a=modfm[:, base:base + 8, v]:
                          e.tensor_copy(out=o, in_=a), reads=[scb], writes=[scb])
                if gate_part:
                    gsc = 1.0 if li == 1 else 0.5
                    kb.op("dve", lambda e, o=sc[:, c_g:c_g + 8], a=modfm[:, base + 16:base + 24, v], gsc=gsc:
                          e.tensor_scalar(out=o, in0=a, scalar1=gsc, scalar2=None, op0=ALU.mult),
                          reads=[scb], writes=[scb])

        NPRE = 8
        for cb in range(3):
            ada_load(cb)
        ada_ld = [3]

        mk = ar.mark()
        tokr = Ring(ar, kb, "tok", 3, 4096, F32)
        posr = Ring(ar, kb, "pos", 2, 4096, F32, "p (m t) -> p m t", m=KC)
        for b in range(NB):
            tok, tokb, tokd = tokr.next()
            kb.dma("sp", tok, xin[b * 128:(b + 1) * 128, :], tokd, writes=[tokb])
            if b < 16:
                pos, posb, posd = posr.next()
                kb.dma("sp", pos, posT[b], posd, writes=[posb])
            for hh in range(2):
                bank = hh
                for i in range(4):
                    m = hh * 4 + i
                    kb.op("pe", lambda e, o=ps[bank][:, i * 128:(i + 1) * 128], a=tok[:, m * 128:(m + 1) * 128]:
                          e.transpose(o, a, ident),
                          reads=[tokb, cbuf], writes=[psb[bank]], inc=(i == 3))
                pv = ps[bank][:, :].rearrange("p (a b) -> p a b", a=4)
                xo = x[:, hh * 4:hh * 4 + 4, b * 128:(b + 1) * 128]
                wr = [xb[hh * 4 + i][b] for i in range(4)]
                if b < 16:
                    kb.op("dve", lambda e, o=xo, a=pv, c=pos[:, hh * 4:hh * 4 + 4, :]:
                          e.tensor_tensor(out=o, in0=a, in1=c, op=ALU.add),
                          reads=[psb[bank], posb], writes=wr)
                else:
                    kb.op("act", lambda e, o=xo, a=pv: e.activation(out=o, in_=a, func=AF.Copy),
                          reads=[psb[bank]], writes=wr)
            if b < NPRE:
                ada_compute()
                if ada_ld[0] < NPRE + 3:
                    ada_load(ada_ld[0])
                    ada_ld[0] += 1
        ar.release(mk)

        derive_scalars(0, gate_part=False)
        ada_next = [NPRE + 3]
        gate1_done = [False]

        def ada_hook(k=2):
            if not gate1_done[0]:
                while ada_pending:
                    ada_compute()
                ada_load(ada_next[0])
                ada_next[0] += 1
                ada_compute()
                derive_scalars(0, norm_part=False)
                gate1_done[0] = True
            for _ in range(3):
                if ada_pending:
                    ada_compute()
            for _ in range(k):
                if ada_next[0] < 36:
                    ada_load(ada_next[0])
                    ada_next[0] += 1

        def rms_stats(n, sqr, rsr, ssbank):
            for m in range(KC):
                sq, sqb, _ = sqr.next()
                if m == KC - 1:
                    kb.op("dve", lambda e, o=sq, a=x[:, m, n * 512:(n + 1) * 512]: e.tensor_tensor(out=o, in0=a, in1=a, op=ALU.mult),
                          reads=xbufs(m, n), writes=[sqb])
                else:
                    kb.op("act", lambda e, o=sq, a=x[:, m, n * 512:(n + 1) * 512]: e.activation(out=o, in_=a, func=AF.Square),
                          reads=xbufs(m, n), writes=[sqb])
                kb.op("pe", lambda e, a=sq, m=m: e.matmul(ps[ssbank][:, :], ones, a,
                                                          start=(m == 0), stop=(m == KC - 1)),
                      reads=[sqb, cbuf], writes=[psb[ssbank]], inc=True)
            rs, rsb, _ = rsr.next()
            kb.op("act", lambda e, o=rs: e.activation(out=o, in_=ps[ssbank][:, :], func=AF.Ln, scale=1.0 / D, bias=epsc),
                  reads=[psb[ssbank], cbuf], writes=[rsb])
            kb.op("act", lambda e, o=rs: e.activation(out=o, in_=o, func=AF.Exp, scale=-0.5), reads=[rsb], writes=[rsb])
            return rs, rsb

        def norm_mod(li, h, hb, sqr, rsr, tmr, ssbank, tiles=None):
            for n in (range(NT) if tiles is None else tiles):
                v = tsel(n)
                rs, rsb = rms_stats(n, sqr, rsr, ssbank)
                for m in range(KC):
                    tm, tmb, _ = tmr.next()
                    kb.op("dve", lambda e, o=tm, a=x[:, m, n * 512:(n + 1) * 512], s=scal(3 * li, v, m), r=rs:
                          e.scalar_tensor_tensor(out=o, in0=a, scalar=s, in1=r, op0=ALU.mult, op1=ALU.mult),
                          reads=xbufs(m, n) + [rsb, scb], writes=[tmb])
                    if m % 2 == 0:
                        kb.op("act", lambda e, o=h[:, m, n * 512:(n + 1) * 512], a=tm, s=scal(3 * li + 1, v, m):
                              e.activation(out=o, in_=a, func=AF.Identity, bias=s, scale=1.0),
                              reads=[tmb, scb], writes=[hb[m][n]])
                    else:
                        kb.op("dve", lambda e, o=h[:, m, n * 512:(n + 1) * 512], a=tm, s=scal(3 * li + 1, v, m):
                              e.tensor_scalar(out=o, in0=a, scalar1=s, scalar2=None, op0=ALU.add),
                              reads=[tmb, scb], writes=[hb[m][n]])

        def ffn(li, wg, wu, wd):
            mk = ar.mark()
            h = ar.alloc(KC * T * 2, BF16, "p (m t) -> p m t", m=KC)
            hb = [[ar.buf() for _ in range(NT)] for _ in range(KC)]
            sqr = Ring(ar, kb, "sq", 2, 1024, BF16, dma=False)
            rsr = Ring(ar, kb, "rs", 2, 2048, F32, dma=False)
            tmr = Ring(ar, kb, "tm", 3, 2048, F32, dma=False)
            Ar = Ring(ar, kb, "A", 2, 2 * T * 2, BF16, "p (j t) -> p j t", j=2, dma=False)
            if debug == "h" and li == 0:
                norm_mod(li, h, hb, sqr, rsr, tmr, 7)
            if debug == "h" and li == 0:
                d_dh = kb.dsem("dbgh")
                kb.dma("sp", dbgh, h.rearrange("p m t -> p (m t)"), d_dh, reads=[hb[m][n] for m in range(KC) for n in range(NT)])
                raise _Stop()
            NG = NFF // 2
            hall = [hb[m][n] for m in range(KC) for n in range(NT)]
            gk = 3 * li + 2

            def load_gu(g):
                sg, sgb, sgd = wslab.next()
                su, sub, sud = wslab.next()
                sgv = sg.rearrange("p (k c) -> p k c", k=KC)
                suv = su.rearrange("p (k c) -> p k c", k=KC)
                kb.dma("pool", sgv, wg[:, g * 256:(g + 1) * 256].rearrange("(k p) c -> p k c", p=128), sgd, writes=[sgb])
                kb.dma("pool", suv, wu[:, g * 256:(g + 1) * 256].rearrange("(k p) c -> p k c", p=128), sud, writes=[sub])
                return sgv, sgb, suv, sub

            def load_d(g):
                sd, sdb, sdd = wdr.next()
                sdv = sd.rearrange("p (j c) -> p j c", j=2)
                kb.dma("pool", sdv, wd[g * 256:(g + 1) * 256, :].rearrange("(j p) c -> p j c", p=128), sdd, writes=[sdb])
                return sdv, sdb

            wdr = Ring(ar, kb, "wd", 2, 4096, BF16)
            gu_next = load_gu(0)
            prev = None
            pbank = 0
            ybank = [0]

            def y_group(prev, m, n):
                pA, pAb, (sdv, sdb) = prev
                bk = 4 + ybank[0]
                ybank[0] = (ybank[0] + 1) % (3 if li == 0 else 4)
                for j in range(2):
                    kb.op("pe", lambda e, o=ps[bk][:, :], a=sdv[:, j, m * 128:(m + 1) * 128], r=pA[:, j, n * 512:(n + 1) * 512], j=j:
                          e.matmul(o, a, r, start=(j == 0), stop=(j == 1)),
                          reads=[sdb, pAb], writes=[psb[bk]], inc=(j == 1))
                xs = x[:, m, n * 512:(n + 1) * 512]
                kb.op("dve", lambda e, o=xs, a=ps[bk][:, :], s=scal(gk, tsel(n), m):
                      e.scalar_tensor_tensor(out=o, in0=a, scalar=s, in1=o, op0=ALU.mult, op1=ALU.add),
                      reads=[psb[bk], scb], writes=xbufs(m, n))

            for g in range(NG + 1):
                ylist = [(m, n) for m in range(KC) for n in range(NT)] if prev is not None else []
                if g < NG:
                    sgv, sgb, suv, sub = gu_next
                    dcur = load_d(g)
                    Aap, Ab, _ = Ar.next()
                    for j in range(2):
                        for n in range(NT):
                            if g == 0 and j == 0 and not (debug == "h" and li == 0):
                                norm_mod(li, h, hb, sqr, rsr, tmr, 7, tiles=[n])
                            bg, bu = 2 * pbank, 2 * pbank + 1
                            pbank ^= 1
                            for (bk, sv, sb_) in ((bg, sgv, sgb), (bu, suv, sub)):
                                for kc in range(KC):
                                    kb.op("pe", lambda e, o=ps[bk][:, :], a=sv[:, kc, j * 128:(j + 1) * 128], r=h[:, kc, n * 512:(n + 1) * 512], kc=kc:
                                          e.matmul(o, a, r, start=(kc == 0), stop=(kc == KC - 1)),
                                          reads=[sb_] + [hb[kc][n]], writes=[psb[bk]], inc=(kc == KC - 1))
                                if bk == bg:
                                    for _ in range(2):
                                        if ylist:
                                            y_group(prev, *ylist.pop(0))
                            tm, tmb, _ = tmr.next()
                            kb.op("act", lambda e, o=tm, a=ps[bg][:, :]: e.activation(out=o, in_=a, func=AF.Silu),
                                  reads=[psb[bg]], writes=[tmb])
                            kb.op("dve", lambda e, o=Aap[:, j, n * 512:(n + 1) * 512], a=tm, b=ps[bu][:, :]:
                                  e.tensor_tensor(out=o, in0=a, in1=b, op=ALU.mult),
                                  reads=[tmb, psb[bu]], writes=[Ab])
                            for _ in range(2):
                                if ylist:
                                    y_group(prev, *ylist.pop(0))
                    if g + 1 < NG:
                        gu_next = load_gu(g + 1)
                    if li == 0:
                        ada_hook()
                    cur = (Aap, Ab, dcur)
                else:
                    cur = None
                while ylist:
                    y_group(prev, *ylist.pop(0))
                prev = cur
            ar.release(mk)

        def dump_x(tag):
            if debug == tag:
                d_dbg = kb.dsem("dbg")
                kb.dma("sp", dbg, x.rearrange("p m t -> p (m t)"), d_dbg,
                       reads=[xb[m][b] for m in range(KC) for b in range(NB)])

        try:
            ffn(0, wg1, wu1, wd1)
        except _Stop:
            pass
        while ada_pending or ada_next[0] < 36:
            ada_hook(3)
        derive_scalars(1)
        derive_scalars(2)
        ar.release(mkA)
        dump_x("ffn1")

        def _mixer_and_rest():
            mkM = ar.mark()
            alr = [ar.alloc(T * 2, BF16, parts=17) for _ in range(2)]
            alrb = [ar.buf() for _ in range(2)]
            SQ = 1.0 / math.sqrt(128.0)

            if debug != "ffn1":
                mk = ar.mark()
                h2 = ar.alloc(KC * T * 2, BF16, "p (m t) -> p m t", m=KC)
                h2b = [[ar.buf() for _ in range(NT)] for _ in range(KC)]
                stg = Ring(ar, kb, "stg", 3, 1024, BF16)
                pstg = Ring(ar, kb, "pstg", 3, 2048, BF16, "p (b c) -> p b c", b=4)
                kvst = Ring(ar, kb, "kvst", 2, 3072, BF16)
                pre_slabs = []

                def slab_prefetch(wsrc, ncol):
                    slab, slb, sld = wslab.next()
                    sv = slab.rearrange("p (k c) -> p k c", k=KC)[:, :, 0:ncol]
                    kb.dma("pool", sv, wsrc.rearrange("(k p) c -> p k c", p=128), sld, writes=[slb])
                    return sv, slb

                pre_slabs.append(slab_prefetch(w_in[:, 0:256], 256))
                pre_slabs.append(slab_prefetch(w_in[:, 256:512], 256))
                mk3 = ar.mark()
                wkv = ar.alloc(KC * 1536 * 2, BF16, "p (k c) -> p k c", k=KC)
                wkvb = ar.buf()
                d_wkv = kb.dsem("wkv")
                for i in range(6):
                    c0 = 1024 + i * 256
                    kb.dma("pool", wkv[:, :, i * 256:(i + 1) * 256], w_in[:, c0:c0 + 256].rearrange("(k p) c -> p k c", p=128),
                           d_wkv, writes=[wkvb])
                mk2 = ar.mark()
                sqr = Ring(ar, kb, "sq", 2, 1024, BF16, dma=False)
                rsr = Ring(ar, kb, "rs", 2, 2048, F32, dma=False)
                tmr = Ring(ar, kb, "tm", 3, 2048, F32, dma=False)

                for d_ in range(2):
                    kb.op("dve", lambda e, o=alr[d_]: e.memset(o, 1.0), writes=[alrb[d_]])

                pp_b, qkT_b, kvt_b, gsp_b = Buf(), Buf(), Buf(), Buf()
                pxin_b = [Buf() for _ in range(4)]
                swb = [0]

                def fm_sweep(wsrc, ncol, epi, pre=None, pre_tile=None):
                    sv, slb = pre if pre is not None else slab_prefetch(wsrc, ncol)
                    nj = max(1, ncol // 128)
                    M = min(ncol, 128)
                    for j in range(nj):
                        for n in range(NT):
                            if pre_tile is not None and j == 0:
                                pre_tile(n)
                            bank = swb[0]
                            swb[0] = (swb[0] + 1) % 4
                            for kc in range(KC):
                                kb.op("pe", lambda e, o=ps[bank][0:M, :], a=sv[:, kc, j * M:(j + 1) * M], r=h2[:, kc, n * 512:(n + 1) * 512], kc=kc:
                                      e.matmul(o, a, r, start=(kc == 0), stop=(kc == KC - 1)),
                                      reads=[slb, h2b[kc][n]], writes=[psb[bank]], inc=(kc == KC - 1))
                            epi(j, n, bank, M)

                s1b = [0]

                pend_f = []

                def flush_f():
                    while pend_f:
                        g, n, fs, fsb = pend_f.pop(0)
                        pst, pstb, pstd = pstg.next()
                        for bl in range(4):
                            b2 = 4 + s1b[0]
                            s1b[0] = (s1b[0] + 1) % 4
                            kb.op("pe", lambda e, o=ps[b2][:, 0:256], a=fs[:, bl * 128:(bl + 1) * 128]:
                                  e.matmul(o, a, cs128, start=True, stop=True),
                                  reads=[fsb, cbuf], writes=[psb[b2]])
                            kb.op("dve", lambda e, o=pst[:, bl, :], a=ps[b2][:, 0:256]: e.tensor_copy(out=o, in_=a),
                                  reads=[psb[b2]], writes=[pstb])
                        dstt = px_in[n] if n < 4 else pp
                        kb.dma("sp", dstt[:, g * 256:(g + 1) * 256].rearrange("(b p) c -> p b c", p=128), pst, pstd,
                               reads=[pstb], writes=[pxin_b[n] if n < 4 else pp_b])

                def epi_f(gbase):
                    def epi(j, n, bank, M):
                        g = gbase + j
                        fs, fsb, _ = stg.next()
                        kb.op("act", lambda e, o=fs, a=ps[bank][:, :]: e.activation(out=o, in_=a, func=AF.Copy),
                              reads=[psb[bank]], writes=[fsb])
                        flush_f()
                        pend_f.append((g, n, fs, fsb))
                    return epi

                fm_sweep(w_in[:, 0:256], 256, epi_f(0), pre_slabs[0],
                         pre_tile=lambda n: norm_mod(1, h2, h2b, sqr, rsr, tmr, 7, tiles=[n]))
                ar.release(mk2)
                fm_sweep(w_in[:, 256:512], 256, epi_f(2), pre_slabs[1])
                flush_f()
                d_ccp = [kb.dsem(f"ccp{i}") for i in range(4)]
                pxout_b = [Buf() for _ in range(4)]

                def emit_cc(i):
                    kb.raw("pool", lambda e, i=i: e.collective_compute("AllGather", ALU.bypass,
                                                                  replica_groups=[[0, 1], [2, 3], [4, 5], [6, 7]],
                                                                  ins=[px_in[i].ap().opt()], outs=[px_out[i].ap().opt()]),
                           (d_ccp[i], 1), reads=[pxin_b[i]], writes=[pxout_b[i]])


                def epi_qk(idx0, scale):
                    def epi(j, n, bank, M):
                        st_, stb, std = stg.next()
                        kb.op("act", lambda e, o=st_, a=ps[bank][:, :]: e.activation(out=o, in_=a, func=AF.Identity, scale=scale),
                              reads=[psb[bank]], writes=[stb])
                        kb.dma("sp", qkT[n * 4:(n + 1) * 4, :, idx0 + j, :].rearrange("b p t -> p b t"),
                               st_.rearrange("p (b t) -> p b t", b=4), std, reads=[stb], writes=[qkT_b])
                    return epi

                fm_sweep(w_in[:, 512:768], 256, epi_qk(0, SQ))
                emit_cc(0)
                fm_sweep(w_in[:, 768:1024], 256, epi_qk(2, SQ))
                fm_sweep(w_in[:, 1024:1280], 256, epi_qk(4, 1.0))
                emit_cc(1)
                fm_sweep(w_in[:, 1280:1536], 256, epi_qk(6, 1.0))
                for b in range(NB):
                    kst, kstb, kstd = kvst.next()
                    for c3 in range(3):
                        bank = swb[0]
                        swb[0] = (swb[0] + 1) % 4
                        for kc in range(KC):
                            kb.op("pe", lambda e, o=ps[bank][:, :], a=h2[:, kc, b * 128:(b + 1) * 128], r=wkv[:, kc, c3 * 512:(c3 + 1) * 512], kc=kc:
                                  e.matmul(o, a, r, start=(kc == 0), stop=(kc == KC - 1)),
                                  reads=[wkvb, h2b[kc][b // 4]], writes=[psb[bank]], inc=(kc == KC - 1))
                        if c3 % 2 == 0:
                            kb.op("act", lambda e, o=kst[:, c3 * 512:(c3 + 1) * 512], a=ps[bank][:, :]: e.activation(out=o, in_=a, func=AF.Copy),
                                  reads=[psb[bank]], writes=[kstb])
                        else:
                            kb.op("dve", lambda e, o=kst[:, c3 * 512:(c3 + 1) * 512], a=ps[bank][:, :]: e.tensor_copy(out=o, in_=a),
                                  reads=[psb[bank]], writes=[kstb])
                    kb.dma("sp", kvt[b], kst, kstd, reads=[kstb], writes=[kvt_b])
                ar.release(mk3)

                def epi_alr(d_):
                    def epi(j, n, bank, M):
                        kb.op("act", lambda e, o=alr[d_][0:16, n * 512:(n + 1) * 512], a=ps[bank][0:16, :]:
                              e.activation(out=o, in_=a, func=AF.Copy), reads=[psb[bank]], writes=[alrb[d_]])
                    return epi

                fm_sweep(w_alr[:, 0:16], 16, epi_alr(0))
                fm_sweep(w_alr[:, 16:32], 16, epi_alr(1))

                def epi_gate(c0, func):
                    def epi(j, n, bank, M):
                        st_, stb, std = stg.next()
                        kb.op("act", lambda e, o=st_, a=ps[bank][:, :]: e.activation(out=o, in_=a, func=func),
                              reads=[psb[bank]], writes=[stb])
                        kb.dma("sp", gsp[c0 + j, :, n * 512:(n + 1) * 512], st_, std, reads=[stb], writes=[gsp_b])
                    return epi

                for i in range(4):
                    fm_sweep(w_in[:, 2560 + i * 256:2560 + (i + 1) * 256], 256, epi_gate(2 * i, AF.Silu))
                    if i in (0, 2):
                        emit_cc(2 + i // 2)
                for i in range(8):
                    fm_sweep(w_in[:, 3616 + i * 256:3616 + (i + 1) * 256], 256, epi_gate(8 + 2 * i, AF.Sigmoid))
                ar.release(mk)
                if debug == "E":
                    raise _Stop()

                mk = ar.mark()
                wal = ar.alloc(2 * 512 * 2, BF16, "p (a c) -> p a c", a=2, parts=17)
                walb = ar.buf()
                d_wal = kb.dsem("wal")
                kb.dma("pool", wal, walpha.rearrange("a p c -> p a c"), d_wal, writes=[walb])
                Sr = Ring(ar, kb, "S", 2, 4096, F32, "p (h v) -> p h v", dma=False, h=4)
                Sbfr = Ring(ar, kb, "sbf", 2, 2048, BF16, "p (h v) -> p h v", dma=False, h=4)
                qkr = Ring(ar, kb, "qk", 2, 2048, BF16, "p (a t) -> p a t", a=8)
                kvr = Ring(ar, kb, "kv", 4, 3072, BF16)
                ltr = Ring(ar, kb, "lt", 2, 1024, BF16, dma=False)
                e1r = Ring(ar, kb, "e1", 2, 2048, F32, dma=False)
                e2r = Ring(ar, kb, "e2", 2, 2048, F32, dma=False)
                e3r = Ring(ar, kb, "e3", 2, 2048, F32, dma=False)
                decr = Ring(ar, kb, "dec", 3, 32, F32, dma=False)
                qtr = Ring(ar, kb, "qt", 3, 1024, BF16, "p (h t) -> p h t", dma=False, h=4)
                ktr = Ring(ar, kb, "kt", 2, 1024, BF16, "p (h t) -> p h t", dma=False, h=4)
                khr = Ring(ar, kb, "kh", 2, 1024, BF16, dma=False)
                atr = Ring(ar, kb, "at", 2, 1024, BF16, dma=False)
                o1r = Ring(ar, kb, "o1", 2, 4096, F32, "p (c t) -> p c t", c=8)
                osr = Ring(ar, kb, "os", 2, 4096, F32, "p (c t) -> p c t", dma=False, c=8)
                oqr = Ring(ar, kb, "oq", 1, 2048, BF16, "p (c t) -> p c t", dma=False, c=8)
                rs2r = Ring(ar, kb, "rs2", 1, 2048, F32, dma=False)
                ogr = Ring(ar, kb, "og", 2, 2048, BF16, "p (c t) -> p c t", c=8)
                sx2 = ar.alloc(8192, F32, "p (r c) -> p r c", r=2)
                sx2b = ar.buf()
                o1sp_b, ogsp_b = Buf(), Buf()
                d_st = kb.dsem("st")
                st_b = Buf()
                sxin_b, sxout_b = Buf(), Buf()
                d_sx = kb.dsem("sx")
                d_ccs = kb.dsem("ccs")
                d_si = kb.dsem("sinit")
                d_sx2 = kb.dsem("sx2")
                cur = {}

                def set_state(S, Sb):
                    cur["S"], cur["Sb"] = S, Sb
                    sbf, sbfb, _ = Sbfr.next()
                    kb.op("act", lambda e, o=sbf, S=S: e.activation(out=o, in_=S, func=AF.Copy), reads=[Sb], writes=[sbfb])
                    cur["sbf"], cur["sbfb"] = sbf, sbfb

                steps = []
                for b in range(16):
                    steps.append(dict(b=b, dn=0, init="sinit" if b == 0 else None, save="xchg" if b == 15 else None))
                for sq_ in range(2):
                    b0 = 16 + 2 * sq_
                    steps.append(dict(b=b0, dn=0, init="zero", save=None))
                    steps.append(dict(b=b0 + 1, dn=0, init=None, save=st1[sq_]))
                for sq_ in range(2):
                    b0 = 16 + 2 * sq_
                    steps.append(dict(b=b0 + 1, dn=1, init="zero", save=None))
                    steps.append(dict(b=b0, dn=1, init=None, save=st2[sq_]))
                for b in range(15, -1, -1):
                    steps.append(dict(b=b, dn=1, init="exch" if b == 15 else None, save=None))
                NS = len(steps)

                def st_at(t, lag):
                    i = t - lag
                    return steps[i] if 0 <= i < NS else None

                for t in range(NS + 8):
                    c0, c1, c2, c3, c4, c5, c6, c7 = (st_at(t, k) for k in range(8))
                    if c6 is not None:
                        c = c6
                        b, dn = c["b"], c["dn"]
                        last = 127 if dn == 0 else 0
                        if c["init"] == "sinit":
                            S0, S0b, _ = Sr.next()
                            kb.dma("sp", S0, sinit, d_si, writes=[S0b])
                            set_state(S0, S0b)
                        elif c["init"] == "zero":
                            S0, S0b, _ = Sr.next()
                            kb.op("dve", lambda e, S0=S0: e.memset(S0, 0.0), writes=[S0b])
                            set_state(S0, S0b)
                        elif c["init"] == "exch":
                            kb.dma("sp", sx2, sx_out[:, :].rearrange("(r p) c -> p r c", p=128), d_sx2, reads=[sxout_b], writes=[sx2b])
                            S3, S3b, _ = Sr.next()
                            Sf = S3.rearrange("p h v -> p (h v)")
                            kb.op("dve", lambda e, Sf=Sf: e.tensor_scalar(out=Sf, in0=sx2[:, 0, :], scalar1=fv[:, FV_SEL:FV_SEL + 1], scalar2=None, op0=ALU.mult),
                                  reads=[sx2b, cbuf], writes=[S3b])
                            kb.op("dve", lambda e, Sf=Sf: e.scalar_tensor_tensor(out=Sf, in0=sx2[:, 1, :], scalar=fv[:, FV_SEL + 1:FV_SEL + 2], in1=Sf, op0=ALU.mult, op1=ALU.add),
                                  reads=[sx2b, cbuf], writes=[S3b])
                            set_state(S3, S3b)
                        S, Sb, sbf, sbfb = cur["S"], cur["Sb"], cur["sbf"], cur["sbfb"]
                        kv, kvb, qt, qtb, at, atb, dec, decb = (c[k] for k in ("kv", "kvb", "qt", "qtb", "at", "atb", "dec", "decb"))
                        S2, S2b, _ = Sr.next()
                        for h in range(4):
                            bank = 6 + h // 2
                            col = (h % 2) * 256
                            kb.op("dve", lambda e, bank=bank, col=col, h=h, S=S, S2=S2, dec=dec: e.scalar_tensor_tensor(out=S2[:, h, :], in0=S[:, h, :], scalar=dec[:, h:h + 1],
                                                                                              in1=ps[bank][:, col:col + 256], op0=ALU.mult, op1=ALU.add),
                                  reads=[Sb, decb, psb[bank]], writes=[S2b])
                        for h in range(4):
                            bank = 4 + h // 2
                            for cc_ in range(2):
                                col = ((h % 2) * 2 + cc_) * 128
                                v0 = 512 + h * 256 + cc_ * 128
                                kb.op("pe", lambda e, bank=bank, col=col, v0=v0, h=h, kv=kv, at=at: e.matmul(ps[bank][:, col:col + 128], kv[:, v0:v0 + 128], at[:, h * 128:(h + 1) * 128], start=True, stop=False),
                                      reads=[kvb, atb], writes=[psb[bank]], inc=False)
                                kb.op("pe", lambda e, bank=bank, col=col, cc_=cc_, h=h, sbf=sbf, qt=qt: e.matmul(ps[bank][:, col:col + 128], sbf[:, h, cc_ * 128:(cc_ + 1) * 128], qt[:, h, :], start=False, stop=True),
                                      reads=[sbfb, qtb], writes=[psb[bank]], inc=(h % 2 == 1 and cc_ == 1))
                    if c7 is not None and c7["dn"] == 1:
                        c = c7
                        oq, oqb, _ = oqr.next()
                        kb.op("act", lambda e, o=oq, a=c["os"]: e.activation(out=o, in_=a, func=AF.Square), reads=[c["osb"]], writes=[oqb])
                        c["oq"], c["oqb"] = oq, oqb
                    if c5 is not None:
                        c = c5
                        dn = c["dn"]
                        qt, qtb, kt, ktb, kh, khb, kv, kvb = (c[k] for k in ("qt", "qtb", "kt", "ktb", "kh", "khb", "kv", "kvb"))
                        for h in range(4):
                            kb.op("pe", lambda e, h=h, kt=kt, qt=qt: e.matmul(ps[3][:, h * 128:(h + 1) * 128], kt[:, h, :], qt[:, h, :], start=True, stop=True),
                                  reads=[ktb, qtb], writes=[psb[3]], inc=(h == 3))
                        at, atb, _ = atr.next()
                        kb.op("dve", lambda e, o=at, dn=dn: e.tensor_tensor(out=o, in0=ps[3][:, :], in1=maskc[:, dn, :], op=ALU.mult),
                              reads=[psb[3], cbuf], writes=[atb])
                        c["at"], c["atb"] = at, atb
                        for h in range(4):
                            bank = 6 + h // 2
                            col = (h % 2) * 256
                            kb.op("pe", lambda e, bank=bank, col=col, h=h, kh=kh, kv=kv: e.matmul(ps[bank][:, col:col + 256], kh[:, h * 128:(h + 1) * 128], kv[:, 512 + h * 256:512 + (h + 1) * 256], start=True, stop=True),
                                  reads=[khb, kvb], writes=[psb[bank]], inc=(h % 2 == 1))
                        o1, o1b, o1d = o1r.next()
                        c.update(o1=o1, o1b=o1b, o1d=o1d, o1_loaded=True)
                        if dn == 1:
                            kb.dma("sp", o1, o1sp[c["b"]], o1d, reads=[o1sp_b], writes=[o1b])
                    if c3 is not None:
                        c = c3
                        b = c["b"]
                        e1, e1b, _ = e1r.next()
                        e2, e2b, _ = e2r.next()
                        e3, e3b, _ = e3r.next()
                        kb.op("act", lambda e, o=e1: e.activation(out=o, in_=ps[1][:, :], func=AF.Exp), reads=[psb[1]], writes=[e1b])
                        kb.op("act", lambda e, o=e2: e.activation(out=o, in_=ps[1][:, :], func=AF.Exp, scale=-1.0), reads=[psb[1]], writes=[e2b])
                        kb.op("act", lambda e, o=e3: e.activation(out=o, in_=ps[2][:, :], func=AF.Exp), reads=[psb[2]], writes=[e3b])
                        qk, qkb, qkd = qkr.next()
                        kb.dma("sp", qk, qkT[b], qkd, reads=[qkT_b], writes=[qkb])
                        kv, kvb, kvd = kvr.next()
                        kb.dma("sp", kv, kvt[b], kvd, reads=[kvt_b], writes=[kvb])
                        c.update(e1=e1, e1b=e1b, e2=e2, e2b=e2b, e3=e3, e3b=e3b, qk=qk, qkb=qkb, kv=kv, kvb=kvb)
                    if c2 is not None:
                        c = c2
                        dn = c["dn"]
                        lt, ltb = c["lt"], c["ltb"]
                        for h in range(4):
                            kb.op("pe", lambda e, h=h, lt=lt, dn=dn: e.matmul(ps[1][:, h * 128:(h + 1) * 128], lt[:, h * 128:(h + 1) * 128], trib[:, 2 * dn, :], start=True, stop=True),
                                  reads=[ltb, cbuf], writes=[psb[1]], inc=(h == 3))
                        kb.op("pe", lambda e, lt=lt, dn=dn: e.matmul(ps[2][:, :], trib[:, 2 * dn + 1, :], lt, start=True, stop=True),
                              reads=[ltb, cbuf], writes=[psb[2]])
                    if c1 is not None:
                        c = c1
                        lt, ltb, _ = ltr.next()
                        kb.op("act", lambda e: e.activation(out=ps[0][:, :], in_=ps[0][:, :], func=AF.Exp, scale=-1.0), reads=[psb[0]], writes=[psb[0]])
                        kb.op("act", lambda e, o=lt: e.activation(out=o, in_=ps[0][:, :], func=AF.Ln, bias=1.0, scale=1.0), reads=[psb[0]], writes=[ltb])
                        c["lt"], c["ltb"] = lt, ltb
                    if c6 is not None:
                        c = c6
                        b, dn = c["b"], c["dn"]
                        set_state(S2, S2b)
                        if c["save"] == "xchg":
                            kb.dma("sp", sx_in[:, :], S2.rearrange("p h v -> p (h v)"), d_sx, reads=[S2b], writes=[sxin_b])
                            kb.raw("pool", lambda e: e.collective_compute("AllGather", ALU.bypass,
                                                                          replica_groups=[[0, 1], [2, 3], [4, 5], [6, 7]],
                                                                          ins=[sx_in.ap().opt()], outs=[sx_out.ap().opt()]),
                                   (d_ccs, 1), reads=[sxin_b], writes=[sxout_b])
                        elif c["save"] is not None:
                            kb.dma("sp", c["save"], S2, d_st, reads=[S2b], writes=[st_b])
                        if dn == 0:
                            o1, o1b, o1d = c["o1"], c["o1b"], c["o1d"]
                            kb.op("act", lambda e, o=o1[:, 0:4, :]: e.activation(out=o, in_=ps[4][:, :].rearrange("p (c t) -> p c t", c=4), func=AF.Copy),
                                  reads=[psb[4]], writes=[o1b])
                            kb.op("act", lambda e, o=o1[:, 4:8, :]: e.activation(out=o, in_=ps[5][:, :].rearrange("p (c t) -> p c t", c=4), func=AF.Copy),
                                  reads=[psb[5]], writes=[o1b])
                            kb.dma("sp", o1sp[b], o1, o1d, reads=[o1b], writes=[o1sp_b])
                        else:
                            os_, osb, _ = osr.next()
                            o1, o1b = c["o1"], c["o1b"]
                            if not c["o1_loaded"]:
                                kb.dma("sp", o1, o1sp[b], c["o1d"], reads=[o1sp_b], writes=[o1b])
                            for hh in range(2):
                                kb.op("dve", lambda e, hh=hh, o=os_[:, hh * 4:hh * 4 + 4, :], o1=o1: e.tensor_tensor(out=o, in0=ps[4 + hh][:, :].rearrange("p (c t) -> p c t", c=4),
                                                                                                         in1=o1[:, hh * 4:hh * 4 + 4, :], op=ALU.add),
                                      reads=[psb[4 + hh], o1b], writes=[osb])
                            c["os"], c["osb"] = os_, osb
                    if c4 is not None:
                        c = c4
                        dn = c["dn"]
                        last = 127 if dn == 0 else 0
                        qk, qkb, kv, kvb, e1, e1b, e2, e2b, e3, e3b = (c[k] for k in ("qk", "qkb", "kv", "kvb", "e1", "e1b", "e2", "e2b", "e3", "e3b"))
                        qt, qtb, _ = qtr.next()
                        kt, ktb, _ = ktr.next()
                        kh, khb, _ = khr.next()
                        dec, decb, _ = decr.next()
                        kb.op("dve", lambda e, o=qt, qk=qk, e1=e1: e.tensor_tensor(out=o, in0=qk[:, 0:4, :], in1=e1.rearrange("p (h t) -> p h t", h=4), op=ALU.mult),
                              reads=[qkb, e1b], writes=[qtb])
                        kb.op("dve", lambda e, o=kt, qk=qk, e2=e2: e.tensor_tensor(out=o, in0=qk[:, 4:8, :], in1=e2.rearrange("p (h t) -> p h t", h=4), op=ALU.mult),
                              reads=[qkb, e2b], writes=[ktb])
                        kb.op("dve", lambda e, o=kh, kv=kv, e3=e3: e.tensor_tensor(out=o, in0=kv[:, 0:512], in1=e3, op=ALU.mult),
                              reads=[kvb, e3b], writes=[khb])
                        kb.op("dve", lambda e, o=dec[:, 0:4], e1=e1, last=last: e.tensor_copy(out=o, in_=e1.rearrange("p (h t) -> p h t", h=4)[:, :, last]),
                              reads=[e1b], writes=[decb])
                        c.update(qt=qt, qtb=qtb, kt=kt, ktb=ktb, kh=kh, khb=khb, dec=dec, decb=decb)
                    if c0 is not None:
                        c = c0
                        b, dn = c["b"], c["dn"]
                        kb.op("pe", lambda e, b=b, dn=dn: e.matmul(ps[0][:, :], alr[dn][0:17, b * 128:(b + 1) * 128], wal[:, dn, :], start=True, stop=True),
                              reads=[alrb[dn], walb], writes=[psb[0]])
                    if c7 is not None and c7["dn"] == 1:
                        c = c7
                        oq, oqb, os_, osb = c["oq"], c["oqb"], c["os"], c["osb"]
                        for h in range(4):
                            for cc_ in range(2):
                                kb.op("pe", lambda e, h=h, cc_=cc_, oq=oq: e.matmul(ps[3][:, h * 128:(h + 1) * 128], ones, oq[:, 2 * h + cc_, :], start=(cc_ == 0), stop=(cc_ == 1)),
                                      reads=[oqb, cbuf], writes=[psb[3]], inc=(h == 3 and cc_ == 1))
                        rs2, rs2b, _ = rs2r.next()
                        kb.op("act", lambda e, o=rs2: e.activation(out=o, in_=ps[3][:, :], func=AF.Ln, scale=1.0 / 256.0, bias=epsc),
                              reads=[psb[3], cbuf], writes=[rs2b])
                        kb.op("act", lambda e, o=rs2: e.activation(out=o, in_=o, func=AF.Exp, scale=-0.5), reads=[rs2b], writes=[rs2b])
                        os4 = os_.rearrange("p (h c) t -> p h c t", h=4)
                        rsb4 = rs2.rearrange("p (h t) -> p h t", h=4).unsqueeze(2).broadcast_to([128, 4, 2, 128])
                        og, ogb, ogd = ogr.next()
                        kb.op("dve", lambda e, o=og.rearrange("p (h c) t -> p h c t", h=4), a=os4, r=rsb4: e.tensor_tensor(out=o, in0=a, in1=r, op=ALU.mult),
                              reads=[osb, rs2b], writes=[ogb])
                        kb.dma("sp", ogsp[c["b"]], og, ogd, reads=[ogb], writes=[ogsp_b])
                ar.release(mkM)
                if debug == "F":
                    raise _Stop()

                mk = ar.mark()
                mT = ar.alloc(4 * T * 2, BF16, "p (g t) -> p g t", g=4)
                mTb = ar.buf()
                wpf = ar.alloc(4 * 1024 * 2, BF16, "p (g c) -> p g c", g=4)
                wpg = ar.alloc(8 * 1024 * 2, BF16, "p (k c) -> p k c", k=8)
                wo = ar.alloc(8 * 1024 * 2, BF16, "p (k c) -> p k c", k=8)
                wpb = ar.buf()
                d_wp = kb.dsem("wp")
                for i in range(2):
                    kb.dma("pool", wpf[:, :, i * 512:(i + 1) * 512], w_pf[:, i * 512:(i + 1) * 512].rearrange("(g p) c -> p g c", p=128), d_wp, writes=[wpb])
                for i in range(4):
                    kb.dma("pool", wpg[:, :, i * 256:(i + 1) * 256], w_pg[:, i * 256:(i + 1) * 256].rearrange("(k p) c -> p k c", p=128), d_wp, writes=[wpb])
                for i in range(4):
                    kb.dma("pool", wo[:, :, i * 256:(i + 1) * 256], w_out[:, i * 256:(i + 1) * 256].rearrange("(k p) c -> p k c", p=128), d_wp, writes=[wpb])
                for c8 in range(8):
                    kb.op("act", lambda e, c8=c8: e.activation(out=wpg[:, c8, :], in_=wpg[:, c8, :], func=AF.Identity, scale=fv[:, FV_GN + c8:FV_GN + c8 + 1]),
                          reads=[wpb, cbuf], writes=[wpb])
                mkg = ar.mark()
                pcr = Ring(ar, kb, "pc", 8, 2048, BF16)
                tbr = Ring(ar, kb, "tb", 8, 2048, BF16, "p (a t) -> p a t", a=2)
                SC_S = 1.0 / math.sqrt(4096.0 * 128.0)
                SC_P = 1.0 / math.sqrt(256.0 * 128.0)
                for n in range(4):
                    for rc in range(32):
                        pc, pcb, pcd = pcr.next()
                        ti, rk, cc_ = rc // 8, (rc % 8) // 4, rc % 4
                        r0 = rk * 512 + cc_ * 128
                        kb.dma("sp", pc, px_out[ti][r0:r0 + 128, :], pcd, reads=[pxout_b[ti]], writes=[pcb])
                        tb, tbb, tbd = tbr.next()
                        kb.dma("sp", tb, tabs_d[rc][n], tbd, writes=[tbb])
                        for g in range(4):
                            kb.op("pe", lambda e, g=g, pc=pc, tb=tb, rc=rc: e.matmul(ps[g][:, :], pc[:, g * 256:g * 256 + 128], tb[:, 0, :], start=(rc == 0), stop=False),
                                  reads=[pcb, tbb], writes=[psb[g]], inc=False)
                            kb.op("pe", lambda e, g=g, pc=pc, tb=tb, rc=rc: e.matmul(ps[g][:, :], pc[:, g * 256 + 128:g * 256 + 256], tb[:, 1, :], start=False, stop=(rc == 31)),
                                  reads=[pcb, tbb], writes=[psb[g]], inc=(g == 3))
                    for g in range(4):
                        kb.op("act", lambda e, g=g, n=n: e.activation(out=mT[:, g, n * 512:(n + 1) * 512], in_=ps[g][:, :], func=AF.Identity, scale=SC_S),
                              reads=[psb[g]], writes=[mTb])
                for sq_ in range(2):
                    for rc in range(2):
                        pc, pcb, pcd = pcr.next()
                        r0 = sq_ * 256 + rc * 128
                        kb.dma("sp", pc, pp[r0:r0 + 128, :], pcd, reads=[pp_b], writes=[pcb])
                        tb, tbb, tbd = tbr.next()
                        kb.dma("sp", tb[:, :, 0:256], tabp_d[rc], tbd, writes=[tbb])
                        for g in range(4):
                            kb.op("pe", lambda e, g=g, pc=pc, tb=tb, rc=rc: e.matmul(ps[4 + g][:, 0:256], pc[:, g * 256:g * 256 + 128], tb[:, 0, 0:256], start=(rc == 0), stop=False),
                                  reads=[pcb, tbb], writes=[psb[4 + g]], inc=False)
                            kb.op("pe", lambda e, g=g, pc=pc, tb=tb, rc=rc: e.matmul(ps[4 + g][:, 0:256], pc[:, g * 256 + 128:g * 256 + 256], tb[:, 1, 0:256], start=False, stop=(rc == 1)),
                                  reads=[pcb, tbb], writes=[psb[4 + g]], inc=(g == 3))
                    for g in range(4):
                        t0 = TS + sq_ * 256
                        kb.op("act", lambda e, g=g, t0=t0: e.activation(out=mT[:, g, t0:t0 + 256], in_=ps[4 + g][:, 0:256], func=AF.Identity, scale=SC_P),
                              reads=[psb[4 + g]], writes=[mTb])
                ar.release(mkg)
                if debug == "G":
                    raise _Stop()

                srtr = Ring(ar, kb, "srt", 1, 8192, BF16, "p (c t) -> p c t", c=8)
                ogt = ar.alloc(8 * 512 * 2, BF16, "p (c b t) -> p c b t", c=8, b=4)
                ogtb = ar.buf()
                d_ogt = kb.dsem("ogt")
                ypre = ar.alloc(8 * 512 * 2, BF16, "p (m t) -> p m t", m=8)
                ypb = [ar.buf() for _ in range(8)]
                gtr = Ring(ar, kb, "gt", 4, 1024, BF16)
                t1r = Ring(ar, kb, "t1", 2, 2048, F32, dma=False)
                t2r = Ring(ar, kb, "t2", 2, 2048, F32, dma=False)
                hb_ = [0, 0, 0]
                for n in range(NT):
                    for bl in range(4):
                        kb.dma("sp", ogt[:, :, bl, :], ogsp[n * 4 + bl], d_ogt, reads=[ogsp_b], writes=[ogtb])
                    ogv = ogt.rearrange("p c b t -> p c (b t)")
                    srt, srtb, srtd = srtr.next()
                    kb.dma("sp", srt, gsp[0:8, :, n * 512:(n + 1) * 512].rearrange("c p t -> p c t"), srtd, reads=[gsp_b], writes=[srtb])
                    kb.op("dve", lambda e, o=ogv, g_=srt: e.tensor_tensor(out=o, in0=o, in1=g_, op=ALU.mult), reads=[ogtb, srtb], writes=[ogtb])
                    for m in range(8):
                        ba = hb_[0] % 2
                        hb_[0] += 1
                        bb = 2 + hb_[1] % 2
                        hb_[1] += 1
                        for g in range(4):
                            kb.op("pe", lambda e, ba=ba, g=g, m=m, n=n: e.matmul(ps[ba][:, :], wpf[:, g, m * 128:(m + 1) * 128], mT[:, g, n * 512:(n + 1) * 512], start=(g == 0), stop=(g == 3)),
                                  reads=[wpb, mTb], writes=[psb[ba]], inc=(g == 3))
                        for c8 in range(8):
                            kb.op("pe", lambda e, bb=bb, c8=c8, m=m: e.matmul(ps[bb][:, :], wpg[:, c8, m * 128:(m + 1) * 128], ogv[:, c8, :], start=(c8 == 0), stop=(c8 == 7)),
                                  reads=[wpb, ogtb], writes=[psb[bb]], inc=(c8 == 7))
                        ga, gab, gad = gtr.next()
                        kb.dma("sp", ga, gsp[8 + m, :, n * 512:(n + 1) * 512], gad, reads=[gsp_b], writes=[gab])
                        gb_, gbb, gbd = gtr.next()
                        kb.dma("sp", gb_, gsp[16 + m, :, n * 512:(n + 1) * 512], gbd, reads=[gsp_b], writes=[gbb])
                        t1, t1b, _ = t1r.next()
                        t2, t2b, _ = t2r.next()
                        kb.op("dve", lambda e, o=t1, ba=ba, ga=ga: e.tensor_tensor(out=o, in0=ps[ba][:, :], in1=ga, op=ALU.mult),
                              reads=[psb[ba], gab], writes=[t1b])
                        kb.op("dve", lambda e, o=t2, bb=bb, gb_=gb_: e.tensor_tensor(out=o, in0=ps[bb][:, :], in1=gb_, op=ALU.mult),
                              reads=[psb[bb], gbb], writes=[t2b])
                        kb.op("dve", lambda e, m=m, t1=t1, t2=t2: e.tensor_tensor(out=ypre[:, m, :], in0=t1, in1=t2, op=ALU.add),
                              reads=[t1b, t2b], writes=[ypb[m]])
                    for m2 in range(8):
                        bk = 4 + hb_[2] % 3
                        hb_[2] += 1
                        for m in range(8):
                            kb.op("pe", lambda e, bk=bk, m=m, m2=m2: e.matmul(ps[bk][:, :], wo[:, m, m2 * 128:(m2 + 1) * 128], ypre[:, m, :], start=(m == 0), stop=(m == 7)),
                                  reads=[wpb, ypb[m]], writes=[psb[bk]], inc=(m == 7))
                        xs = x[:, m2, n * 512:(n + 1) * 512]
                        kb.op("dve", lambda e, o=xs, bk=bk, s=scal(5, tsel(n), m2): e.scalar_tensor_tensor(out=o, in0=ps[bk][:, :], scalar=s, in1=o, op0=ALU.mult, op1=ALU.add),
                              reads=[psb[bk], scb], writes=xbufs(m2, n))
                ar.release(mk)
                dump_x("mix")

                if debug != "mix":
                    ffn(2, wg2, wu2, wd2)
                    dump_x("ffn2")

                    mk = ar.mark()
                    sqr = Ring(ar, kb, "sq", 2, 1024, BF16, dma=False)
                    rsr = Ring(ar, kb, "rs", 2, 2048, F32, dma=False)
                    xnr = Ring(ar, kb, "xn", 2, 8 * 2048, F32, "p (m t) -> p m t", dma=False, m=8)
                    osg = Ring(ar, kb, "osg", 3, 4096, F32)
                    yb_ = Buf()
                    tb_ = [0]
                    for n in range(NT):
                        rs, rsb = rms_stats(n, sqr, rsr, 7)
                        xn, xnb, _ = xnr.next()
                        for m in range(KC):
                            kb.op("dve", lambda e, o=xn[:, m, :], a=x[:, m, n * 512:(n + 1) * 512], s=fv[:, FV_NF + m:FV_NF + m + 1], r=rs:
                                  e.scalar_tensor_tensor(out=o, in0=a, scalar=s, in1=r, op0=ALU.mult, op1=ALU.mult),
                                  reads=xbufs(m, n) + [rsb, cbuf], writes=[xnb])
                        for bl in range(4):
                            og_, ogb_, ogd_ = osg.next()
                            for hh in range(2):
                                bank = tb_[0] % 4
                                tb_[0] += 1
                                for i in range(4):
                                    m = hh * 4 + i
                                    kb.op("pe", lambda e, bank=bank, i=i, m=m, xn=xn, bl=bl: e.transpose(ps[bank][:, i * 128:(i + 1) * 128], xn[:, m, bl * 128:(bl + 1) * 128], ident),
                                          reads=[xnb, cbuf], writes=[psb[bank]], inc=(i == 3))
                                if hh == 0:
                                    kb.op("act", lambda e, o=og_[:, 0:512], bank=bank: e.activation(out=o, in_=ps[bank][:, :], func=AF.Copy),
                                          reads=[psb[bank]], writes=[ogb_])
                                else:
                                    kb.op("dve", lambda e, o=og_[:, 512:1024], bank=bank: e.tensor_copy(out=o, in_=ps[bank][:, :]),
                                          reads=[psb[bank]], writes=[ogb_])
                            r0 = (n * 4 + bl) * 128
                            kb.dma("sp", yout[r0:r0 + 128, :], og_, ogd_, reads=[ogb_], writes=[yb_])
                    ar.release(mk)

        try:
            _mixer_and_rest()
        except _Stop:
            dump_x(debug)

        kb.wait_all("sp")
        kb.replay(block)
    return nc


def _bf16(a):
    return np.asarray(a, dtype=np.float32).astype(ml_dtypes.bfloat16)


def _grid_pos():
    rows = 4096 // 64
    row = np.repeat(np.arange(rows, dtype=np.float32), 64)
    col = np.tile(np.arange(64, dtype=np.float32), rows)
    n_freq = D // 4
    omega = (np.float32(10000.0) ** (-np.arange(n_freq, dtype=np.float32) / np.float32(n_freq))).astype(np.float32)
    ra = row[:, None] * omega
    ca = col[:, None] * omega
    return np.concatenate([np.sin(ra), np.cos(ra), np.sin(ca), np.cos(ca)], axis=-1).astype(np.float32)


def _consts():
    ident = np.eye(128, dtype=np.float32)
    i = np.arange(128)
    L1 = (i[:, None] <= i[None, :]).astype(np.float32)
    U1 = (i[:, None] > i[None, :]).astype(np.float32)
    L2 = (i[:, None] >= i[None, :]).astype(np.float32)
    U2 = (i[:, None] < i[None, :]).astype(np.float32)
    tri = np.stack([L1, U1, L2, U2], 1) * np.float32(-1.0 / 16.0)
    mask = np.stack([np.tile(L1, (1, 4)), np.tile(L2, (1, 4))], 1)
    c = np.arange(128)
    ang = 2 * np.pi * ((c[:, None] * c[None, :]) % 128) / 128.0
    cs128 = np.concatenate([np.cos(ang), -np.sin(ang)], 1)
    return ident, tri.astype(np.float32), _bf16(mask), _bf16(cs128)


def _tables(flip):
    rc = np.arange(32)[:, None]
    p = np.arange(128)[None, :]
    local = (rc // 8) * 512 + (rc % 4) * 128 + p
    rpos = np.where(((rc % 8) // 4) == 0, local, 4095 - local).reshape(4096)
    j = np.arange(2048)
    cpos = (4095 - j) if flip else j
    ang = 2 * np.pi * ((rpos[:, None].astype(np.int64) * cpos[None, :]) % 4096) / 4096.0
    tabs = np.stack([np.cos(ang), np.sin(ang)], 1)
    tabs = np.ascontiguousarray(tabs.reshape(32, 128, 2, 4, 512).transpose(0, 3, 1, 2, 4))
    rp = np.arange(256)
    ppos = (255 - rp) if flip else rp
    angp = 2 * np.pi * ((ppos[:, None] * ppos[None, :]) % 256) / 256.0
    tabp = np.stack([np.cos(angp), np.sin(angp)], 1).reshape(2, 128, 2, 256)
    return _bf16(tabs), _bf16(tabp)


def _fm(vec):
    return np.ascontiguousarray(np.asarray(vec, np.float32).reshape(-1, 128).T)


def make_in_maps(inp):
    f32 = lambda a: np.ascontiguousarray(np.asarray(a, dtype=np.float32))
    pos = _grid_pos()
    ident, tri, mask, cs128 = _consts()
    tabs = [_tables(False), _tables(True)]
    shared = {
        "w_ada": f32(inp["w_ada"][0]),
        "w_ffn1_gate": f32(inp["w_ffn1_gate"][0]), "w_ffn1_up": f32(inp["w_ffn1_up"][0]), "w_ffn1_down": f32(inp["w_ffn1_down"][0]),
        "w_ffn2_gate": f32(inp["w_ffn2_gate"][0]), "w_ffn2_up": f32(inp["w_ffn2_up"][0]), "w_ffn2_down": f32(inp["w_ffn2_down"][0]),
        "w_in": f32(inp["w_in"][0]),
        "w_proj_fourier": f32(inp["w_proj_fourier"][0]), "w_proj_gla": f32(inp["w_proj_gla"][0]), "w_out": f32(inp["w_out"][0]),
        "ident": ident, "tri": tri, "maskc": mask, "cs128": cs128,
    }
    w_in = shared["w_in"]
    alr_f, alr_b = w_in[:, 3584:3600], w_in[:, 3600:3616]
    wa_f = np.concatenate([f32(inp["w_alpha_fwd"][0]), f32(inp["b_alpha_fwd"][0])[None]], 0)
    wa_b = np.concatenate([f32(inp["w_alpha_bwd"][0]), f32(inp["b_alpha_bwd"][0])[None]], 0)
    b_ada = _fm(inp["b_ada"][0])
    maps = []
    for c in range(8):
        b, half = c // 2, c % 2
        flip = half == 1
        sl = slice(half * TS, (half + 1) * TS)
        xs = f32(inp["x_sample"][b, sl])
        ps_ = pos[sl]
        xp = [f32(inp["x_prompt"][2 * c]), f32(inp["x_prompt"][2 * c + 1])]
        if flip:
            xs, ps_ = xs[::-1], ps_[::-1]
            xp = [a[::-1] for a in xp]
        xin = np.ascontiguousarray(np.concatenate([xs] + xp, 0))
        posT = np.ascontiguousarray(ps_.reshape(16, 128, KC, 128).transpose(0, 3, 2, 1))
        st = inp["state_gla_bwd"] if flip else inp["state_gla_fwd"]
        sinit = np.ascontiguousarray(f32(st[b, 0]).transpose(1, 0, 2))
        cc = np.stack([f32(inp["c"][b]), f32(inp["c_ctx"])], 0)
        cT = np.ascontiguousarray(cc.reshape(2, KC, 128).transpose(2, 1, 0))
        fvec = np.zeros((128, FV_N), np.float32)
        fvec[:, FV_BADA:FV_BADA + 144] = np.repeat(b_ada, 2, axis=1)
        fvec[:, FV_N1:FV_N1 + 8] = _fm(inp["norm_ffn1"][0])
        fvec[:, FV_N2:FV_N2 + 8] = _fm(inp["norm_mix"][0])
        fvec[:, FV_N3:FV_N3 + 8] = _fm(inp["norm_ffn2"][0])
        fvec[:, FV_NF:FV_NF + 8] = _fm(inp["final_norm"])
        fvec[:, FV_GN:FV_GN + 8] = _fm(inp["gla_norm"][0])
        fvec[:, FV_SEL:FV_SEL + 2] = np.array([1.0, 0.0] if flip else [0.0, 1.0], np.float32)
        m = dict(shared)
        m.update({
            "xin": xin, "posT": posT, "sinit": sinit, "cT": cT, "fvec": fvec,
            "w_alr": np.ascontiguousarray(np.concatenate([alr_b, alr_f] if flip else [alr_f, alr_b], 1)),
            "walpha": np.ascontiguousarray(np.stack([wa_b, wa_f] if flip else [wa_f, wa_b], 0)),
            "tabs": tabs[half][0], "tabp": tabs[half][1],
        })
        maps.append(m)
    return maps


def assemble(results):
    y_prompt = np.zeros((16, 256, D), np.float32)
    y_sample = np.zeros((4, 4096, D), np.float32)
    nsf = np.zeros((16, 1, 4, 128, 256), np.float32)
    nsb = np.zeros((16, 1, 4, 128, 256), np.float32)
    for c in range(8):
        r = results[c]
        b, half = c // 2, c % 2
        flip = half == 1
        y = r["yout"]
        ys, yp = y[:TS], [y[TS:TS + 256], y[TS + 256:]]
        if flip:
            ys = ys[::-1]
            yp = [a[::-1] for a in yp]
        y_sample[b, half * TS:(half + 1) * TS] = ys
        for s in range(2):
            y_prompt[2 * c + s] = yp[s]
            a1 = r["st1"][s].transpose(1, 0, 2)
            a2 = r["st2"][s].transpose(1, 0, 2)
            if flip:
                a1, a2 = a2, a1
            nsf[2 * c + s, 0] = a1
            nsb[2 * c + s, 0] = a2
    return y_prompt, y_sample, nsf, nsb


def kernel(**inputs):
    nc = build_nc()
    in_maps = make_in_maps(inputs)
    res = run_bass_kernel_spmd(nc, in_maps, core_ids=list(range(8)))
    return assemble(res.results)
```

```python
import math
from contextlib import ExitStack

import ml_dtypes
import numpy as np

import concourse.bass as bass
import concourse.mybir as mybir
from concourse.bass_utils import run_bass_kernel_spmd

F32 = mybir.dt.float32
BF16 = mybir.dt.bfloat16
AF = mybir.ActivationFunctionType
ALU = mybir.AluOpType

D = 1024
KC = 8
DFF = 2816
NFF = 22
T = 2560
NT = 5
NB = 20
TS = 2048
NCOLS_IN = 5664
EPS = 1e-6
NMODV = 72

FV_BADA = 0
FV_N1 = 144
FV_N2 = 152
FV_N3 = 160
FV_NF = 168
FV_GN = 176
FV_SEL = 184
FV_N = 186


class _Stop(Exception):
    pass


class Buf:
    __slots__ = ("w", "r")

    def __init__(self, seed=None):
        self.w = {}
        self.r = dict(seed) if seed else {}


class DSem:
    def __init__(self, h):
        self.h = h
        self.n = 0


class KB:
    ENG = ("pe", "act", "dve", "pool", "sp")

    def __init__(self, nc, es):
        self.nc = nc
        self.es = es
        self.q = {e: [] for e in self.ENG}
        self.sem = {e: es.enter_context(nc.semaphore("s_" + e)) for e in self.ENG}
        self.cnt = {e: 0 for e in self.ENG}
        self.waited = {e: {} for e in self.ENG}
        self.pend_r = {e: [] for e in self.ENG}
        self.pend_w = {e: [] for e in self.ENG}
        self.dsems = []
        self.semname = {}
        for e in self.ENG:
            self.semname[id(self.sem[e])] = e

    def dsem(self, name):
        self.nds = getattr(self, "nds", 0) + 1
        d = DSem(self.es.enter_context(self.nc.semaphore(f"d{self.nds}_{name}")))
        self.dsems.append(d)
        return d

    def snapshot(self):
        s = {}
        for e in self.ENG:
            if self.cnt[e]:
                s[id(self.sem[e])] = (self.sem[e], self.cnt[e])
        for d in self.dsems:
            if d.n:
                s[id(d.h)] = (d.h, d.n)
        return s

    def _need(self, eng, waits, ev):
        sem, val = ev
        k = id(sem)
        if eng == "pe" and sem is self.sem["pe"]:
            return
        if self.waited[eng].get(k, 0) >= val:
            return
        if k in waits and waits[k][1] >= val:
            return
        waits[k] = (sem, val)

    def _deps(self, eng, reads, writes):
        waits = {}
        for b in reads:
            for ev in b.w.values():
                self._need(eng, waits, ev)
        for b in writes:
            for ev in b.w.values():
                self._need(eng, waits, ev)
            for ev in b.r.values():
                self._need(eng, waits, ev)
        for k, (sem, val) in waits.items():
            self.waited[eng][k] = val
        return list(waits.values())

    def _commit(self, ev, reads, writes):
        k = id(ev[0])
        for b in reads:
            b.r[k] = ev
        for b in writes:
            b.w[k] = ev

    def op(self, eng, fn, reads=(), writes=(), inc=True):
        reads = list(reads)
        writes = list(writes)
        waits = self._deps(eng, reads, writes)
        if inc:
            self.cnt[eng] += 1
            ev = (self.sem[eng], self.cnt[eng])
            self._commit(ev, reads + self.pend_r[eng], writes + self.pend_w[eng])
            self.pend_r[eng] = []
            self.pend_w[eng] = []
            self.q[eng].append((waits, fn, (self.sem[eng], 1)))
        else:
            self.pend_r[eng] += reads
            self.pend_w[eng] += writes
            self.q[eng].append((waits, fn, None))

    def dma(self, queue, out, in_, dsem, reads=(), writes=(), **kw):
        reads = list(reads)
        writes = list(writes)
        waits = self._deps(queue, reads, writes)
        dsem.n += 16
        ev = (dsem.h, dsem.n)
        self._commit(ev, reads, writes)
        self.q[queue].append((waits, lambda e: e.dma_start(out=out, in_=in_, **kw), (dsem.h, 16)))

    def raw(self, queue, fn, dsem_inc, reads=(), writes=()):
        reads = list(reads)
        writes = list(writes)
        waits = self._deps(queue, reads, writes)
        d, n = dsem_inc
        d.n += n
        ev = (d.h, d.n)
        self._commit(ev, reads, writes)
        self.q[queue].append((waits, fn, (d.h, n)))

    def wait_all(self, eng):
        waits = []
        for k, (sem, val) in self.snapshot().items():
            if sem is self.sem[eng]:
                continue
            if self.waited[eng].get(k, 0) >= val:
                continue
            self.waited[eng][k] = val
            waits.append((sem, val))
        self.q[eng].append((waits, None, None))

    def replay(self, block):
        def run(eng):
            def f(e):
                for waits, fn, inc in self.q[eng]:
                    for sem, val in waits:
                        e.wait_ge(sem, val)
                    if fn is None:
                        continue
                    ins = fn(e)
                    if inc is not None:
                        ins.then_inc(inc[0], inc[1])
            return f

        block.tensor(run("pe"))
        block.scalar(run("act"))
        block.vector(run("dve"))
        block.gpsimd(run("pool"))
        block.sync(run("sp"))


class Arena:
    def __init__(self, kb, tens, nbytes):
        self.kb = kb
        self.t = tens
        self.top = 0
        self.cap = nbytes
        self.seed = None

    def mark(self):
        return self.top

    def release(self, mark):
        self.top = mark
        self.seed = self.kb.snapshot()

    def alloc(self, nbytes, dtype, pat=None, parts=128, **kw):
        assert nbytes % 4 == 0
        off = self.top
        self.top += (nbytes + 31) // 32 * 32
        assert self.top <= self.cap, f"SBUF arena overflow {self.top} > {self.cap}"
        ap = self.t[0:parts, off // 4:(off + nbytes) // 4]
        if dtype != F32:
            ap = ap.bitcast(dtype)
        if pat:
            ap = ap.rearrange(pat, **kw)
        return ap

    def buf(self):
        return Buf(self.seed)


class Ring:
    def __init__(self, ar, kb, name, n, nbytes, dtype, pat=None, parts=128, dma=True, **kw):
        self.aps = [ar.alloc(nbytes, dtype, pat, parts, **kw) for _ in range(n)]
        self.bufs = [ar.buf() for _ in range(n)]
        self.ds = [kb.dsem(f"{name}{i}") for i in range(n)] if dma else [None] * n
        self.i = -1
        self.n = n

    def next(self):
        self.i = (self.i + 1) % self.n
        return self.aps[self.i], self.bufs[self.i], self.ds[self.i]


def build_nc(debug=None):
    nc = bass.Bass("TRN2", target_bir_lowering=False)

    def din(name, shape, dt=F32):
        return nc.dram_tensor(name, list(shape), dt, kind="ExternalInput").ap()

    def dout(name, shape, dt=F32):
        return nc.dram_tensor(name, list(shape), dt, kind="ExternalOutput").ap()

    xin = din("xin", [T, D])
    posT = din("posT", [16, 128, KC, 128])
    sinit = din("sinit", [128, 4, 256])
    cT = din("cT", [128, KC, 2])
    fvec = din("fvec", [128, FV_N])
    w_ada = din("w_ada", [D, 9216])
    wg1 = din("w_ffn1_gate", [D, DFF]); wu1 = din("w_ffn1_up", [D, DFF]); wd1 = din("w_ffn1_down", [DFF, D])
    wg2 = din("w_ffn2_gate", [D, DFF]); wu2 = din("w_ffn2_up", [D, DFF]); wd2 = din("w_ffn2_down", [DFF, D])
    w_in = din("w_in", [D, NCOLS_IN])
    w_alr = din("w_alr", [D, 32])
    walpha = din("walpha", [2, 17, 512])
    w_pf = din("w_proj_fourier", [512, D])
    w_pg = din("w_proj_gla", [D, D])
    w_out = din("w_out", [D, D])
    ident_d = din("ident", [128, 128])
    tri_d = din("tri", [128, 4, 128])
    mask_d = din("maskc", [128, 2, 512], BF16)
    cs128_d = din("cs128", [128, 256], BF16)
    tabs_d = din("tabs", [32, 4, 128, 2, 512], BF16)
    tabp_d = din("tabp", [2, 128, 2, 256], BF16)

    yout = dout("yout", [T, D])
    st1 = dout("st1", [2, 128, 4, 256])
    st2 = dout("st2", [2, 128, 4, 256])
    dbg = dout("dbg", [128, 8 * T]) if debug else None
    dbgh = dout("dbgh", [128, 8 * T], BF16) if debug == "h" else None

    px_in = [nc.dram_tensor(f"px_in{i}", [512, 1024], BF16) for i in range(4)]
    px_out = [nc.dram_tensor(f"px_out{i}", [1024, 1024], BF16) for i in range(4)]
    pp = nc.dram_tensor("pp", [512, 1024], BF16)
    sx_in = nc.dram_tensor("sx_in", [128, 1024], F32)
    sx_out = nc.dram_tensor("sx_out", [256, 1024], F32)
    qkT = nc.dram_tensor("qkT", [NB, 128, 8, 128], BF16)
    kvt = nc.dram_tensor("kvt", [NB, 128, 1536], BF16)
    gsp = nc.dram_tensor("gsp", [24, 128, T], BF16)
    o1sp = nc.dram_tensor("o1sp", [NB, 128, 8, 128], F32)
    ogsp = nc.dram_tensor("ogsp", [NB, 128, 8, 128], BF16)

    es = ExitStack()
    with es:
        ARENA_BYTES = 212000
        arena_t = es.enter_context(nc.sbuf_tensor("arena", [128, ARENA_BYTES // 4], F32))
        ps = [es.enter_context(nc.psum_tensor(f"ps{i}", [128, 512], F32)) for i in range(8)]
        kb = KB(nc, es)
        ar = Arena(kb, arena_t, ARENA_BYTES)
        psb = [Buf() for _ in range(8)]
        block = es.enter_context(nc.Block())

        x = ar.alloc(KC * T * 4, F32, "p (m t) -> p m t", m=KC)
        xb = [[Buf() for _ in range(NB)] for _ in range(KC)]
        ident = ar.alloc(512, F32)
        ones = ar.alloc(256, BF16)
        tri = ar.alloc(2048, F32, "p (a b) -> p a b", a=4)
        trib = ar.alloc(1024, BF16, "p (a b) -> p a b", a=4)
        maskc = ar.alloc(2048, BF16, "p (a b) -> p a b", a=2)
        cs128 = ar.alloc(512, BF16)
        epsc = ar.alloc(32, F32)[:, 0:1]
        fv = ar.alloc(FV_N * 4, F32)
        modfm = ar.alloc(NMODV * 2 * 4, F32, "p (c v) -> p c v", v=2)
        sc = ar.alloc(9 * 16 * 4, F32)

        def scal(k, v, m):
            c = (k * 2 + v) * 8 + m
            return sc[:, c:c + 1]
        cbuf = Buf()
        d_const = kb.dsem("const")
        wslab = Ring(ar, kb, "ws", 4, 4096, BF16)

        def xbufs(m, n):
            return [xb[m][4 * n + i] for i in range(4)]

        def tsel(n):
            return 0 if n < 4 else 1

        for dst, src in ((ident, ident_d), (tri, tri_d), (maskc, mask_d), (cs128, cs128_d), (fv, fvec)):
            kb.dma("sp", dst, src, d_const, writes=[cbuf])
        kb.op("dve", lambda e: e.memset(ones, 1.0), writes=[cbuf])
        kb.op("dve", lambda e: e.memset(epsc, EPS), writes=[cbuf])
        kb.op("dve", lambda e: e.tensor_copy(out=trib, in_=tri), reads=[cbuf], writes=[cbuf])

        mkA = ar.mark()
        ctf = ar.alloc(KC * 2 * 4, F32, "p (k v) -> p k v", v=2)
        ctb = ar.alloc(KC * 2 * 2, BF16, "p (k v) -> p k v", v=2)
        ctbuf = ar.buf()
        d_ct = kb.dsem("ct")
        kb.dma("sp", ctf, cT, d_ct, writes=[ctbuf])
        kb.op("act", lambda e: e.activation(out=ctb, in_=ctf, func=AF.Silu), reads=[ctbuf], writes=[ctbuf])
        ADABANK = 7
        adar = Ring(ar, kb, "ada", 3, 4096, BF16)
        mstr = Ring(ar, kb, "mst", 2, 1024, F32, parts=2, dma=False)
        scb = Buf()
        ada_pending = []

        def ada_load(cb):
            slab, slb, sld = adar.next()
            sv = slab.rearrange("p (k c) -> p k c", k=KC)
            kb.dma("pool", sv, w_ada[:, cb * 256:(cb + 1) * 256].rearrange("(k p) c -> p k c", p=128), sld, writes=[slb])
            ada_pending.append((cb, sv, slb))

        def ada_compute():
            cb, sv, slb = ada_pending.pop(0)
            for kc in range(KC):
                kb.op("pe", lambda e, a=ctb[:, kc, :], r=sv[:, kc, :], kc=kc:
                      e.matmul(ps[ADABANK][0:2, 0:256], a, r, start=(kc == 0), stop=(kc == KC - 1)),
                      reads=[ctbuf, slb], writes=[psb[ADABANK]], inc=(kc == KC - 1))
            ms, msb, _ = mstr.next()
            kb.op("act", lambda e, o=ms: e.activation(out=o, in_=ps[ADABANK][0:2, 0:256], func=AF.Copy), reads=[psb[ADABANK]], writes=[msb])
            for j in range(2):
                kb.op("pe", lambda e, j=j, ms=ms: e.matmul(ps[ADABANK][:, 256 + 2 * j:258 + 2 * j], ms[:, j * 128:(j + 1) * 128], ident[0:2, 0:2], start=True, stop=True),
                      reads=[msb, cbuf], writes=[psb[ADABANK]], inc=(j == 1))
            kb.op("dve", lambda e, cb=cb: e.tensor_tensor(out=modfm[:, 2 * cb:2 * cb + 2, :], in0=ps[ADABANK][:, 256:260].rearrange("p (c v) -> p c v", v=2),
                                                         in1=fv[:, FV_BADA + 4 * cb:FV_BADA + 4 * cb + 4].rearrange("p (c v) -> p c v", v=2), op=ALU.add),
                  reads=[psb[ADABANK], cbuf], writes=[scb])

        def derive_scalars(li, norm_part=True, gate_part=True):
            fvn = (FV_N1, FV_N2, FV_N3)[li]
            base = li * 24
            for v in range(2):
                c_a = ((3 * li) * 2 + v) * 8
                c_s = ((3 * li + 1) * 2 + v) * 8
                c_g = ((3 * li + 2) * 2 + v) * 8
                if norm_part:
                    kb.op("dve", lambda e, o=sc[:, c_a:c_a + 8], a=modfm[:, base + 8:base + 16, v], g=fv[:, fvn:fvn + 8]:
                          e.scalar_tensor_tensor(out=o, in0=a, scalar=1.0, in1=g, op0=ALU.add, op1=ALU.mult),
                          reads=[scb, cbuf], writes=[scb])
                    kb.op("dve", lambda e, o=sc[:, c_s:c_s + 8], a=modfm[:, base:base + 8, v]:
                          e.tensor_copy(out=o, in_=a), reads=[scb], writes=[scb])
                if gate_part:
                    gsc = 1.0 if li == 1 else 0.5
                    kb.op("dve", lambda e, o=sc[:, c_g:c_g + 8], a=modfm[:, base + 16:base + 24, v], gsc=gsc:
                          e.tensor_scalar(out=o, in0=a, scalar1=gsc, scalar2=None, op0=ALU.mult),
                          reads=[scb], writes=[scb])

        NPRE = 8
        for cb in range(3):
            ada_load(cb)
        ada_ld = [3]

        mk = ar.mark()
        tokr = Ring(ar, kb, "tok", 3, 4096, F32)
        posr = Ring(ar, kb, "pos", 2, 4096, F32, "p (m t) -> p m t", m=KC)
        for b in range(NB):
            tok, tokb, tokd = tokr.next()
            kb.dma("sp", tok, xin[b * 128:(b + 1) * 128, :], tokd, writes=[tokb])
            if b < 16:
                pos, posb, posd = posr.next()
                kb.dma("sp", pos, posT[b], posd, writes=[posb])
            for hh in range(2):
                bank = hh
                for i in range(4):
                    m = hh * 4 + i
                    kb.op("pe", lambda e, o=ps[bank][:, i * 128:(i + 1) * 128], a=tok[:, m * 128:(m + 1) * 128]:
                          e.transpose(o, a, ident),
                          reads=[tokb, cbuf], writes=[psb[bank]], inc=(i == 3))
                pv = ps[bank][:, :].rearrange("p (a b) -> p a b", a=4)
                xo = x[:, hh * 4:hh * 4 + 4, b * 128:(b + 1) * 128]
                wr = [xb[hh * 4 + i][b] for i in range(4)]
                if b < 16:
                    kb.op("dve", lambda e, o=xo, a=pv, c=pos[:, hh * 4:hh * 4 + 4, :]:
                          e.tensor_tensor(out=o, in0=a, in1=c, op=ALU.add),
                          reads=[psb[bank], posb], writes=wr)
                else:
                    kb.op("act", lambda e, o=xo, a=pv: e.activation(out=o, in_=a, func=AF.Copy),
                          reads=[psb[bank]], writes=wr)
            if b < NPRE:
                ada_compute()
                if ada_ld[0] < NPRE + 3:
                    ada_load(ada_ld[0])
                    ada_ld[0] += 1
        ar.release(mk)

        derive_scalars(0, gate_part=False)
        ada_next = [NPRE + 3]
        gate1_done = [False]

        def ada_hook(k=2):
            if not gate1_done[0]:
                while ada_pending:
                    ada_compute()
                ada_load(ada_next[0])
                ada_next[0] += 1
                ada_compute()
                derive_scalars(0, norm_part=False)
                gate1_done[0] = True
            for _ in range(3):
                if ada_pending:
                    ada_compute()
            for _ in range(k):
                if ada_next[0] < 36:
                    ada_load(ada_next[0])
                    ada_next[0] += 1

        def rms_stats(n, sqr, rsr, ssbank):
            for m in range(KC):
                sq, sqb, _ = sqr.next()
                kb.op("act", lambda e, o=sq, a=x[:, m, n * 512:(n + 1) * 512]: e.activation(out=o, in_=a, func=AF.Square),
                      reads=xbufs(m, n), writes=[sqb])
                kb.op("pe", lambda e, a=sq, m=m: e.matmul(ps[ssbank][:, :], ones, a,
                                                          start=(m == 0), stop=(m == KC - 1)),
                      reads=[sqb, cbuf], writes=[psb[ssbank]], inc=True)
            rs, rsb, _ = rsr.next()
            kb.op("act", lambda e, o=rs: e.activation(out=o, in_=ps[ssbank][:, :], func=AF.Ln, scale=1.0 / D, bias=epsc),
                  reads=[psb[ssbank], cbuf], writes=[rsb])
            kb.op("act", lambda e, o=rs: e.activation(out=o, in_=o, func=AF.Exp, scale=-0.5), reads=[rsb], writes=[rsb])
            return rs, rsb

        def norm_mod(li, h, hb, sqr, rsr, tmr, ssbank, tiles=None):
            for n in (range(NT) if tiles is None else tiles):
                v = tsel(n)
                rs, rsb = rms_stats(n, sqr, rsr, ssbank)
                for m in range(KC):
                    tm, tmb, _ = tmr.next()
                    kb.op("dve", lambda e, o=tm, a=x[:, m, n * 512:(n + 1) * 512], s=scal(3 * li, v, m), r=rs:
                          e.scalar_tensor_tensor(out=o, in0=a, scalar=s, in1=r, op0=ALU.mult, op1=ALU.mult),
                          reads=xbufs(m, n) + [rsb, scb], writes=[tmb])
                    if m % 2 == 0:
                        kb.op("act", lambda e, o=h[:, m, n * 512:(n + 1) * 512], a=tm, s=scal(3 * li + 1, v, m):
                              e.activation(out=o, in_=a, func=AF.Identity, bias=s, scale=1.0),
                              reads=[tmb, scb], writes=[hb[m][n]])
                    else:
                        kb.op("dve", lambda e, o=h[:, m, n * 512:(n + 1) * 512], a=tm, s=scal(3 * li + 1, v, m):
                              e.tensor_scalar(out=o, in0=a, scalar1=s, scalar2=None, op0=ALU.add),
                              reads=[tmb, scb], writes=[hb[m][n]])

        def ffn(li, wg, wu, wd):
            mk = ar.mark()
            h = ar.alloc(KC * T * 2, BF16, "p (m t) -> p m t", m=KC)
            hb = [[ar.buf() for _ in range(NT)] for _ in range(KC)]
            sqr = Ring(ar, kb, "sq", 2, 1024, BF16, dma=False)
            rsr = Ring(ar, kb, "rs", 2, 2048, F32, dma=False)
            tmr = Ring(ar, kb, "tm", 3, 2048, F32, dma=False)
            Ar = Ring(ar, kb, "A", 2, 2 * T * 2, BF16, "p (j t) -> p j t", j=2, dma=False)
            if debug == "h" and li == 0:
                norm_mod(li, h, hb, sqr, rsr, tmr, 7)
            if debug == "h" and li == 0:
                d_dh = kb.dsem("dbgh")
                kb.dma("sp", dbgh, h.rearrange("p m t -> p (m t)"), d_dh, reads=[hb[m][n] for m in range(KC) for n in range(NT)])
                raise _Stop()
            NG = NFF // 2
            hall = [hb[m][n] for m in range(KC) for n in range(NT)]
            gk = 3 * li + 2

            def load_gu(g):
                sg, sgb, sgd = wslab.next()
                su, sub, sud = wslab.next()
                sgv = sg.rearrange("p (k c) -> p k c", k=KC)
                suv = su.rearrange("p (k c) -> p k c", k=KC)
                kb.dma("pool", sgv, wg[:, g * 256:(g + 1) * 256].rearrange("(k p) c -> p k c", p=128), sgd, writes=[sgb])
                kb.dma("pool", suv, wu[:, g * 256:(g + 1) * 256].rearrange("(k p) c -> p k c", p=128), sud, writes=[sub])
                return sgv, sgb, suv, sub

            def load_d(g):
                sd, sdb, sdd = wdr.next()
                sdv = sd.rearrange("p (j c) -> p j c", j=2)
                kb.dma("pool", sdv, wd[g * 256:(g + 1) * 256, :].rearrange("(j p) c -> p j c", p=128), sdd, writes=[sdb])
                return sdv, sdb

            wdr = Ring(ar, kb, "wd", 2, 4096, BF16)
            gu_next = load_gu(0)
            prev = None
            pbank = 0
            ybank = [0]

            def y_group(prev, m, n):
                pA, pAb, (sdv, sdb) = prev
                bk = 4 + ybank[0]
                ybank[0] = (ybank[0] + 1) % (3 if li == 0 else 4)
                for j in range(2):
                    kb.op("pe", lambda e, o=ps[bk][:, :], a=sdv[:, j, m * 128:(m + 1) * 128], r=pA[:, j, n * 512:(n + 1) * 512], j=j:
                          e.matmul(o, a, r, start=(j == 0), stop=(j == 1)),
                          reads=[sdb, pAb], writes=[psb[bk]], inc=(j == 1))
                xs = x[:, m, n * 512:(n + 1) * 512]
                kb.op("dve", lambda e, o=xs, a=ps[bk][:, :], s=scal(gk, tsel(n), m):
                      e.scalar_tensor_tensor(out=o, in0=a, scalar=s, in1=o, op0=ALU.mult, op1=ALU.add),
                      reads=[psb[bk], scb], writes=xbufs(m, n))

            for g in range(NG + 1):
                ylist = [(m, n) for m in range(KC) for n in range(NT)] if prev is not None else []
                if g < NG:
                    sgv, sgb, suv, sub = gu_next
                    dcur = load_d(g)
                    Aap, Ab, _ = Ar.next()
                    for j in range(2):
                        for n in range(NT):
                            if g == 0 and j == 0 and not (debug == "h" and li == 0):
                                if n == 0:
                                    norm_mod(li, h, hb, sqr, rsr, tmr, 7, tiles=[0])
                                if n + 1 < NT:
                                    norm_mod(li, h, hb, sqr, rsr, tmr, 7, tiles=[n + 1])
                            bg, bu = 2 * pbank, 2 * pbank + 1
                            pbank ^= 1
                            for (bk, sv, sb_) in ((bg, sgv, sgb), (bu, suv, sub)):
                                for kc in range(KC):
                                    kb.op("pe", lambda e, o=ps[bk][:, :], a=sv[:, kc, j * 128:(j + 1) * 128], r=h[:, kc, n * 512:(n + 1) * 512], kc=kc:
                                          e.matmul(o, a, r, start=(kc == 0), stop=(kc == KC - 1)),
                                          reads=[sb_] + [hb[kc][n]], writes=[psb[bk]], inc=(kc == KC - 1))
                                if bk == bg:
                                    for _ in range(2):
                                        if ylist:
                                            y_group(prev, *ylist.pop(0))
                            tm, tmb, _ = tmr.next()
                            kb.op("act", lambda e, o=tm, a=ps[bg][:, :]: e.activation(out=o, in_=a, func=AF.Silu),
                                  reads=[psb[bg]], writes=[tmb])
                            kb.op("dve", lambda e, o=Aap[:, j, n * 512:(n + 1) * 512], a=tm, b=ps[bu][:, :]:
                                  e.tensor_tensor(out=o, in0=a, in1=b, op=ALU.mult),
                                  reads=[tmb, psb[bu]], writes=[Ab])
                            for _ in range(2):
                                if ylist:
                                    y_group(prev, *ylist.pop(0))
                    if g + 1 < NG:
                        gu_next = load_gu(g + 1)
                    if li == 0:
                        ada_hook()
                    cur = (Aap, Ab, dcur)
                else:
                    cur = None
                while ylist:
                    y_group(prev, *ylist.pop(0))
                prev = cur
            ar.release(mk)

        def dump_x(tag):
            if debug == tag:
                d_dbg = kb.dsem("dbg")
                kb.dma("sp", dbg, x.rearrange("p m t -> p (m t)"), d_dbg,
                       reads=[xb[m][b] for m in range(KC) for b in range(NB)])

        try:
            ffn(0, wg1, wu1, wd1)
        except _Stop:
            pass
        while ada_pending or ada_next[0] < 36:
            ada_hook(3)
        derive_scalars(1)
        derive_scalars(2)
        ar.release(mkA)
        dump_x("ffn1")

        def _mixer_and_rest():
            mkM = ar.mark()
            alr = [ar.alloc(T * 2, BF16, parts=17) for _ in range(2)]
            alrb = [ar.buf() for _ in range(2)]
            SQ = 1.0 / math.sqrt(128.0)

            if debug != "ffn1":
                mk = ar.mark()
                h2 = ar.alloc(KC * T * 2, BF16, "p (m t) -> p m t", m=KC)
                h2b = [[ar.buf() for _ in range(NT)] for _ in range(KC)]
                stg = Ring(ar, kb, "stg", 3, 1024, BF16)
                pstg = Ring(ar, kb, "pstg", 3, 2048, BF16, "p (b c) -> p b c", b=4)
                kvst = Ring(ar, kb, "kvst", 2, 3072, BF16)
                pre_slabs = []

                def slab_prefetch(wsrc, ncol):
                    slab, slb, sld = wslab.next()
                    sv = slab.rearrange("p (k c) -> p k c", k=KC)[:, :, 0:ncol]
                    kb.dma("pool", sv, wsrc.rearrange("(k p) c -> p k c", p=128), sld, writes=[slb])
                    return sv, slb

                pre_slabs.append(slab_prefetch(w_in[:, 0:256], 256))
                pre_slabs.append(slab_prefetch(w_in[:, 256:512], 256))
                mk3 = ar.mark()
                wkv = ar.alloc(KC * 1536 * 2, BF16, "p (k c) -> p k c", k=KC)
                wkvb = ar.buf()
                d_wkv = kb.dsem("wkv")
                for i in range(6):
                    c0 = 1024 + i * 256
                    kb.dma("pool", wkv[:, :, i * 256:(i + 1) * 256], w_in[:, c0:c0 + 256].rearrange("(k p) c -> p k c", p=128),
                           d_wkv, writes=[wkvb])
                mk2 = ar.mark()
                sqr = Ring(ar, kb, "sq", 2, 1024, BF16, dma=False)
                rsr = Ring(ar, kb, "rs", 2, 2048, F32, dma=False)
                tmr = Ring(ar, kb, "tm", 3, 2048, F32, dma=False)

                for d_ in range(2):
                    kb.op("dve", lambda e, o=alr[d_]: e.memset(o, 1.0), writes=[alrb[d_]])

                pp_b, qkT_b, kvt_b, gsp_b = Buf(), Buf(), Buf(), Buf()
                pxin_b = [Buf() for _ in range(4)]
                swb = [0]

                def fm_sweep(wsrc, ncol, epi, pre=None, pre_tile=None):
                    sv, slb = pre if pre is not None else slab_prefetch(wsrc, ncol)
                    nj = max(1, ncol // 128)
                    M = min(ncol, 128)
                    for j in range(nj):
                        for n in range(NT):
                            if pre_tile is not None and j == 0:
                                pre_tile(n)
                            bank = swb[0]
                            swb[0] = (swb[0] + 1) % 4
                            for kc in range(KC):
                                kb.op("pe", lambda e, o=ps[bank][0:M, :], a=sv[:, kc, j * M:(j + 1) * M], r=h2[:, kc, n * 512:(n + 1) * 512], kc=kc:
                                      e.matmul(o, a, r, start=(kc == 0), stop=(kc == KC - 1)),
                                      reads=[slb, h2b[kc][n]], writes=[psb[bank]], inc=(kc == KC - 1))
                            epi(j, n, bank, M)

                s1b = [0]

                pend_f = []

                def flush_f():
                    while pend_f:
                        g, n, fs, fsb = pend_f.pop(0)
                        pst, pstb, pstd = pstg.next()
                        for bl in range(4):
                            b2 = 4 + s1b[0]
                            s1b[0] = (s1b[0] + 1) % 4
                            kb.op("pe", lambda e, o=ps[b2][:, 0:256], a=fs[:, bl * 128:(bl + 1) * 128]:
                                  e.matmul(o, a, cs128, start=True, stop=True),
                                  reads=[fsb, cbuf], writes=[psb[b2]])
                            kb.op("dve", lambda e, o=pst[:, bl, :], a=ps[b2][:, 0:256]: e.tensor_copy(out=o, in_=a),
                                  reads=[psb[b2]], writes=[pstb])
                        dstt = px_in[n] if n < 4 else pp
                        kb.dma("sp", dstt[:, g * 256:(g + 1) * 256].rearrange("(b p) c -> p b c", p=128), pst, pstd,
                               reads=[pstb], writes=[pxin_b[n] if n < 4 else pp_b])

                def epi_f(gbase):
                    def epi(j, n, bank, M):
                        g = gbase + j
                        fs, fsb, _ = stg.next()
                        kb.op("act", lambda e, o=fs, a=ps[bank][:, :]: e.activation(out=o, in_=a, func=AF.Copy),
                              reads=[psb[bank]], writes=[fsb])
                        flush_f()
                        pend_f.append((g, n, fs, fsb))
                    return epi

                fm_sweep(w_in[:, 0:256], 256, epi_f(0), pre_slabs[0],
                         pre_tile=lambda n: norm_mod(1, h2, h2b, sqr, rsr, tmr, 7,
                                                     tiles=([0] if n == 0 else []) + ([n + 1] if n + 1 < NT else [])))
                ar.release(mk2)
                fm_sweep(w_in[:, 256:512], 256, epi_f(2), pre_slabs[1])
                flush_f()
                d_ccp = [kb.dsem(f"ccp{i}") for i in range(4)]
                pxout_b = [Buf() for _ in range(4)]

                def emit_cc(i):
                    kb.raw("pool", lambda e, i=i: e.collective_compute("AllGather", ALU.bypass,
                                                                  replica_groups=[[0, 1], [2, 3], [4, 5], [6, 7]],
                                                                  ins=[px_in[i].ap().opt()], outs=[px_out[i].ap().opt()]),
                           (d_ccp[i], 1), reads=[pxin_b[i]], writes=[pxout_b[i]])


                def epi_qk(idx0, scale):
                    def epi(j, n, bank, M):
                        st_, stb, std = stg.next()
                        kb.op("act", lambda e, o=st_, a=ps[bank][:, :]: e.activation(out=o, in_=a, func=AF.Identity, scale=scale),
                              reads=[psb[bank]], writes=[stb])
                        kb.dma("sp", qkT[n * 4:(n + 1) * 4, :, idx0 + j, :].rearrange("b p t -> p b t"),
                               st_.rearrange("p (b t) -> p b t", b=4), std, reads=[stb], writes=[qkT_b])
                    return epi

                fm_sweep(w_in[:, 512:768], 256, epi_qk(0, SQ))
                emit_cc(0)
                fm_sweep(w_in[:, 768:1024], 256, epi_qk(2, SQ))
                fm_sweep(w_in[:, 1024:1280], 256, epi_qk(4, 1.0))
                emit_cc(1)
                fm_sweep(w_in[:, 1280:1536], 256, epi_qk(6, 1.0))
                for b in range(NB):
                    kst, kstb, kstd = kvst.next()
                    for c3 in range(3):
                        bank = swb[0]
                        swb[0] = (swb[0] + 1) % 4
                        for kc in range(KC):
                            kb.op("pe", lambda e, o=ps[bank][:, :], a=h2[:, kc, b * 128:(b + 1) * 128], r=wkv[:, kc, c3 * 512:(c3 + 1) * 512], kc=kc:
                                  e.matmul(o, a, r, start=(kc == 0), stop=(kc == KC - 1)),
                                  reads=[wkvb, h2b[kc][b // 4]], writes=[psb[bank]], inc=(kc == KC - 1))
                        if c3 % 2 == 0:
                            kb.op("act", lambda e, o=kst[:, c3 * 512:(c3 + 1) * 512], a=ps[bank][:, :]: e.activation(out=o, in_=a, func=AF.Copy),
                                  reads=[psb[bank]], writes=[kstb])
                        else:
                            kb.op("dve", lambda e, o=kst[:, c3 * 512:(c3 + 1) * 512], a=ps[bank][:, :]: e.tensor_copy(out=o, in_=a),
                                  reads=[psb[bank]], writes=[kstb])
                    kb.dma("sp", kvt[b], kst, kstd, reads=[kstb], writes=[kvt_b])
                ar.release(mk3)

                def epi_alr(d_):
                    def epi(j, n, bank, M):
                        kb.op("act", lambda e, o=alr[d_][0:16, n * 512:(n + 1) * 512], a=ps[bank][0:16, :]:
                              e.activation(out=o, in_=a, func=AF.Copy), reads=[psb[bank]], writes=[alrb[d_]])
                    return epi

                fm_sweep(w_alr[:, 0:16], 16, epi_alr(0))
                fm_sweep(w_alr[:, 16:32], 16, epi_alr(1))

                def epi_gate(c0, func):
                    def epi(j, n, bank, M):
                        st_, stb, std = stg.next()
                        kb.op("act", lambda e, o=st_, a=ps[bank][:, :]: e.activation(out=o, in_=a, func=func),
                              reads=[psb[bank]], writes=[stb])
                        kb.dma("sp", gsp[c0 + j, :, n * 512:(n + 1) * 512], st_, std, reads=[stb], writes=[gsp_b])
                    return epi

                for i in range(4):
                    fm_sweep(w_in[:, 2560 + i * 256:2560 + (i + 1) * 256], 256, epi_gate(2 * i, AF.Silu))
                    if i in (0, 2):
                        emit_cc(2 + i // 2)
                for i in range(8):
                    fm_sweep(w_in[:, 3616 + i * 256:3616 + (i + 1) * 256], 256, epi_gate(8 + 2 * i, AF.Sigmoid))
                ar.release(mk)
                if debug == "E":
                    raise _Stop()

                mk = ar.mark()
                wal = ar.alloc(2 * 512 * 2, BF16, "p (a c) -> p a c", a=2, parts=17)
                walb = ar.buf()
                d_wal = kb.dsem("wal")
                kb.dma("pool", wal, walpha.rearrange("a p c -> p a c"), d_wal, writes=[walb])
                Sr = Ring(ar, kb, "S", 2, 4096, F32, "p (h v) -> p h v", dma=False, h=4)
                Sbfr = Ring(ar, kb, "sbf", 2, 2048, BF16, "p (h v) -> p h v", dma=False, h=4)
                qkr = Ring(ar, kb, "qk", 2, 2048, BF16, "p (a t) -> p a t", a=8)
                kvr = Ring(ar, kb, "kv", 4, 3072, BF16)
                ltr = Ring(ar, kb, "lt", 2, 1024, BF16, dma=False)
                e1r = Ring(ar, kb, "e1", 2, 2048, F32, dma=False)
                e2r = Ring(ar, kb, "e2", 2, 2048, F32, dma=False)
                e3r = Ring(ar, kb, "e3", 2, 2048, F32, dma=False)
                decr = Ring(ar, kb, "dec", 3, 32, F32, dma=False)
                qtr = Ring(ar, kb, "qt", 3, 1024, BF16, "p (h t) -> p h t", dma=False, h=4)
                ktr = Ring(ar, kb, "kt", 2, 1024, BF16, "p (h t) -> p h t", dma=False, h=4)
                khr = Ring(ar, kb, "kh", 2, 1024, BF16, dma=False)
                atr = Ring(ar, kb, "at", 2, 1024, BF16, dma=False)
                o1r = Ring(ar, kb, "o1", 2, 4096, F32, "p (c t) -> p c t", c=8)
                osr = Ring(ar, kb, "os", 2, 4096, F32, "p (c t) -> p c t", dma=False, c=8)
                oqr = Ring(ar, kb, "oq", 1, 2048, BF16, "p (c t) -> p c t", dma=False, c=8)
                rs2r = Ring(ar, kb, "rs2", 1, 2048, F32, dma=False)
                ogr = Ring(ar, kb, "og", 2, 2048, BF16, "p (c t) -> p c t", c=8)
                sx2 = ar.alloc(8192, F32, "p (r c) -> p r c", r=2)
                sx2b = ar.buf()
                o1sp_b, ogsp_b = Buf(), Buf()
                d_st = kb.dsem("st")
                st_b = Buf()
                sxin_b, sxout_b = Buf(), Buf()
                d_sx = kb.dsem("sx")
                d_ccs = kb.dsem("ccs")
                d_si = kb.dsem("sinit")
                d_sx2 = kb.dsem("sx2")
                cur = {}

                def set_state(S, Sb):
                    cur["S"], cur["Sb"] = S, Sb
                    sbf, sbfb, _ = Sbfr.next()
                    kb.op("act", lambda e, o=sbf, S=S: e.activation(out=o, in_=S, func=AF.Copy), reads=[Sb], writes=[sbfb])
                    cur["sbf"], cur["sbfb"] = sbf, sbfb

                steps = []
                for b in range(16):
                    steps.append(dict(b=b, dn=0, init="sinit" if b == 0 else None, save="xchg" if b == 15 else None))
                for sq_ in range(2):
                    b0 = 16 + 2 * sq_
                    steps.append(dict(b=b0, dn=0, init="zero", save=None))
                    steps.append(dict(b=b0 + 1, dn=0, init=None, save=st1[sq_]))
                for sq_ in range(2):
                    b0 = 16 + 2 * sq_
                    steps.append(dict(b=b0 + 1, dn=1, init="zero", save=None))
                    steps.append(dict(b=b0, dn=1, init=None, save=st2[sq_]))
                for b in range(15, -1, -1):
                    steps.append(dict(b=b, dn=1, init="exch" if b == 15 else None, save=None))
                NS = len(steps)

                def st_at(t, lag):
                    i = t - lag
                    return steps[i] if 0 <= i < NS else None

                for t in range(NS + 8):
                    c0, c1, c2, c3, c4, c5, c6, c7 = (st_at(t, k) for k in range(8))
                    if c6 is not None:
                        c = c6
                        b, dn = c["b"], c["dn"]
                        last = 127 if dn == 0 else 0
                        if c["init"] == "sinit":
                            S0, S0b, _ = Sr.next()
                            kb.dma("sp", S0, sinit, d_si, writes=[S0b])
                            set_state(S0, S0b)
                        elif c["init"] == "zero":
                            S0, S0b, _ = Sr.next()
                            kb.op("dve", lambda e, S0=S0: e.memset(S0, 0.0), writes=[S0b])
                            set_state(S0, S0b)
                        elif c["init"] == "exch":
                            kb.dma("sp", sx2, sx_out[:, :].rearrange("(r p) c -> p r c", p=128), d_sx2, reads=[sxout_b], writes=[sx2b])
                            S3, S3b, _ = Sr.next()
                            Sf = S3.rearrange("p h v -> p (h v)")
                            kb.op("dve", lambda e, Sf=Sf: e.tensor_scalar(out=Sf, in0=sx2[:, 0, :], scalar1=fv[:, FV_SEL:FV_SEL + 1], scalar2=None, op0=ALU.mult),
                                  reads=[sx2b, cbuf], writes=[S3b])
                            kb.op("dve", lambda e, Sf=Sf: e.scalar_tensor_tensor(out=Sf, in0=sx2[:, 1, :], scalar=fv[:, FV_SEL + 1:FV_SEL + 2], in1=Sf, op0=ALU.mult, op1=ALU.add),
                                  reads=[sx2b, cbuf], writes=[S3b])
                            set_state(S3, S3b)
                        S, Sb, sbf, sbfb = cur["S"], cur["Sb"], cur["sbf"], cur["sbfb"]
                        kv, kvb, qt, qtb, at, atb, dec, decb = (c[k] for k in ("kv", "kvb", "qt", "qtb", "at", "atb", "dec", "decb"))
                        S2, S2b, _ = Sr.next()
                        for h in range(4):
                            bank = 6 + h // 2
                            col = (h % 2) * 256
                            kb.op("dve", lambda e, bank=bank, col=col, h=h, S=S, S2=S2, dec=dec: e.scalar_tensor_tensor(out=S2[:, h, :], in0=S[:, h, :], scalar=dec[:, h:h + 1],
                                                                                              in1=ps[bank][:, col:col + 256], op0=ALU.mult, op1=ALU.add),
                                  reads=[Sb, decb, psb[bank]], writes=[S2b])
                        for h in range(4):
                            bank = 4 + h // 2
                            for cc_ in range(2):
                                col = ((h % 2) * 2 + cc_) * 128
                                v0 = 512 + h * 256 + cc_ * 128
                                kb.op("pe", lambda e, bank=bank, col=col, v0=v0, h=h, kv=kv, at=at: e.matmul(ps[bank][:, col:col + 128], kv[:, v0:v0 + 128], at[:, h * 128:(h + 1) * 128], start=True, stop=False),
                                      reads=[kvb, atb], writes=[psb[bank]], inc=False)
                                kb.op("pe", lambda e, bank=bank, col=col, cc_=cc_, h=h, sbf=sbf, qt=qt: e.matmul(ps[bank][:, col:col + 128], sbf[:, h, cc_ * 128:(cc_ + 1) * 128], qt[:, h, :], start=False, stop=True),
                                      reads=[sbfb, qtb], writes=[psb[bank]], inc=(h % 2 == 1 and cc_ == 1))
                    if c7 is not None and c7["dn"] == 1:
                        c = c7
                        oq, oqb, _ = oqr.next()
                        kb.op("act", lambda e, o=oq, a=c["os"]: e.activation(out=o, in_=a, func=AF.Square), reads=[c["osb"]], writes=[oqb])
                        c["oq"], c["oqb"] = oq, oqb
                    if c5 is not None:
                        c = c5
                        dn = c["dn"]
                        qt, qtb, kt, ktb, kh, khb, kv, kvb = (c[k] for k in ("qt", "qtb", "kt", "ktb", "kh", "khb", "kv", "kvb"))
                        for h in range(4):
                            kb.op("pe", lambda e, h=h, kt=kt, qt=qt: e.matmul(ps[3][:, h * 128:(h + 1) * 128], kt[:, h, :], qt[:, h, :], start=True, stop=True),
                                  reads=[ktb, qtb], writes=[psb[3]], inc=(h == 3))
                        at, atb, _ = atr.next()
                        kb.op("dve", lambda e, o=at, dn=dn: e.tensor_tensor(out=o, in0=ps[3][:, :], in1=maskc[:, dn, :], op=ALU.mult),
                              reads=[psb[3], cbuf], writes=[atb])
                        c["at"], c["atb"] = at, atb
                        for h in range(4):
                            bank = 6 + h // 2
                            col = (h % 2) * 256
                            kb.op("pe", lambda e, bank=bank, col=col, h=h, kh=kh, kv=kv: e.matmul(ps[bank][:, col:col + 256], kh[:, h * 128:(h + 1) * 128], kv[:, 512 + h * 256:512 + (h + 1) * 256], start=True, stop=True),
                                  reads=[khb, kvb], writes=[psb[bank]], inc=(h % 2 == 1))
                        o1, o1b, o1d = o1r.next()
                        c.update(o1=o1, o1b=o1b, o1d=o1d, o1_loaded=True)
                        if dn == 1:
                            kb.dma("sp", o1, o1sp[c["b"]], o1d, reads=[o1sp_b], writes=[o1b])
                    if c3 is not None:
                        c = c3
                        b = c["b"]
                        e1, e1b, _ = e1r.next()
                        e2, e2b, _ = e2r.next()
                        e3, e3b, _ = e3r.next()
                        kb.op("act", lambda e, o=e1: e.activation(out=o, in_=ps[1][:, :], func=AF.Exp), reads=[psb[1]], writes=[e1b])
                        kb.op("act", lambda e, o=e2: e.activation(out=o, in_=ps[1][:, :], func=AF.Exp, scale=-1.0), reads=[psb[1]], writes=[e2b])
                        kb.op("act", lambda e, o=e3: e.activation(out=o, in_=ps[2][:, :], func=AF.Exp), reads=[psb[2]], writes=[e3b])
                        qk, qkb, qkd = qkr.next()
                        kb.dma("sp", qk, qkT[b], qkd, reads=[qkT_b], writes=[qkb])
                        kv, kvb, kvd = kvr.next()
                        kb.dma("sp", kv, kvt[b], kvd, reads=[kvt_b], writes=[kvb])
                        c.update(e1=e1, e1b=e1b, e2=e2, e2b=e2b, e3=e3, e3b=e3b, qk=qk, qkb=qkb, kv=kv, kvb=kvb)
                    if c2 is not None:
                        c = c2
                        dn = c["dn"]
                        lt, ltb = c["lt"], c["ltb"]
                        for h in range(4):
                            kb.op("pe", lambda e, h=h, lt=lt, dn=dn: e.matmul(ps[1][:, h * 128:(h + 1) * 128], lt[:, h * 128:(h + 1) * 128], trib[:, 2 * dn, :], start=True, stop=True),
                                  reads=[ltb, cbuf], writes=[psb[1]], inc=(h == 3))
                        kb.op("pe", lambda e, lt=lt, dn=dn: e.matmul(ps[2][:, :], trib[:, 2 * dn + 1, :], lt, start=True, stop=True),
                              reads=[ltb, cbuf], writes=[psb[2]])
                    if c1 is not None:
                        c = c1
                        lt, ltb, _ = ltr.next()
                        kb.op("act", lambda e: e.activation(out=ps[0][:, :], in_=ps[0][:, :], func=AF.Exp, scale=-1.0), reads=[psb[0]], writes=[psb[0]])
                        kb.op("act", lambda e, o=lt: e.activation(out=o, in_=ps[0][:, :], func=AF.Ln, bias=1.0, scale=1.0), reads=[psb[0]], writes=[ltb])
                        c["lt"], c["ltb"] = lt, ltb
                    if c6 is not None:
                        c = c6
                        b, dn = c["b"], c["dn"]
                        set_state(S2, S2b)
                        if c["save"] == "xchg":
                            kb.dma("sp", sx_in[:, :], S2.rearrange("p h v -> p (h v)"), d_sx, reads=[S2b], writes=[sxin_b])
                            kb.raw("pool", lambda e: e.collective_compute("AllGather", ALU.bypass,
                                                                          replica_groups=[[0, 1], [2, 3], [4, 5], [6, 7]],
                                                                          ins=[sx_in.ap().opt()], outs=[sx_out.ap().opt()]),
                                   (d_ccs, 1), reads=[sxin_b], writes=[sxout_b])
                        elif c["save"] is not None:
                            kb.dma("sp", c["save"], S2, d_st, reads=[S2b], writes=[st_b])
                        if dn == 0:
                            o1, o1b, o1d = c["o1"], c["o1b"], c["o1d"]
                            kb.op("act", lambda e, o=o1[:, 0:4, :]: e.activation(out=o, in_=ps[4][:, :].rearrange("p (c t) -> p c t", c=4), func=AF.Copy),
                                  reads=[psb[4]], writes=[o1b])
                            kb.op("act", lambda e, o=o1[:, 4:8, :]: e.activation(out=o, in_=ps[5][:, :].rearrange("p (c t) -> p c t", c=4), func=AF.Copy),
                                  reads=[psb[5]], writes=[o1b])
                            kb.dma("sp", o1sp[b], o1, o1d, reads=[o1b], writes=[o1sp_b])
                        else:
                            os_, osb, _ = osr.next()
                            o1, o1b = c["o1"], c["o1b"]
                            if not c["o1_loaded"]:
                                kb.dma("sp", o1, o1sp[b], c["o1d"], reads=[o1sp_b], writes=[o1b])
                            for hh in range(2):
                                kb.op("dve", lambda e, hh=hh, o=os_[:, hh * 4:hh * 4 + 4, :], o1=o1: e.tensor_tensor(out=o, in0=ps[4 + hh][:, :].rearrange("p (c t) -> p c t", c=4),
                                                                                                         in1=o1[:, hh * 4:hh * 4 + 4, :], op=ALU.add),
                                      reads=[psb[4 + hh], o1b], writes=[osb])
                            c["os"], c["osb"] = os_, osb
                    if c4 is not None:
                        c = c4
                        dn = c["dn"]
                        last = 127 if dn == 0 else 0
                        qk, qkb, kv, kvb, e1, e1b, e2, e2b, e3, e3b = (c[k] for k in ("qk", "qkb", "kv", "kvb", "e1", "e1b", "e2", "e2b", "e3", "e3b"))
                        qt, qtb, _ = qtr.next()
                        kt, ktb, _ = ktr.next()
                        kh, khb, _ = khr.next()
                        dec, decb, _ = decr.next()
                        kb.op("dve", lambda e, o=qt, qk=qk, e1=e1: e.tensor_tensor(out=o, in0=qk[:, 0:4, :], in1=e1.rearrange("p (h t) -> p h t", h=4), op=ALU.mult),
                              reads=[qkb, e1b], writes=[qtb])
                        kb.op("dve", lambda e, o=kt, qk=qk, e2=e2: e.tensor_tensor(out=o, in0=qk[:, 4:8, :], in1=e2.rearrange("p (h t) -> p h t", h=4), op=ALU.mult),
                              reads=[qkb, e2b], writes=[ktb])
                        kb.op("dve", lambda e, o=kh, kv=kv, e3=e3: e.tensor_tensor(out=o, in0=kv[:, 0:512], in1=e3, op=ALU.mult),
                              reads=[kvb, e3b], writes=[khb])
                        kb.op("dve", lambda e, o=dec[:, 0:4], e1=e1, last=last: e.tensor_copy(out=o, in_=e1.rearrange("p (h t) -> p h t", h=4)[:, :, last]),
                              reads=[e1b], writes=[decb])
                        c.update(qt=qt, qtb=qtb, kt=kt, ktb=ktb, kh=kh, khb=khb, dec=dec, decb=decb)
                    if c0 is not None:
                        c = c0
                        b, dn = c["b"], c["dn"]
                        kb.op("pe", lambda e, b=b, dn=dn: e.matmul(ps[0][:, :], alr[dn][0:17, b * 128:(b + 1) * 128], wal[:, dn, :], start=True, stop=True),
                              reads=[alrb[dn], walb], writes=[psb[0]])
                    if c7 is not None and c7["dn"] == 1:
                        c = c7
                        oq, oqb, os_, osb = c["oq"], c["oqb"], c["os"], c["osb"]
                        for h in range(4):
                            for cc_ in range(2):
                                kb.op("pe", lambda e, h=h, cc_=cc_, oq=oq: e.matmul(ps[3][:, h * 128:(h + 1) * 128], ones, oq[:, 2 * h + cc_, :], start=(cc_ == 0), stop=(cc_ == 1)),
                                      reads=[oqb, cbuf], writes=[psb[3]], inc=(h == 3 and cc_ == 1))
                        rs2, rs2b, _ = rs2r.next()
                        kb.op("act", lambda e, o=rs2: e.activation(out=o, in_=ps[3][:, :], func=AF.Ln, scale=1.0 / 256.0, bias=epsc),
                              reads=[psb[3], cbuf], writes=[rs2b])
                        kb.op("act", lambda e, o=rs2: e.activation(out=o, in_=o, func=AF.Exp, scale=-0.5), reads=[rs2b], writes=[rs2b])
                        os4 = os_.rearrange("p (h c) t -> p h c t", h=4)
                        rsb4 = rs2.rearrange("p (h t) -> p h t", h=4).unsqueeze(2).broadcast_to([128, 4, 2, 128])
                        og, ogb, ogd = ogr.next()
                        kb.op("dve", lambda e, o=og.rearrange("p (h c) t -> p h c t", h=4), a=os4, r=rsb4: e.tensor_tensor(out=o, in0=a, in1=r, op=ALU.mult),
                              reads=[osb, rs2b], writes=[ogb])
                        kb.dma("sp", ogsp[c["b"]], og, ogd, reads=[ogb], writes=[ogsp_b])
                ar.release(mkM)
                if debug == "F":
                    raise _Stop()

                mk = ar.mark()
                mT = ar.alloc(4 * T * 2, BF16, "p (g t) -> p g t", g=4)
                mTb = ar.buf()
                wpf = ar.alloc(4 * 1024 * 2, BF16, "p (g c) -> p g c", g=4)
                wpg = ar.alloc(8 * 1024 * 2, BF16, "p (k c) -> p k c", k=8)
                wo = ar.alloc(8 * 1024 * 2, BF16, "p (k c) -> p k c", k=8)
                wpb = ar.buf()
                d_wp = kb.dsem("wp")
                for i in range(2):
                    kb.dma("pool", wpf[:, :, i * 512:(i + 1) * 512], w_pf[:, i * 512:(i + 1) * 512].rearrange("(g p) c -> p g c", p=128), d_wp, writes=[wpb])
                for i in range(4):
                    kb.dma("pool", wpg[:, :, i * 256:(i + 1) * 256], w_pg[:, i * 256:(i + 1) * 256].rearrange("(k p) c -> p k c", p=128), d_wp, writes=[wpb])
                for i in range(4):
                    kb.dma("pool", wo[:, :, i * 256:(i + 1) * 256], w_out[:, i * 256:(i + 1) * 256].rearrange("(k p) c -> p k c", p=128), d_wp, writes=[wpb])
                for c8 in range(8):
                    kb.op("act", lambda e, c8=c8: e.activation(out=wpg[:, c8, :], in_=wpg[:, c8, :], func=AF.Identity, scale=fv[:, FV_GN + c8:FV_GN + c8 + 1]),
                          reads=[wpb, cbuf], writes=[wpb])
                mkg = ar.mark()
                pcr = Ring(ar, kb, "pc", 8, 2048, BF16)
                tbr = Ring(ar, kb, "tb", 8, 2048, BF16, "p (a t) -> p a t", a=2)
                SC_S = 1.0 / math.sqrt(4096.0 * 128.0)
                SC_P = 1.0 / math.sqrt(256.0 * 128.0)
                for n in range(4):
                    for rc in range(32):
                        pc, pcb, pcd = pcr.next()
                        ti, rk, cc_ = rc // 8, (rc % 8) // 4, rc % 4
                        r0 = rk * 512 + cc_ * 128
                        kb.dma("sp", pc, px_out[ti][r0:r0 + 128, :], pcd, reads=[pxout_b[ti]], writes=[pcb])
                        tb, tbb, tbd = tbr.next()
                        kb.dma("sp", tb, tabs_d[rc][n], tbd, writes=[tbb])
                        for g in range(4):
                            kb.op("pe", lambda e, g=g, pc=pc, tb=tb, rc=rc: e.matmul(ps[g][:, :], pc[:, g * 256:g * 256 + 128], tb[:, 0, :], start=(rc == 0), stop=False),
                                  reads=[pcb, tbb], writes=[psb[g]], inc=False)
                            kb.op("pe", lambda e, g=g, pc=pc, tb=tb, rc=rc: e.matmul(ps[g][:, :], pc[:, g * 256 + 128:g * 256 + 256], tb[:, 1, :], start=False, stop=(rc == 31)),
                                  reads=[pcb, tbb], writes=[psb[g]], inc=(g == 3))
                    for g in range(4):
                        kb.op("act", lambda e, g=g, n=n: e.activation(out=mT[:, g, n * 512:(n + 1) * 512], in_=ps[g][:, :], func=AF.Identity, scale=SC_S),
                              reads=[psb[g]], writes=[mTb])
                for sq_ in range(2):
                    for rc in range(2):
                        pc, pcb, pcd = pcr.next()
                        r0 = sq_ * 256 + rc * 128
                        kb.dma("sp", pc, pp[r0:r0 + 128, :], pcd, reads=[pp_b], writes=[pcb])
                        tb, tbb, tbd = tbr.next()
                        kb.dma("sp", tb[:, :, 0:256], tabp_d[rc], tbd, writes=[tbb])
                        for g in range(4):
                            kb.op("pe", lambda e, g=g, pc=pc, tb=tb, rc=rc: e.matmul(ps[4 + g][:, 0:256], pc[:, g * 256:g * 256 + 128], tb[:, 0, 0:256], start=(rc == 0), stop=False),
                                  reads=[pcb, tbb], writes=[psb[4 + g]], inc=False)
                            kb.op("pe", lambda e, g=g, pc=pc, tb=tb, rc=rc: e.matmul(ps[4 + g][:, 0:256], pc[:, g * 256 + 128:g * 256 + 256], tb[:, 1, 0:256], start=False, stop=(rc == 1)),
                                  reads=[pcb, tbb], writes=[psb[4 + g]], inc=(g == 3))
                    for g in range(4):
                        t0 = TS + sq_ * 256
                        kb.op("act", lambda e, g=g, t0=t0: e.activation(out=mT[:, g, t0:t0 + 256], in_=ps[4 + g][:, 0:256], func=AF.Identity, scale=SC_P),
                              reads=[psb[4 + g]], writes=[mTb])
                ar.release(mkg)
                if debug == "G":
                    raise _Stop()

                srtr = Ring(ar, kb, "srt", 1, 8192, BF16, "p (c t) -> p c t", c=8)
                ogt = ar.alloc(8 * 512 * 2, BF16, "p (c b t) -> p c b t", c=8, b=4)
                ogtb = ar.buf()
                d_ogt = kb.dsem("ogt")
                ypre = ar.alloc(8 * 512 * 2, BF16, "p (m t) -> p m t", m=8)
                ypb = [ar.buf() for _ in range(8)]
                gtr = Ring(ar, kb, "gt", 4, 1024, BF16)
                t1r = Ring(ar, kb, "t1", 2, 2048, F32, dma=False)
                t2r = Ring(ar, kb, "t2", 2, 2048, F32, dma=False)
                hb_ = [0, 0, 0]
                for n in range(NT):
                    for bl in range(4):
                        kb.dma("sp", ogt[:, :, bl, :], ogsp[n * 4 + bl], d_ogt, reads=[ogsp_b], writes=[ogtb])
                    ogv = ogt.rearrange("p c b t -> p c (b t)")
                    srt, srtb, srtd = srtr.next()
                    kb.dma("sp", srt, gsp[0:8, :, n * 512:(n + 1) * 512].rearrange("c p t -> p c t"), srtd, reads=[gsp_b], writes=[srtb])
                    kb.op("dve", lambda e, o=ogv, g_=srt: e.tensor_tensor(out=o, in0=o, in1=g_, op=ALU.mult), reads=[ogtb, srtb], writes=[ogtb])
                    for m in range(8):
                        ba = hb_[0] % 2
                        hb_[0] += 1
                        bb = 2 + hb_[1] % 2
                        hb_[1] += 1
                        for g in range(4):
                            kb.op("pe", lambda e, ba=ba, g=g, m=m, n=n: e.matmul(ps[ba][:, :], wpf[:, g, m * 128:(m + 1) * 128], mT[:, g, n * 512:(n + 1) * 512], start=(g == 0), stop=(g == 3)),
                                  reads=[wpb, mTb], writes=[psb[ba]], inc=(g == 3))
                        for c8 in range(8):
                            kb.op("pe", lambda e, bb=bb, c8=c8, m=m: e.matmul(ps[bb][:, :], wpg[:, c8, m * 128:(m + 1) * 128], ogv[:, c8, :], start=(c8 == 0), stop=(c8 == 7)),
                                  reads=[wpb, ogtb], writes=[psb[bb]], inc=(c8 == 7))
                        ga, gab, gad = gtr.next()
                        kb.dma("sp", ga, gsp[8 + m, :, n * 512:(n + 1) * 512], gad, reads=[gsp_b], writes=[gab])
                        gb_, gbb, gbd = gtr.next()
                        kb.dma("sp", gb_, gsp[16 + m, :, n * 512:(n + 1) * 512], gbd, reads=[gsp_b], writes=[gbb])
                        t1, t1b, _ = t1r.next()
                        t2, t2b, _ = t2r.next()
                        kb.op("dve", lambda e, o=t1, ba=ba, ga=ga: e.tensor_tensor(out=o, in0=ps[ba][:, :], in1=ga, op=ALU.mult),
                              reads=[psb[ba], gab], writes=[t1b])
                        kb.op("dve", lambda e, o=t2, bb=bb, gb_=gb_: e.tensor_tensor(out=o, in0=ps[bb][:, :], in1=gb_, op=ALU.mult),
                              reads=[psb[bb], gbb], writes=[t2b])
                        kb.op("dve", lambda e, m=m, t1=t1, t2=t2: e.tensor_tensor(out=ypre[:, m, :], in0=t1, in1=t2, op=ALU.add),
                              reads=[t1b, t2b], writes=[ypb[m]])
                    for m2 in range(8):
                        bk = 4 + hb_[2] % 3
                        hb_[2] += 1
                        for m in range(8):
                            kb.op("pe", lambda e, bk=bk, m=m, m2=m2: e.matmul(ps[bk][:, :], wo[:, m, m2 * 128:(m2 + 1) * 128], ypre[:, m, :], start=(m == 0), stop=(m == 7)),
                                  reads=[wpb, ypb[m]], writes=[psb[bk]], inc=(m == 7))
                        xs = x[:, m2, n * 512:(n + 1) * 512]
                        kb.op("dve", lambda e, o=xs, bk=bk, s=scal(5, tsel(n), m2): e.scalar_tensor_tensor(out=o, in0=ps[bk][:, :], scalar=s, in1=o, op0=ALU.mult, op1=ALU.add),
                              reads=[psb[bk], scb], writes=xbufs(m2, n))
                ar.release(mk)
                dump_x("mix")

                if debug != "mix":
                    ffn(2, wg2, wu2, wd2)
                    dump_x("ffn2")

                    mk = ar.mark()
                    sqr = Ring(ar, kb, "sq", 2, 1024, BF16, dma=False)
                    rsr = Ring(ar, kb, "rs", 2, 2048, F32, dma=False)
                    xnr = Ring(ar, kb, "xn", 2, 8 * 2048, F32, "p (m t) -> p m t", dma=False, m=8)
                    osg = Ring(ar, kb, "osg", 3, 4096, F32)
                    yb_ = Buf()
                    tb_ = [0]
                    for n in range(NT):
                        rs, rsb = rms_stats(n, sqr, rsr, 7)
                        xn, xnb, _ = xnr.next()
                        for m in range(KC):
                            kb.op("dve", lambda e, o=xn[:, m, :], a=x[:, m, n * 512:(n + 1) * 512], s=fv[:, FV_NF + m:FV_NF + m + 1], r=rs:
                                  e.scalar_tensor_tensor(out=o, in0=a, scalar=s, in1=r, op0=ALU.mult, op1=ALU.mult),
                                  reads=xbufs(m, n) + [rsb, cbuf], writes=[xnb])
                        for bl in range(4):
                            og_, ogb_, ogd_ = osg.next()
                            for hh in range(2):
                                bank = tb_[0] % 4
                                tb_[0] += 1
                                for i in range(4):
                                    m = hh * 4 + i
                                    kb.op("pe", lambda e, bank=bank, i=i, m=m, xn=xn, bl=bl: e.transpose(ps[bank][:, i * 128:(i + 1) * 128], xn[:, m, bl * 128:(bl + 1) * 128], ident),
                                          reads=[xnb, cbuf], writes=[psb[bank]], inc=(i == 3))
                                if hh == 0:
                                    kb.op("act", lambda e, o=og_[:, 0:512], bank=bank: e.activation(out=o, in_=ps[bank][:, :], func=AF.Copy),
                                          reads=[psb[bank]], writes=[ogb_])
                                else:
                                    kb.op("dve", lambda e, o=og_[:, 512:1024], bank=bank: e.tensor_copy(out=o, in_=ps[bank][:, :]),
                                          reads=[psb[bank]], writes=[ogb_])
                            r0 = (n * 4 + bl) * 128
                            kb.dma("sp", yout[r0:r0 + 128, :], og_, ogd_, reads=[ogb_], writes=[yb_])
                    ar.release(mk)

        try:
            _mixer_and_rest()
        except _Stop:
            dump_x(debug)

        kb.wait_all("sp")
        kb.replay(block)
    return nc


def _bf16(a):
    return np.asarray(a, dtype=np.float32).astype(ml_dtypes.bfloat16)


def _grid_pos():
    rows = 4096 // 64
    row = np.repeat(np.arange(rows, dtype=np.float32), 64)
    col = np.tile(np.arange(64, dtype=np.float32), rows)
    n_freq = D // 4
    omega = (np.float32(10000.0) ** (-np.arange(n_freq, dtype=np.float32) / np.float32(n_freq))).astype(np.float32)
    ra = row[:, None] * omega
    ca = col[:, None] * omega
    return np.concatenate([np.sin(ra), np.cos(ra), np.sin(ca), np.cos(ca)], axis=-1).astype(np.float32)


def _consts():
    ident = np.eye(128, dtype=np.float32)
    i = np.arange(128)
    L1 = (i[:, None] <= i[None, :]).astype(np.float32)
    U1 = (i[:, None] > i[None, :]).astype(np.float32)
    L2 = (i[:, None] >= i[None, :]).astype(np.float32)
    U2 = (i[:, None] < i[None, :]).astype(np.float32)
    tri = np.stack([L1, U1, L2, U2], 1) * np.float32(-1.0 / 16.0)
    mask = np.stack([np.tile(L1, (1, 4)), np.tile(L2, (1, 4))], 1)
    c = np.arange(128)
    ang = 2 * np.pi * ((c[:, None] * c[None, :]) % 128) / 128.0
    cs128 = np.concatenate([np.cos(ang), -np.sin(ang)], 1)
    return ident, tri.astype(np.float32), _bf16(mask), _bf16(cs128)


def _tables(flip):
    rc = np.arange(32)[:, None]
    p = np.arange(128)[None, :]
    local = (rc // 8) * 512 + (rc % 4) * 128 + p
    rpos = np.where(((rc % 8) // 4) == 0, local, 4095 - local).reshape(4096)
    j = np.arange(2048)
    cpos = (4095 - j) if flip else j
    ang = 2 * np.pi * ((rpos[:, None].astype(np.int64) * cpos[None, :]) % 4096) / 4096.0
    tabs = np.stack([np.cos(ang), np.sin(ang)], 1)
    tabs = np.ascontiguousarray(tabs.reshape(32, 128, 2, 4, 512).transpose(0, 3, 1, 2, 4))
    rp = np.arange(256)
    ppos = (255 - rp) if flip else rp
    angp = 2 * np.pi * ((ppos[:, None] * ppos[None, :]) % 256) / 256.0
    tabp = np.stack([np.cos(angp), np.sin(angp)], 1).reshape(2, 128, 2, 256)
    return _bf16(tabs), _bf16(tabp)


def _fm(vec):
    return np.ascontiguousarray(np.asarray(vec, np.float32).reshape(-1, 128).T)


def make_in_maps(inp):
    f32 = lambda a: np.ascontiguousarray(np.asarray(a, dtype=np.float32))
    pos = _grid_pos()
    ident, tri, mask, cs128 = _consts()
    tabs = [_tables(False), _tables(True)]
    shared = {
        "w_ada": f32(inp["w_ada"][0]),
        "w_ffn1_gate": f32(inp["w_ffn1_gate"][0]), "w_ffn1_up": f32(inp["w_ffn1_up"][0]), "w_ffn1_down": f32(inp["w_ffn1_down"][0]),
        "w_ffn2_gate": f32(inp["w_ffn2_gate"][0]), "w_ffn2_up": f32(inp["w_ffn2_up"][0]), "w_ffn2_down": f32(inp["w_ffn2_down"][0]),
        "w_in": f32(inp["w_in"][0]),
        "w_proj_fourier": f32(inp["w_proj_fourier"][0]), "w_proj_gla": f32(inp["w_proj_gla"][0]), "w_out": f32(inp["w_out"][0]),
        "ident": ident, "tri": tri, "maskc": mask, "cs128": cs128,
    }
    w_in = shared["w_in"]
    alr_f, alr_b = w_in[:, 3584:3600], w_in[:, 3600:3616]
    wa_f = np.concatenate([f32(inp["w_alpha_fwd"][0]), f32(inp["b_alpha_fwd"][0])[None]], 0)
    wa_b = np.concatenate([f32(inp["w_alpha_bwd"][0]), f32(inp["b_alpha_bwd"][0])[None]], 0)
    b_ada = _fm(inp["b_ada"][0])
    maps = []
    for c in range(8):
        b, half = c // 2, c % 2
        flip = half == 1
        sl = slice(half * TS, (half + 1) * TS)
        xs = f32(inp["x_sample"][b, sl])
        ps_ = pos[sl]
        xp = [f32(inp["x_prompt"][2 * c]), f32(inp["x_prompt"][2 * c + 1])]
        if flip:
            xs, ps_ = xs[::-1], ps_[::-1]
            xp = [a[::-1] for a in xp]
        xin = np.ascontiguousarray(np.concatenate([xs] + xp, 0))
        posT = np.ascontiguousarray(ps_.reshape(16, 128, KC, 128).transpose(0, 3, 2, 1))
        st = inp["state_gla_bwd"] if flip else inp["state_gla_fwd"]
        sinit = np.ascontiguousarray(f32(st[b, 0]).transpose(1, 0, 2))
        cc = np.stack([f32(inp["c"][b]), f32(inp["c_ctx"])], 0)
        cT = np.ascontiguousarray(cc.reshape(2, KC, 128).transpose(2, 1, 0))
        fvec = np.zeros((128, FV_N), np.float32)
        fvec[:, FV_BADA:FV_BADA + 144] = np.repeat(b_ada, 2, axis=1)
        fvec[:, FV_N1:FV_N1 + 8] = _fm(inp["norm_ffn1"][0])
        fvec[:, FV_N2:FV_N2 + 8] = _fm(inp["norm_mix"][0])
        fvec[:, FV_N3:FV_N3 + 8] = _fm(inp["norm_ffn2"][0])
        fvec[:, FV_NF:FV_NF + 8] = _fm(inp["final_norm"])
        fvec[:, FV_GN:FV_GN + 8] = _fm(inp["gla_norm"][0])
        fvec[:, FV_SEL:FV_SEL + 2] = np.array([1.0, 0.0] if flip else [0.0, 1.0], np.float32)
        m = dict(shared)
        m.update({
            "xin": xin, "posT": posT, "sinit": sinit, "cT": cT, "fvec": fvec,
            "w_alr": np.ascontiguousarray(np.concatenate([alr_b, alr_f] if flip else [alr_f, alr_b], 1)),
            "walpha": np.ascontiguousarray(np.stack([wa_b, wa_f] if flip else [wa_f, wa_b], 0)),
            "tabs": tabs[half][0], "tabp": tabs[half][1],
        })
        maps.append(m)
    return maps


def assemble(results):
    y_prompt = np.zeros((16, 256, D), np.float32)
    y_sample = np.zeros((4, 4096, D), np.float32)
    nsf = np.zeros((16, 1, 4, 128, 256), np.float32)
    nsb = np.zeros((16, 1, 4, 128, 256), np.float32)
    for c in range(8):
        r = results[c]
        b, half = c // 2, c % 2
        flip = half == 1
        y = r["yout"]
        ys, yp = y[:TS], [y[TS:TS + 256], y[TS + 256:]]
        if flip:
            ys = ys[::-1]
            yp = [a[::-1] for a in yp]
        y_sample[b, half * TS:(half + 1) * TS] = ys
        for s in range(2):
            y_prompt[2 * c + s] = yp[s]
            a1 = r["st1"][s].transpose(1, 0, 2)
            a2 = r["st2"][s].transpose(1, 0, 2)
            if flip:
                a1, a2 = a2, a1
            nsf[2 * c + s, 0] = a1
            nsb[2 * c + s, 0] = a2
    return y_prompt, y_sample, nsf, nsb


def kernel(**inputs):
    nc = build_nc()
    in_maps = make_in_maps(inputs)
    res = run_bass_kernel_spmd(nc, in_maps, core_ids=list(range(8)))
    return assemble(res.results)
```

```python
import math
from contextlib import ExitStack

import ml_dtypes
import numpy as np

import concourse.bass as bass
import concourse.mybir as mybir
from concourse.bass_utils import run_bass_kernel_spmd

F32 = mybir.dt.float32
BF16 = mybir.dt.bfloat16
AF = mybir.ActivationFunctionType
ALU = mybir.AluOpType

D = 1024
KC = 8
DFF = 2816
NFF = 22
T = 2560
NT = 5
NB = 20
TS = 2048
NCOLS_IN = 5664
EPS = 1e-6
NMODV = 72

FV_BADA = 0
FV_N1 = 144
FV_N2 = 152
FV_N3 = 160
FV_NF = 168
FV_GN = 176
FV_SEL = 184
FV_N = 186


class _Stop(Exception):
    pass


class Buf:
    __slots__ = ("w", "r")

    def __init__(self, seed=None):
        self.w = {}
        self.r = dict(seed) if seed else {}


class DSem:
    def __init__(self, h):
        self.h = h
        self.n = 0


class KB:
    ENG = ("pe", "act", "dve", "pool", "sp")

    def __init__(self, nc, es):
        self.nc = nc
        self.es = es
        self.q = {e: [] for e in self.ENG}
        self.sem = {e: es.enter_context(nc.semaphore("s_" + e)) for e in self.ENG}
        self.cnt = {e: 0 for e in self.ENG}
        self.waited = {e: {} for e in self.ENG}
        self.pend_r = {e: [] for e in self.ENG}
        self.pend_w = {e: [] for e in self.ENG}
        self.dsems = []
        self.semname = {}
        for e in self.ENG:
            self.semname[id(self.sem[e])] = e

    def dsem(self, name):
        self.nds = getattr(self, "nds", 0) + 1
        d = DSem(self.es.enter_context(self.nc.semaphore(f"d{self.nds}_{name}")))
        self.dsems.append(d)
        return d

    def snapshot(self):
        s = {}
        for e in self.ENG:
            if self.cnt[e]:
                s[id(self.sem[e])] = (self.sem[e], self.cnt[e])
        for d in self.dsems:
            if d.n:
                s[id(d.h)] = (d.h, d.n)
        return s

    def _need(self, eng, waits, ev):
        sem, val = ev
        k = id(sem)
        if eng == "pe" and sem is self.sem["pe"]:
            return
        if self.waited[eng].get(k, 0) >= val:
            return
        if k in waits and waits[k][1] >= val:
            return
        waits[k] = (sem, val)

    def _deps(self, eng, reads, writes):
        waits = {}
        for b in reads:
            for ev in b.w.values():
                self._need(eng, waits, ev)
        for b in writes:
            for ev in b.w.values():
                self._need(eng, waits, ev)
            for ev in b.r.values():
                self._need(eng, waits, ev)
        for k, (sem, val) in waits.items():
            self.waited[eng][k] = val
        return list(waits.values())

    def _commit(self, ev, reads, writes):
        k = id(ev[0])
        for b in reads:
            b.r[k] = ev
        for b in writes:
            b.w[k] = ev

    def op(self, eng, fn, reads=(), writes=(), inc=True):
        reads = list(reads)
        writes = list(writes)
        waits = self._deps(eng, reads, writes)
        if inc:
            self.cnt[eng] += 1
            ev = (self.sem[eng], self.cnt[eng])
            self._commit(ev, reads + self.pend_r[eng], writes + self.pend_w[eng])
            self.pend_r[eng] = []
            self.pend_w[eng] = []
            self.q[eng].append((waits, fn, (self.sem[eng], 1)))
        else:
            self.pend_r[eng] += reads
            self.pend_w[eng] += writes
            self.q[eng].append((waits, fn, None))

    def dma(self, queue, out, in_, dsem, reads=(), writes=(), **kw):
        reads = list(reads)
        writes = list(writes)
        waits = self._deps(queue, reads, writes)
        dsem.n += 16
        ev = (dsem.h, dsem.n)
        self._commit(ev, reads, writes)
        self.q[queue].append((waits, lambda e: e.dma_start(out=out, in_=in_, **kw), (dsem.h, 16)))

    def raw(self, queue, fn, dsem_inc, reads=(), writes=()):
        reads = list(reads)
        writes = list(writes)
        waits = self._deps(queue, reads, writes)
        d, n = dsem_inc
        d.n += n
        ev = (d.h, d.n)
        self._commit(ev, reads, writes)
        self.q[queue].append((waits, fn, (d.h, n)))

    def wait_all(self, eng):
        waits = []
        for k, (sem, val) in self.snapshot().items():
            if sem is self.sem[eng]:
                continue
            if self.waited[eng].get(k, 0) >= val:
                continue
            self.waited[eng][k] = val
            waits.append((sem, val))
        self.q[eng].append((waits, None, None))

    def replay(self, block):
        def run(eng):
            def f(e):
                for waits, fn, inc in self.q[eng]:
                    for sem, val in waits:
                        e.wait_ge(sem, val)
                    if fn is None:
                        continue
                    ins = fn(e)
                    if inc is not None:
                        ins.then_inc(inc[0], inc[1])
            return f

        block.tensor(run("pe"))
        block.scalar(run("act"))
        block.vector(run("dve"))
        block.gpsimd(run("pool"))
        block.sync(run("sp"))


class Arena:
    def __init__(self, kb, tens, nbytes):
        self.kb = kb
        self.t = tens
        self.top = 0
        self.cap = nbytes
        self.seed = None

    def mark(self):
        return self.top

    def release(self, mark):
        self.top = mark
        self.seed = self.kb.snapshot()

    def alloc(self, nbytes, dtype, pat=None, parts=128, **kw):
        assert nbytes % 4 == 0
        off = self.top
        self.top += (nbytes + 31) // 32 * 32
        assert self.top <= self.cap, f"SBUF arena overflow {self.top} > {self.cap}"
        ap = self.t[0:parts, off // 4:(off + nbytes) // 4]
        if dtype != F32:
            ap = ap.bitcast(dtype)
        if pat:
            ap = ap.rearrange(pat, **kw)
        return ap

    def buf(self):
        return Buf(self.seed)


class Ring:
    def __init__(self, ar, kb, name, n, nbytes, dtype, pat=None, parts=128, dma=True, **kw):
        self.aps = [ar.alloc(nbytes, dtype, pat, parts, **kw) for _ in range(n)]
        self.bufs = [ar.buf() for _ in range(n)]
        self.ds = [kb.dsem(f"{name}{i}") for i in range(n)] if dma else [None] * n
        self.i = -1
        self.n = n

    def next(self):
        self.i = (self.i + 1) % self.n
        return self.aps[self.i], self.bufs[self.i], self.ds[self.i]


def build_nc(debug=None):
    nc = bass.Bass("TRN2", target_bir_lowering=False)

    def din(name, shape, dt=F32):
        return nc.dram_tensor(name, list(shape), dt, kind="ExternalInput").ap()

    def dout(name, shape, dt=F32):
        return nc.dram_tensor(name, list(shape), dt, kind="ExternalOutput").ap()

    xin = din("xin", [T, D])
    posT = din("posT", [16, 128, KC, 128])
    sinit = din("sinit", [128, 4, 256])
    cT = din("cT", [128, KC, 2])
    fvec = din("fvec", [128, FV_N])
    w_ada = din("w_ada", [D, 9216])
    wg1 = din("w_ffn1_gate", [D, DFF]); wu1 = din("w_ffn1_up", [D, DFF]); wd1 = din("w_ffn1_down", [DFF, D])
    wg2 = din("w_ffn2_gate", [D, DFF]); wu2 = din("w_ffn2_up", [D, DFF]); wd2 = din("w_ffn2_down", [DFF, D])
    w_in = din("w_in", [D, NCOLS_IN])
    w_alr = din("w_alr", [D, 32])
    walpha = din("walpha", [2, 17, 512])
    w_pf = din("w_proj_fourier", [512, D])
    w_pg = din("w_proj_gla", [D, D])
    w_out = din("w_out", [D, D])
    ident_d = din("ident", [128, 128])
    tri_d = din("tri", [128, 4, 128])
    mask_d = din("maskc", [128, 2, 512], BF16)
    cs128_d = din("cs128", [128, 256], BF16)
    tabs_d = din("tabs", [32, 4, 128, 2, 512], BF16)
    tabp_d = din("tabp", [2, 128, 2, 256], BF16)

    yout = dout("yout", [T, D])
    st1 = dout("st1", [2, 128, 4, 256])
    st2 = dout("st2", [2, 128, 4, 256])
    dbg = dout("dbg", [128, 8 * T]) if debug else None
    dbgh = dout("dbgh", [128, 8 * T], BF16) if debug == "h" else None

    px_in = [nc.dram_tensor(f"px_in{i}", [512, 1024], BF16) for i in range(4)]
    px_out = [nc.dram_tensor(f"px_out{i}", [1024, 1024], BF16) for i in range(4)]
    pp = nc.dram_tensor("pp", [512, 1024], BF16)
    sx_in = nc.dram_tensor("sx_in", [128, 1024], F32)
    sx_out = nc.dram_tensor("sx_out", [256, 1024], F32)
    qkT = nc.dram_tensor("qkT", [NB, 128, 8, 128], BF16)
    kvt = nc.dram_tensor("kvt", [NB, 128, 1536], BF16)
    gsp = nc.dram_tensor("gsp", [24, 128, T], BF16)
    o1sp = nc.dram_tensor("o1sp", [NB, 128, 8, 128], F32)
    ogsp = nc.dram_tensor("ogsp", [NB, 128, 8, 128], BF16)

    es = ExitStack()
    with es:
        ARENA_BYTES = 212000
        arena_t = es.enter_context(nc.sbuf_tensor("arena", [128, ARENA_BYTES // 4], F32))
        ps = [es.enter_context(nc.psum_tensor(f"ps{i}", [128, 512], F32)) for i in range(8)]
        kb = KB(nc, es)
        ar = Arena(kb, arena_t, ARENA_BYTES)
        psb = [Buf() for _ in range(8)]
        block = es.enter_context(nc.Block())

        x = ar.alloc(KC * T * 4, F32, "p (m t) -> p m t", m=KC)
        xb = [[Buf() for _ in range(NB)] for _ in range(KC)]
        ident = ar.alloc(512, F32)
        ones = ar.alloc(256, BF16)
        tri = ar.alloc(2048, F32, "p (a b) -> p a b", a=4)
        trib = ar.alloc(1024, BF16, "p (a b) -> p a b", a=4)
        maskc = ar.alloc(2048, BF16, "p (a b) -> p a b", a=2)
        cs128 = ar.alloc(512, BF16)
        epsc = ar.alloc(32, F32)[:, 0:1]
        fv = ar.alloc(FV_N * 4, F32)
        modfm = ar.alloc(NMODV * 2 * 4, F32, "p (c v) -> p c v", v=2)
        sc = ar.alloc(9 * 16 * 4, F32)

        def scal(k, v, m):
            c = (k * 2 + v) * 8 + m
            return sc[:, c:c + 1]
        cbuf = Buf()
        d_const = kb.dsem("const")
        wslab = Ring(ar, kb, "ws", 4, 4096, BF16)

        def xbufs(m, n):
            return [xb[m][4 * n + i] for i in range(4)]

        def tsel(n):
            return 0 if n < 4 else 1

        for dst, src in ((ident, ident_d), (tri, tri_d), (maskc, mask_d), (cs128, cs128_d), (fv, fvec)):
            kb.dma("sp", dst, src, d_const, writes=[cbuf])
        kb.op("dve", lambda e: e.memset(ones, 1.0), writes=[cbuf])
        kb.op("dve", lambda e: e.memset(epsc, EPS), writes=[cbuf])
        kb.op("dve", lambda e: e.tensor_copy(out=trib, in_=tri), reads=[cbuf], writes=[cbuf])

        mkA = ar.mark()
        ctf = ar.alloc(KC * 2 * 4, F32, "p (k v) -> p k v", v=2)
        ctb = ar.alloc(KC * 2 * 2, BF16, "p (k v) -> p k v", v=2)
        ctbuf = ar.buf()
        d_ct = kb.dsem("ct")
        kb.dma("sp", ctf, cT, d_ct, writes=[ctbuf])
        kb.op("act", lambda e: e.activation(out=ctb, in_=ctf, func=AF.Silu), reads=[ctbuf], writes=[ctbuf])
        ADABANK = 7
        adar = Ring(ar, kb, "ada", 3, 4096, BF16)
        mstr = Ring(ar, kb, "mst", 2, 1024, F32, parts=2, dma=False)
        scb = Buf()
        ada_pending = []

        def ada_load(cb):
            slab, slb, sld = adar.next()
            sv = slab.rearrange("p (k c) -> p k c", k=KC)
            kb.dma("pool", sv, w_ada[:, cb * 256:(cb + 1) * 256].rearrange("(k p) c -> p k c", p=128), sld, writes=[slb])
            ada_pending.append((cb, sv, slb))

        def ada_compute():
            cb, sv, slb = ada_pending.pop(0)
            for kc in range(KC):
                kb.op("pe", lambda e, a=ctb[:, kc, :], r=sv[:, kc, :], kc=kc:
                      e.matmul(ps[ADABANK][0:2, 0:256], a, r, start=(kc == 0), stop=(kc == KC - 1)),
                      reads=[ctbuf, slb], writes=[psb[ADABANK]], inc=(kc == KC - 1))
            ms, msb, _ = mstr.next()
            kb.op("act", lambda e, o=ms: e.activation(out=o, in_=ps[ADABANK][0:2, 0:256], func=AF.Copy), reads=[psb[ADABANK]], writes=[msb])
            for j in range(2):
                kb.op("pe", lambda e, j=j, ms=ms: e.matmul(ps[ADABANK][:, 256 + 2 * j:258 + 2 * j], ms[:, j * 128:(j + 1) * 128], ident[0:2, 0:2], start=True, stop=True),
                      reads=[msb, cbuf], writes=[psb[ADABANK]], inc=(j == 1))
            kb.op("dve", lambda e, cb=cb: e.tensor_tensor(out=modfm[:, 2 * cb:2 * cb + 2, :], in0=ps[ADABANK][:, 256:260].rearrange("p (c v) -> p c v", v=2),
                                                         in1=fv[:, FV_BADA + 4 * cb:FV_BADA + 4 * cb + 4].rearrange("p (c v) -> p c v", v=2), op=ALU.add),
                  reads=[psb[ADABANK], cbuf], writes=[scb])

        def derive_scalars(li, norm_part=True, gate_part=True):
            fvn = (FV_N1, FV_N2, FV_N3)[li]
            base = li * 24
            for v in range(2):
                c_a = ((3 * li) * 2 + v) * 8
                c_s = ((3 * li + 1) * 2 + v) * 8
                c_g = ((3 * li + 2) * 2 + v) * 8
                if norm_part:
                    kb.op("dve", lambda e, o=sc[:, c_a:c_a + 8], a=modfm[:, base + 8:base + 16, v], g=fv[:, fvn:fvn + 8]:
                          e.scalar_tensor_tensor(out=o, in0=a, scalar=1.0, in1=g, op0=ALU.add, op1=ALU.mult),
                          reads=[scb, cbuf], writes=[scb])
                    kb.op("dve", lambda e, o=sc[:, c_s:c_s + 8], a=modfm[:, base:base + 8, v]:
                          e.tensor_copy(out=o, in_=a), reads=[scb], writes=[scb])
                if gate_part:
                    gsc = 1.0 if li == 1 else 0.5
                    kb.op("dve", lambda e, o=sc[:, c_g:c_g + 8], a=modfm[:, base + 16:base + 24, v], gsc=gsc:
                          e.tensor_scalar(out=o, in0=a, scalar1=gsc, scalar2=None, op0=ALU.mult),
                          reads=[scb], writes=[scb])

        NPRE = 8
        for cb in range(3):
            ada_load(cb)
        ada_ld = [3]

        mk = ar.mark()
        tokr = Ring(ar, kb, "tok", 3, 4096, F32)
        posr = Ring(ar, kb, "pos", 2, 4096, F32, "p (m t) -> p m t", m=KC)
        for b in range(NB):
            tok, tokb, tokd = tokr.next()
            kb.dma("sp", tok, xin[b * 128:(b + 1) * 128, :], tokd, writes=[tokb])
            if b < 16:
                pos, posb, posd = posr.next()
                kb.dma("sp", pos, posT[b], posd, writes=[posb])
            for hh in range(2):
                bank = hh
                for i in range(4):
                    m = hh * 4 + i
                    kb.op("pe", lambda e, o=ps[bank][:, i * 128:(i + 1) * 128], a=tok[:, m * 128:(m + 1) * 128]:
                          e.transpose(o, a, ident),
                          reads=[tokb, cbuf], writes=[psb[bank]], inc=(i == 3))
                pv = ps[bank][:, :].rearrange("p (a b) -> p a b", a=4)
                xo = x[:, hh * 4:hh * 4 + 4, b * 128:(b + 1) * 128]
                wr = [xb[hh * 4 + i][b] for i in range(4)]
                if b < 16:
                    kb.op("dve", lambda e, o=xo, a=pv, c=pos[:, hh * 4:hh * 4 + 4, :]:
                          e.tensor_tensor(out=o, in0=a, in1=c, op=ALU.add),
                          reads=[psb[bank], posb], writes=wr)
                else:
                    kb.op("act", lambda e, o=xo, a=pv: e.activation(out=o, in_=a, func=AF.Copy),
                          reads=[psb[bank]], writes=wr)
            if b < NPRE:
                ada_compute()
                if ada_ld[0] < NPRE + 3:
                    ada_load(ada_ld[0])
                    ada_ld[0] += 1
        ar.release(mk)

        derive_scalars(0, gate_part=False)
        ada_next = [NPRE + 3]
        gate1_done = [False]

        def ada_hook(k=2):
            if not gate1_done[0]:
                while ada_pending:
                    ada_compute()
                ada_load(ada_next[0])
                ada_next[0] += 1
                ada_compute()
                derive_scalars(0, norm_part=False)
                gate1_done[0] = True
            for _ in range(3):
                if ada_pending:
                    ada_compute()
            for _ in range(k):
                if ada_next[0] < 36:
                    ada_load(ada_next[0])
                    ada_next[0] += 1

        def rms_stats(n, sqr, rsr, ssbank):
            for m in range(KC):
                sq, sqb, _ = sqr.next()
                kb.op("act", lambda e, o=sq, a=x[:, m, n * 512:(n + 1) * 512]: e.activation(out=o, in_=a, func=AF.Square),
                      reads=xbufs(m, n), writes=[sqb])
                kb.op("pe", lambda e, a=sq, m=m: e.matmul(ps[ssbank][:, :], ones, a,
                                                          start=(m == 0), stop=(m == KC - 1)),
                      reads=[sqb, cbuf], writes=[psb[ssbank]], inc=True)
            rs, rsb, _ = rsr.next()
            kb.op("act", lambda e, o=rs: e.activation(out=o, in_=ps[ssbank][:, :], func=AF.Ln, scale=1.0 / D, bias=epsc),
                  reads=[psb[ssbank], cbuf], writes=[rsb])
            kb.op("act", lambda e, o=rs: e.activation(out=o, in_=o, func=AF.Exp, scale=-0.5), reads=[rsb], writes=[rsb])
            return rs, rsb

        def norm_mod(li, h, hb, sqr, rsr, tmr, ssbank, tiles=None):
            for n in (range(NT) if tiles is None else tiles):
                v = tsel(n)
                rs, rsb = rms_stats(n, sqr, rsr, ssbank)
                for m in range(KC):
                    tm, tmb, _ = tmr.next()
                    kb.op("dve", lambda e, o=tm, a=x[:, m, n * 512:(n + 1) * 512], s=scal(3 * li, v, m), r=rs:
                          e.scalar_tensor_tensor(out=o, in0=a, scalar=s, in1=r, op0=ALU.mult, op1=ALU.mult),
                          reads=xbufs(m, n) + [rsb, scb], writes=[tmb])
                    if m % 2 == 0:
                        kb.op("act", lambda e, o=h[:, m, n * 512:(n + 1) * 512], a=tm, s=scal(3 * li + 1, v, m):
                              e.activation(out=o, in_=a, func=AF.Identity, bias=s, scale=1.0),
                              reads=[tmb, scb], writes=[hb[m][n]])
                    else:
                        kb.op("dve", lambda e, o=h[:, m, n * 512:(n + 1) * 512], a=tm, s=scal(3 * li + 1, v, m):
                              e.tensor_scalar(out=o, in0=a, scalar1=s, scalar2=None, op0=ALU.add),
                              reads=[tmb, scb], writes=[hb[m][n]])

        def ffn(li, wg, wu, wd):
            mk = ar.mark()
            h = ar.alloc(KC * T * 2, BF16, "p (m t) -> p m t", m=KC)
            hb = [[ar.buf() for _ in range(NT)] for _ in range(KC)]
            sqr = Ring(ar, kb, "sq", 2, 1024, BF16, dma=False)
            rsr = Ring(ar, kb, "rs", 2, 2048, F32, dma=False)
            tmr = Ring(ar, kb, "tm", 3, 2048, F32, dma=False)
            Ar = Ring(ar, kb, "A", 2, 2 * T * 2, BF16, "p (j t) -> p j t", j=2, dma=False)
            if debug == "h" and li == 0:
                norm_mod(li, h, hb, sqr, rsr, tmr, 7)
            if debug == "h" and li == 0:
                d_dh = kb.dsem("dbgh")
                kb.dma("sp", dbgh, h.rearrange("p m t -> p (m t)"), d_dh, reads=[hb[m][n] for m in range(KC) for n in range(NT)])
                raise _Stop()
            NG = NFF // 2
            hall = [hb[m][n] for m in range(KC) for n in range(NT)]
            gk = 3 * li + 2

            def load_gu(g):
                sg, sgb, sgd = wslab.next()
                su, sub, sud = wslab.next()
                sgv = sg.rearrange("p (k c) -> p k c", k=KC)
                suv = su.rearrange("p (k c) -> p k c", k=KC)
                kb.dma("pool", sgv, wg[:, g * 256:(g + 1) * 256].rearrange("(k p) c -> p k c", p=128), sgd, writes=[sgb])
                kb.dma("pool", suv, wu[:, g * 256:(g + 1) * 256].rearrange("(k p) c -> p k c", p=128), sud, writes=[sub])
                return sgv, sgb, suv, sub

            def load_d(g):
                sd, sdb, sdd = wdr.next()
                sdv = sd.rearrange("p (j c) -> p j c", j=2)
                kb.dma("pool", sdv, wd[g * 256:(g + 1) * 256, :].rearrange("(j p) c -> p j c", p=128), sdd, writes=[sdb])
                return sdv, sdb

            wdr = Ring(ar, kb, "wd", 2, 4096, BF16)
            gu_next = load_gu(0)
            prev = None
            pbank = 0
            ybank = [0]

            def y_group(prev, m, n):
                pA, pAb, (sdv, sdb) = prev
                bk = 4 + ybank[0]
                ybank[0] = (ybank[0] + 1) % (3 if li == 0 else 4)
                for j in range(2):
                    kb.op("pe", lambda e, o=ps[bk][:, :], a=sdv[:, j, m * 128:(m + 1) * 128], r=pA[:, j, n * 512:(n + 1) * 512], j=j:
                          e.matmul(o, a, r, start=(j == 0), stop=(j == 1)),
                          reads=[sdb, pAb], writes=[psb[bk]], inc=(j == 1))
                xs = x[:, m, n * 512:(n + 1) * 512]
                kb.op("dve", lambda e, o=xs, a=ps[bk][:, :], s=scal(gk, tsel(n), m):
                      e.scalar_tensor_tensor(out=o, in0=a, scalar=s, in1=o, op0=ALU.mult, op1=ALU.add),
                      reads=[psb[bk], scb], writes=xbufs(m, n))

            for g in range(NG + 1):
                ylist = [(m, n) for m in range(KC) for n in range(NT)] if prev is not None else []
                if g < NG:
                    sgv, sgb, suv, sub = gu_next
                    dcur = load_d(g)
                    Aap, Ab, _ = Ar.next()
                    for j in range(2):
                        for n in range(NT):
                            if g == 0 and j == 0 and not (debug == "h" and li == 0):
                                if n == 0:
                                    norm_mod(li, h, hb, sqr, rsr, tmr, 7, tiles=[0])
                                if n + 1 < NT:
                                    norm_mod(li, h, hb, sqr, rsr, tmr, 7, tiles=[n + 1])
                            bg, bu = 2 * pbank, 2 * pbank + 1
                            pbank ^= 1
                            for (bk, sv, sb_) in ((bg, sgv, sgb), (bu, suv, sub)):
                                for kc in range(KC):
                                    kb.op("pe", lambda e, o=ps[bk][:, :], a=sv[:, kc, j * 128:(j + 1) * 128], r=h[:, kc, n * 512:(n + 1) * 512], kc=kc:
                                          e.matmul(o, a, r, start=(kc == 0), stop=(kc == KC - 1)),
                                          reads=[sb_] + [hb[kc][n]], writes=[psb[bk]], inc=(kc == KC - 1))
                                if bk == bg:
                                    for _ in range(2):
                                        if ylist:
                                            y_group(prev, *ylist.pop(0))
                            tm, tmb, _ = tmr.next()
                            kb.op("act", lambda e, o=tm, a=ps[bg][:, :]: e.activation(out=o, in_=a, func=AF.Silu),
                                  reads=[psb[bg]], writes=[tmb])
                            kb.op("dve", lambda e, o=Aap[:, j, n * 512:(n + 1) * 512], a=tm, b=ps[bu][:, :]:
                                  e.tensor_tensor(out=o, in0=a, in1=b, op=ALU.mult),
                                  reads=[tmb, psb[bu]], writes=[Ab])
                            for _ in range(2):
                                if ylist:
                                    y_group(prev, *ylist.pop(0))
                    if g + 1 < NG:
                        gu_next = load_gu(g + 1)
                    if li == 0:
                        ada_hook()
                    cur = (Aap, Ab, dcur)
                else:
                    cur = None
                while ylist:
                    y_group(prev, *ylist.pop(0))
                prev = cur
            ar.release(mk)

        def dump_x(tag):
            if debug == tag:
                d_dbg = kb.dsem("dbg")
                kb.dma("sp", dbg, x.rearrange("p m t -> p (m t)"), d_dbg,
                       reads=[xb[m][b] for m in range(KC) for b in range(NB)])

        try:
            ffn(0, wg1, wu1, wd1)
        except _Stop:
            pass
        while ada_pending or ada_next[0] < 36:
            ada_hook(3)
        derive_scalars(1)
        derive_scalars(2)
        ar.release(mkA)
        dump_x("ffn1")

        def _mixer_and_rest():
            mkM = ar.mark()
            alr = [ar.alloc(T * 2, BF16, parts=17) for _ in range(2)]
            alrb = [ar.buf() for _ in range(2)]
            SQ = 1.0 / math.sqrt(128.0)

            if debug != "ffn1":
                mk = ar.mark()
                h2 = ar.alloc(KC * T * 2, BF16, "p (m t) -> p m t", m=KC)
                h2b = [[ar.buf() for _ in range(NT)] for _ in range(KC)]
                stg = Ring(ar, kb, "stg", 3, 1024, BF16)
                pstg = Ring(ar, kb, "pstg", 3, 2048, BF16, "p (b c) -> p b c", b=4)
                kvst = Ring(ar, kb, "kvst", 2, 3072, BF16)
                pre_slabs = []

                def slab_prefetch(wsrc, ncol):
                    slab, slb, sld = wslab.next()
                    sv = slab.rearrange("p (k c) -> p k c", k=KC)[:, :, 0:ncol]
                    kb.dma("pool", sv, wsrc.rearrange("(k p) c -> p k c", p=128), sld, writes=[slb])
                    return sv, slb

                pre_slabs.append(slab_prefetch(w_in[:, 0:256], 256))
                pre_slabs.append(slab_prefetch(w_in[:, 256:512], 256))
                mk3 = ar.mark()
                wkv = ar.alloc(KC * 1536 * 2, BF16, "p (k c) -> p k c", k=KC)
                wkvb = ar.buf()
                d_wkv = kb.dsem("wkv")
                for i in range(6):
                    c0 = 1024 + i * 256
                    kb.dma("pool", wkv[:, :, i * 256:(i + 1) * 256], w_in[:, c0:c0 + 256].rearrange("(k p) c -> p k c", p=128),
                           d_wkv, writes=[wkvb])
                mk2 = ar.mark()
                sqr = Ring(ar, kb, "sq", 2, 1024, BF16, dma=False)
                rsr = Ring(ar, kb, "rs", 2, 2048, F32, dma=False)
                tmr = Ring(ar, kb, "tm", 3, 2048, F32, dma=False)

                for d_ in range(2):
                    kb.op("dve", lambda e, o=alr[d_]: e.memset(o, 1.0), writes=[alrb[d_]])

                pp_b, qkT_b, kvt_b, gsp_b = Buf(), Buf(), Buf(), Buf()
                pxin_b = [Buf() for _ in range(4)]
                swb = [0]

                def fm_sweep(wsrc, ncol, epi, pre=None, pre_tile=None):
                    sv, slb = pre if pre is not None else slab_prefetch(wsrc, ncol)
                    nj = max(1, ncol // 128)
                    M = min(ncol, 128)
                    for j in range(nj):
                        for n in range(NT):
                            if pre_tile is not None and j == 0:
                                pre_tile(n)
                            bank = swb[0]
                            swb[0] = (swb[0] + 1) % 4
                            for kc in range(KC):
                                kb.op("pe", lambda e, o=ps[bank][0:M, :], a=sv[:, kc, j * M:(j + 1) * M], r=h2[:, kc, n * 512:(n + 1) * 512], kc=kc:
                                      e.matmul(o, a, r, start=(kc == 0), stop=(kc == KC - 1)),
                                      reads=[slb, h2b[kc][n]], writes=[psb[bank]], inc=(kc == KC - 1))
                            epi(j, n, bank, M)

                s1b = [0]

                pend_f = []

                def flush_f():
                    while pend_f:
                        g, n, fs, fsb = pend_f.pop(0)
                        pst, pstb, pstd = pstg.next()
                        for bl in range(4):
                            b2 = 4 + s1b[0]
                            s1b[0] = (s1b[0] + 1) % 4
                            kb.op("pe", lambda e, o=ps[b2][:, 0:256], a=fs[:, bl * 128:(bl + 1) * 128]:
                                  e.matmul(o, a, cs128, start=True, stop=True),
                                  reads=[fsb, cbuf], writes=[psb[b2]])
                            kb.op("dve", lambda e, o=pst[:, bl, :], a=ps[b2][:, 0:256]: e.tensor_copy(out=o, in_=a),
                                  reads=[psb[b2]], writes=[pstb])
                        dstt = px_in[n] if n < 4 else pp
                        kb.dma("sp", dstt[:, g * 256:(g + 1) * 256].rearrange("(b p) c -> p b c", p=128), pst, pstd,
                               reads=[pstb], writes=[pxin_b[n] if n < 4 else pp_b])

                def epi_f(gbase):
                    def epi(j, n, bank, M):
                        g = gbase + j
                        fs, fsb, _ = stg.next()
                        kb.op("act", lambda e, o=fs, a=ps[bank][:, :]: e.activation(out=o, in_=a, func=AF.Copy),
                              reads=[psb[bank]], writes=[fsb])
                        flush_f()
                        pend_f.append((g, n, fs, fsb))
                    return epi

                fm_sweep(w_in[:, 0:256], 256, epi_f(0), pre_slabs[0],
                         pre_tile=lambda n: norm_mod(1, h2, h2b, sqr, rsr, tmr, 7,
                                                     tiles=([0] if n == 0 else []) + ([n + 1] if n + 1 < NT else [])))
                ar.release(mk2)
                fm_sweep(w_in[:, 256:512], 256, epi_f(2), pre_slabs[1])
                flush_f()
                d_ccp = [kb.dsem(f"ccp{i}") for i in range(4)]
                pxout_b = [Buf() for _ in range(4)]

                def emit_cc(i):
                    kb.raw("pool", lambda e, i=i: e.collective_compute("AllGather", ALU.bypass,
                                                                  replica_groups=[[0, 1], [2, 3], [4, 5], [6, 7]],
                                                                  ins=[px_in[i].ap().opt()], outs=[px_out[i].ap().opt()]),
                           (d_ccp[i], 1), reads=[pxin_b[i]], writes=[pxout_b[i]])


                def epi_qk(idx0, scale):
                    def epi(j, n, bank, M):
                        st_, stb, std = stg.next()
                        kb.op("act", lambda e, o=st_, a=ps[bank][:, :]: e.activation(out=o, in_=a, func=AF.Identity, scale=scale),
                              reads=[psb[bank]], writes=[stb])
                        kb.dma("sp", qkT[n * 4:(n + 1) * 4, :, idx0 + j, :].rearrange("b p t -> p b t"),
                               st_.rearrange("p (b t) -> p b t", b=4), std, reads=[stb], writes=[qkT_b])
                    return epi

                fm_sweep(w_in[:, 512:768], 256, epi_qk(0, SQ))
                emit_cc(0)
                fm_sweep(w_in[:, 768:1024], 256, epi_qk(2, SQ))
                fm_sweep(w_in[:, 1024:1280], 256, epi_qk(4, 1.0))
                emit_cc(1)
                fm_sweep(w_in[:, 1280:1536], 256, epi_qk(6, 1.0))
                for b in range(NB):
                    kst, kstb, kstd = kvst.next()
                    for c3 in range(3):
                        bank = swb[0]
                        swb[0] = (swb[0] + 1) % 4
                        for kc in range(KC):
                            kb.op("pe", lambda e, o=ps[bank][:, :], a=h2[:, kc, b * 128:(b + 1) * 128], r=wkv[:, kc, c3 * 512:(c3 + 1) * 512], kc=kc:
                                  e.matmul(o, a, r, start=(kc == 0), stop=(kc == KC - 1)),
                                  reads=[wkvb, h2b[kc][b // 4]], writes=[psb[bank]], inc=(kc == KC - 1))
                        if c3 % 2 == 0:
                            kb.op("act", lambda e, o=kst[:, c3 * 512:(c3 + 1) * 512], a=ps[bank][:, :]: e.activation(out=o, in_=a, func=AF.Copy),
                                  reads=[psb[bank]], writes=[kstb])
                        else:
                            kb.op("dve", lambda e, o=kst[:, c3 * 512:(c3 + 1) * 512], a=ps[bank][:, :]: e.tensor_copy(out=o, in_=a),
                                  reads=[psb[bank]], writes=[kstb])
                    kb.dma("sp", kvt[b], kst, kstd, reads=[kstb], writes=[kvt_b])
                ar.release(mk3)

                def epi_alr(d_):
                    def epi(j, n, bank, M):
                        kb.op("act", lambda e, o=alr[d_][0:16, n * 512:(n + 1) * 512], a=ps[bank][0:16, :]:
                              e.activation(out=o, in_=a, func=AF.Copy), reads=[psb[bank]], writes=[alrb[d_]])
                    return epi

                fm_sweep(w_alr[:, 0:16], 16, epi_alr(0))
                fm_sweep(w_alr[:, 16:32], 16, epi_alr(1))

                def epi_gate(c0, func):
                    def epi(j, n, bank, M):
                        st_, stb, std = stg.next()
                        kb.op("act", lambda e, o=st_, a=ps[bank][:, :]: e.activation(out=o, in_=a, func=func),
                              reads=[psb[bank]], writes=[stb])
                        kb.dma("sp", gsp[c0 + j, :, n * 512:(n + 1) * 512], st_, std, reads=[stb], writes=[gsp_b])
                    return epi

                for i in range(4):
                    fm_sweep(w_in[:, 2560 + i * 256:2560 + (i + 1) * 256], 256, epi_gate(2 * i, AF.Silu))
                    if i in (0, 2):
                        emit_cc(2 + i // 2)
                for i in range(8):
                    fm_sweep(w_in[:, 3616 + i * 256:3616 + (i + 1) * 256], 256, epi_gate(8 + 2 * i, AF.Sigmoid))
                ar.release(mk)
                if debug == "E":
                    raise _Stop()

                mk = ar.mark()
                wal = ar.alloc(2 * 512 * 2, BF16, "p (a c) -> p a c", a=2, parts=17)
                walb = ar.buf()
                d_wal = kb.dsem("wal")
                kb.dma("pool", wal, walpha.rearrange("a p c -> p a c"), d_wal, writes=[walb])
                Sr = Ring(ar, kb, "S", 2, 4096, F32, "p (h v) -> p h v", dma=False, h=4)
                Sbfr = Ring(ar, kb, "sbf", 2, 2048, BF16, "p (h v) -> p h v", dma=False, h=4)
                qkr = Ring(ar, kb, "qk", 2, 2048, BF16, "p (a t) -> p a t", a=8)
                kvr = Ring(ar, kb, "kv", 4, 3072, BF16)
                ltr = Ring(ar, kb, "lt", 2, 1024, BF16, dma=False)
                e1r = Ring(ar, kb, "e1", 2, 2048, F32, dma=False)
                e2r = Ring(ar, kb, "e2", 2, 2048, F32, dma=False)
                e3r = Ring(ar, kb, "e3", 2, 2048, F32, dma=False)
                decr = Ring(ar, kb, "dec", 3, 32, F32, dma=False)
                qtr = Ring(ar, kb, "qt", 3, 1024, BF16, "p (h t) -> p h t", dma=False, h=4)
                ktr = Ring(ar, kb, "kt", 2, 1024, BF16, "p (h t) -> p h t", dma=False, h=4)
                khr = Ring(ar, kb, "kh", 2, 1024, BF16, dma=False)
                atr = Ring(ar, kb, "at", 2, 1024, BF16, dma=False)
                o1r = Ring(ar, kb, "o1", 2, 4096, F32, "p (c t) -> p c t", c=8)
                osr = Ring(ar, kb, "os", 2, 4096, F32, "p (c t) -> p c t", dma=False, c=8)
                oqr = Ring(ar, kb, "oq", 1, 2048, BF16, "p (c t) -> p c t", dma=False, c=8)
                rs2r = Ring(ar, kb, "rs2", 1, 2048, F32, dma=False)
                ogr = Ring(ar, kb, "og", 2, 2048, BF16, "p (c t) -> p c t", c=8)
                sx2 = ar.alloc(8192, F32, "p (r c) -> p r c", r=2)
                sx2b = ar.buf()
                o1sp_b, ogsp_b = Buf(), Buf()
                d_st = kb.dsem("st")
                st_b = Buf()
                sxin_b, sxout_b = Buf(), Buf()
                d_sx = kb.dsem("sx")
                d_ccs = kb.dsem("ccs")
                d_si = kb.dsem("sinit")
                d_sx2 = kb.dsem("sx2")
                cur = {}

                def set_state(S, Sb):
                    cur["S"], cur["Sb"] = S, Sb
                    sbf, sbfb, _ = Sbfr.next()
                    kb.op("act", lambda e, o=sbf, S=S: e.activation(out=o, in_=S, func=AF.Copy), reads=[Sb], writes=[sbfb])
                    cur["sbf"], cur["sbfb"] = sbf, sbfb

                steps = []
                for b in range(16):
                    steps.append(dict(b=b, dn=0, init="sinit" if b == 0 else None, save="xchg" if b == 15 else None))
                for sq_ in range(2):
                    b0 = 16 + 2 * sq_
                    steps.append(dict(b=b0, dn=0, init="zero", save=None))
                    steps.append(dict(b=b0 + 1, dn=0, init=None, save=st1[sq_]))
                for sq_ in range(2):
                    b0 = 16 + 2 * sq_
                    steps.append(dict(b=b0 + 1, dn=1, init="zero", save=None))
                    steps.append(dict(b=b0, dn=1, init=None, save=st2[sq_]))
                for b in range(15, -1, -1):
                    steps.append(dict(b=b, dn=1, init="exch" if b == 15 else None, save=None))
                NS = len(steps)

                def st_at(t, lag):
                    i = t - lag
                    return steps[i] if 0 <= i < NS else None

                for t in range(NS + 8):
                    c0, c1, c2, c3, c4, c5, c6, c7 = (st_at(t, k) for k in range(8))
                    if c6 is not None:
                        c = c6
                        b, dn = c["b"], c["dn"]
                        last = 127 if dn == 0 else 0
                        if c["init"] == "sinit":
                            S0, S0b, _ = Sr.next()
                            kb.dma("sp", S0, sinit, d_si, writes=[S0b])
                            set_state(S0, S0b)
                        elif c["init"] == "zero":
                            S0, S0b, _ = Sr.next()
                            kb.op("dve", lambda e, S0=S0: e.memset(S0, 0.0), writes=[S0b])
                            set_state(S0, S0b)
                        elif c["init"] == "exch":
                            kb.dma("sp", sx2, sx_out[:, :].rearrange("(r p) c -> p r c", p=128), d_sx2, reads=[sxout_b], writes=[sx2b])
                            S3, S3b, _ = Sr.next()
                            Sf = S3.rearrange("p h v -> p (h v)")
                            kb.op("dve", lambda e, Sf=Sf: e.tensor_scalar(out=Sf, in0=sx2[:, 0, :], scalar1=fv[:, FV_SEL:FV_SEL + 1], scalar2=None, op0=ALU.mult),
                                  reads=[sx2b, cbuf], writes=[S3b])
                            kb.op("dve", lambda e, Sf=Sf: e.scalar_tensor_tensor(out=Sf, in0=sx2[:, 1, :], scalar=fv[:, FV_SEL + 1:FV_SEL + 2], in1=Sf, op0=ALU.mult, op1=ALU.add),
                                  reads=[sx2b, cbuf], writes=[S3b])
                            set_state(S3, S3b)
                        S, Sb, sbf, sbfb = cur["S"], cur["Sb"], cur["sbf"], cur["sbfb"]
                        kv, kvb, qt, qtb, at, atb, dec, decb = (c[k] for k in ("kv", "kvb", "qt", "qtb", "at", "atb", "dec", "decb"))
                        S2, S2b, _ = Sr.next()
                        for h in range(4):
                            bank = 6 + h // 2
                            col = (h % 2) * 256
                            kb.op("dve", lambda e, bank=bank, col=col, h=h, S=S, S2=S2, dec=dec: e.scalar_tensor_tensor(out=S2[:, h, :], in0=S[:, h, :], scalar=dec[:, h:h + 1],
                                                                                              in1=ps[bank][:, col:col + 256], op0=ALU.mult, op1=ALU.add),
                                  reads=[Sb, decb, psb[bank]], writes=[S2b])
                        for h in range(4):
                            bank = 4 + h // 2
                            for cc_ in range(2):
                                col = ((h % 2) * 2 + cc_) * 128
                                v0 = 512 + h * 256 + cc_ * 128
                                kb.op("pe", lambda e, bank=bank, col=col, v0=v0, h=h, kv=kv, at=at: e.matmul(ps[bank][:, col:col + 128], kv[:, v0:v0 + 128], at[:, h * 128:(h + 1) * 128], start=True, stop=False),
                                      reads=[kvb, atb], writes=[psb[bank]], inc=False)
                                kb.op("pe", lambda e, bank=bank, col=col, cc_=cc_, h=h, sbf=sbf, qt=qt: e.matmul(ps[bank][:, col:col + 128], sbf[:, h, cc_ * 128:(cc_ + 1) * 128], qt[:, h, :], start=False, stop=True),
                                      reads=[sbfb, qtb], writes=[psb[bank]], inc=(h % 2 == 1 and cc_ == 1))
                    if c7 is not None and c7["dn"] == 1:
                        c = c7
                        oq, oqb, _ = oqr.next()
                        kb.op("act", lambda e, o=oq, a=c["os"]: e.activation(out=o, in_=a, func=AF.Square), reads=[c["osb"]], writes=[oqb])
                        c["oq"], c["oqb"] = oq, oqb
                    if c5 is not None:
                        c = c5
                        dn = c["dn"]
                        qt, qtb, kt, ktb, kh, khb, kv, kvb = (c[k] for k in ("qt", "qtb", "kt", "ktb", "kh", "khb", "kv", "kvb"))
                        for h in range(4):
                            kb.op("pe", lambda e, h=h, kt=kt, qt=qt: e.matmul(ps[3][:, h * 128:(h + 1) * 128], kt[:, h, :], qt[:, h, :], start=True, stop=True),
                                  reads=[ktb, qtb], writes=[psb[3]], inc=(h == 3))
                        at, atb, _ = atr.next()
                        kb.op("dve", lambda e, o=at, dn=dn: e.tensor_tensor(out=o, in0=ps[3][:, :], in1=maskc[:, dn, :], op=ALU.mult),
                              reads=[psb[3], cbuf], writes=[atb])
                        c["at"], c["atb"] = at, atb
                        for h in range(4):
                            bank = 6 + h // 2
                            col = (h % 2) * 256
                            kb.op("pe", lambda e, bank=bank, col=col, h=h, kh=kh, kv=kv: e.matmul(ps[bank][:, col:col + 256], kh[:, h * 128:(h + 1) * 128], kv[:, 512 + h * 256:512 + (h + 1) * 256], start=True, stop=True),
                                  reads=[khb, kvb], writes=[psb[bank]], inc=(h % 2 == 1))
                        o1, o1b, o1d = o1r.next()
                        c.update(o1=o1, o1b=o1b, o1d=o1d, o1_loaded=True)
                        if dn == 1:
                            kb.dma("sp", o1, o1sp[c["b"]], o1d, reads=[o1sp_b], writes=[o1b])
                    if c3 is not None:
                        c = c3
                        b = c["b"]
                        e1, e1b, _ = e1r.next()
                        e2, e2b, _ = e2r.next()
                        e3, e3b, _ = e3r.next()
                        kb.op("act", lambda e, o=e1: e.activation(out=o, in_=ps[1][:, :], func=AF.Exp), reads=[psb[1]], writes=[e1b])
                        kb.op("act", lambda e, o=e2: e.activation(out=o, in_=ps[1][:, :], func=AF.Exp, scale=-1.0), reads=[psb[1]], writes=[e2b])
                        kb.op("act", lambda e, o=e3: e.activation(out=o, in_=ps[2][:, :], func=AF.Exp), reads=[psb[2]], writes=[e3b])
                        qk, qkb, qkd = qkr.next()
                        kb.dma("sp", qk, qkT[b], qkd, reads=[qkT_b], writes=[qkb])
                        kv, kvb, kvd = kvr.next()
                        kb.dma("sp", kv, kvt[b], kvd, reads=[kvt_b], writes=[kvb])
                        c.update(e1=e1, e1b=e1b, e2=e2, e2b=e2b, e3=e3, e3b=e3b, qk=qk, qkb=qkb, kv=kv, kvb=kvb)
                    if c2 is not None:
                        c = c2
                        dn = c["dn"]
                        lt, ltb = c["lt"], c["ltb"]
                        for h in range(4):
                            kb.op("pe", lambda e, h=h, lt=lt, dn=dn: e.matmul(ps[1][:, h * 128:(h + 1) * 128], lt[:, h * 128:(h + 1) * 128], trib[:, 2 * dn, :], start=True, stop=True),
                                  reads=[ltb, cbuf], writes=[psb[1]], inc=(h == 3))
                        kb.op("pe", lambda e, lt=lt, dn=dn: e.matmul(ps[2][:, :], trib[:, 2 * dn + 1, :], lt, start=True, stop=True),
                              reads=[ltb, cbuf], writes=[psb[2]])
                    if c1 is not None:
                        c = c1
                        lt, ltb, _ = ltr.next()
                        kb.op("act", lambda e: e.activation(out=ps[0][:, :], in_=ps[0][:, :], func=AF.Exp, scale=-1.0), reads=[psb[0]], writes=[psb[0]])
                        kb.op("act", lambda e, o=lt: e.activation(out=o, in_=ps[0][:, :], func=AF.Ln, bias=1.0, scale=1.0), reads=[psb[0]], writes=[ltb])
                        c["lt"], c["ltb"] = lt, ltb
                    if c6 is not None:
                        c = c6
                        b, dn = c["b"], c["dn"]
                        set_state(S2, S2b)
                        if c["save"] == "xchg":
                            kb.dma("sp", sx_in[:, :], S2.rearrange("p h v -> p (h v)"), d_sx, reads=[S2b], writes=[sxin_b])
                            kb.raw("pool", lambda e: e.collective_compute("AllGather", ALU.bypass,
                                                                          replica_groups=[[0, 1], [2, 3], [4, 5], [6, 7]],
                                                                          ins=[sx_in.ap().opt()], outs=[sx_out.ap().opt()]),
                                   (d_ccs, 1), reads=[sxin_b], writes=[sxout_b])
                        elif c["save"] is not None:
                            kb.dma("sp", c["save"], S2, d_st, reads=[S2b], writes=[st_b])
                        if dn == 0:
                            o1, o1b, o1d = c["o1"], c["o1b"], c["o1d"]
                            kb.op("act", lambda e, o=o1[:, 0:4, :]: e.activation(out=o, in_=ps[4][:, :].rearrange("p (c t) -> p c t", c=4), func=AF.Copy),
                                  reads=[psb[4]], writes=[o1b])
                            kb.op("act", lambda e, o=o1[:, 4:8, :]: e.activation(out=o, in_=ps[5][:, :].rearrange("p (c t) -> p c t", c=4), func=AF.Copy),
                                  reads=[psb[5]], writes=[o1b])
                            kb.dma("sp", o1sp[b], o1, o1d, reads=[o1b], writes=[o1sp_b])
                        else:
                            os_, osb, _ = osr.next()
                            o1, o1b = c["o1"], c["o1b"]
                            if not c["o1_loaded"]:
                                kb.dma("sp", o1, o1sp[b], c["o1d"], reads=[o1sp_b], writes=[o1b])
                            for hh in range(2):
                                kb.op("dve", lambda e, hh=hh, o=os_[:, hh * 4:hh * 4 + 4, :], o1=o1: e.tensor_tensor(out=o, in0=ps[4 + hh][:, :].rearrange("p (c t) -> p c t", c=4),
                                                                                                         in1=o1[:, hh * 4:hh * 4 + 4, :], op=ALU.add),
                                      reads=[psb[4 + hh], o1b], writes=[osb])
                            c["os"], c["osb"] = os_, osb
                    if c4 is not None:
                        c = c4
                        dn = c["dn"]
                        last = 127 if dn == 0 else 0
                        qk, qkb, kv, kvb, e1, e1b, e2, e2b, e3, e3b = (c[k] for k in ("qk", "qkb", "kv", "kvb", "e1", "e1b", "e2", "e2b", "e3", "e3b"))
                        qt, qtb, _ = qtr.next()
                        kt, ktb, _ = ktr.next()
                        kh, khb, _ = khr.next()
                        dec, decb, _ = decr.next()
                        kb.op("dve", lambda e, o=qt, qk=qk, e1=e1: e.tensor_tensor(out=o, in0=qk[:, 0:4, :], in1=e1.rearrange("p (h t) -> p h t", h=4), op=ALU.mult),
                              reads=[qkb, e1b], writes=[qtb])
                        kb.op("dve", lambda e, o=kt, qk=qk, e2=e2: e.tensor_tensor(out=o, in0=qk[:, 4:8, :], in1=e2.rearrange("p (h t) -> p h t", h=4), op=ALU.mult),
                              reads=[qkb, e2b], writes=[ktb])
                        kb.op("dve", lambda e, o=kh, kv=kv, e3=e3: e.tensor_tensor(out=o, in0=kv[:, 0:512], in1=e3, op=ALU.mult),
                              reads=[kvb, e3b], writes=[khb])
                        kb.op("dve", lambda e, o=dec[:, 0:4], e1=e1, last=last: e.tensor_copy(out=o, in_=e1.rearrange("p (h t) -> p h t", h=4)[:, :, last]),
                              reads=[e1b], writes=[decb])
                        c.update(qt=qt, qtb=qtb, kt=kt, ktb=ktb, kh=kh, khb=khb, dec=dec, decb=decb)
                    if c0 is not None:
                        c = c0
                        b, dn = c["b"], c["dn"]
                        kb.op("pe", lambda e, b=b, dn=dn: e.matmul(ps[0][:, :], alr[dn][0:17, b * 128:(b + 1) * 128], wal[:, dn, :], start=True, stop=True),
                              reads=[alrb[dn], walb], writes=[psb[0]])
                    if c7 is not None and c7["dn"] == 1:
                        c = c7
                        oq, oqb, os_, osb = c["oq"], c["oqb"], c["os"], c["osb"]
                        for h in range(4):
                            for cc_ in range(2):
                                kb.op("pe", lambda e, h=h, cc_=cc_, oq=oq: e.matmul(ps[3][:, h * 128:(h + 1) * 128], ones, oq[:, 2 * h + cc_, :], start=(cc_ == 0), stop=(cc_ == 1)),
                                      reads=[oqb, cbuf], writes=[psb[3]], inc=(h == 3 and cc_ == 1))
                        rs2, rs2b, _ = rs2r.next()
                        kb.op("act", lambda e, o=rs2: e.activation(out=o, in_=ps[3][:, :], func=AF.Ln, scale=1.0 / 256.0, bias=epsc),
                              reads=[psb[3], cbuf], writes=[rs2b])
                        kb.op("act", lambda e, o=rs2: e.activation(out=o, in_=o, func=AF.Exp, scale=-0.5), reads=[rs2b], writes=[rs2b])
                        os4 = os_.rearrange("p (h c) t -> p h c t", h=4)
                        rsb4 = rs2.rearrange("p (h t) -> p h t", h=4).unsqueeze(2).broadcast_to([128, 4, 2, 128])
                        og, ogb, ogd = ogr.next()
                        kb.op("dve", lambda e, o=og.rearrange("p (h c) t -> p h c t", h=4), a=os4, r=rsb4: e.tensor_tensor(out=o, in0=a, in1=r, op=ALU.mult),
                              reads=[osb, rs2b], writes=[ogb])
                        kb.dma("sp", ogsp[c["b"]], og, ogd, reads=[ogb], writes=[ogsp_b])
                ar.release(mkM)
                if debug == "F":
                    raise _Stop()

                mk = ar.mark()
                mT = ar.alloc(4 * T * 2, BF16, "p (g t) -> p g t", g=4)
                mTb = ar.buf()
                wpf = ar.alloc(4 * 1024 * 2, BF16, "p (g c) -> p g c", g=4)
                wpg = ar.alloc(8 * 1024 * 2, BF16, "p (k c) -> p k c", k=8)
                wo = ar.alloc(8 * 1024 * 2, BF16, "p (k c) -> p k c", k=8)
                wpb = ar.buf()
                d_wp = kb.dsem("wp")
                for i in range(2):
                    kb.dma("pool", wpf[:, :, i * 512:(i + 1) * 512], w_pf[:, i * 512:(i + 1) * 512].rearrange("(g p) c -> p g c", p=128), d_wp, writes=[wpb])
                for i in range(4):
                    kb.dma("pool", wpg[:, :, i * 256:(i + 1) * 256], w_pg[:, i * 256:(i + 1) * 256].rearrange("(k p) c -> p k c", p=128), d_wp, writes=[wpb])
                for i in range(4):
                    kb.dma("pool", wo[:, :, i * 256:(i + 1) * 256], w_out[:, i * 256:(i + 1) * 256].rearrange("(k p) c -> p k c", p=128), d_wp, writes=[wpb])
                for c8 in range(8):
                    kb.op("act", lambda e, c8=c8: e.activation(out=wpg[:, c8, :], in_=wpg[:, c8, :], func=AF.Identity, scale=fv[:, FV_GN + c8:FV_GN + c8 + 1]),
                          reads=[wpb, cbuf], writes=[wpb])
                mkg = ar.mark()
                pcr = Ring(ar, kb, "pc", 8, 2048, BF16)
                tbr = Ring(ar, kb, "tb", 8, 2048, BF16, "p (a t) -> p a t", a=2)
                SC_S = 1.0 / math.sqrt(4096.0 * 128.0)
                SC_P = 1.0 / math.sqrt(256.0 * 128.0)
                for n in range(4):
                    for rc in range(32):
                        pc, pcb, pcd = pcr.next()
                        ti, rk, cc_ = rc // 8, (rc % 8) // 4, rc % 4
                        r0 = rk * 512 + cc_ * 128
                        kb.dma("sp", pc, px_out[ti][r0:r0 + 128, :], pcd, reads=[pxout_b[ti]], writes=[pcb])
                        tb, tbb, tbd = tbr.next()
                        kb.dma("sp", tb, tabs_d[rc][n], tbd, writes=[tbb])
                        for g in range(4):
                            kb.op("pe", lambda e, g=g, pc=pc, tb=tb, rc=rc: e.matmul(ps[g][:, :], pc[:, g * 256:g * 256 + 128], tb[:, 0, :], start=(rc == 0), stop=False),
                                  reads=[pcb, tbb], writes=[psb[g]], inc=False)
                            kb.op("pe", lambda e, g=g, pc=pc, tb=tb, rc=rc: e.matmul(ps[g][:, :], pc[:, g * 256 + 128:g * 256 + 256], tb[:, 1, :], start=False, stop=(rc == 31)),
                                  reads=[pcb, tbb], writes=[psb[g]], inc=(g == 3))
                    for g in range(4):
                        kb.op("act", lambda e, g=g, n=n: e.activation(out=mT[:, g, n * 512:(n + 1) * 512], in_=ps[g][:, :], func=AF.Identity, scale=SC_S),
                              reads=[psb[g]], writes=[mTb])
                for sq_ in range(2):
                    for rc in range(2):
                        pc, pcb, pcd = pcr.next()
                        r0 = sq_ * 256 + rc * 128
                        kb.dma("sp", pc, pp[r0:r0 + 128, :], pcd, reads=[pp_b], writes=[pcb])
                        tb, tbb, tbd = tbr.next()
                        kb.dma("sp", tb[:, :, 0:256], tabp_d[rc], tbd, writes=[tbb])
                        for g in range(4):
                            kb.op("pe", lambda e, g=g, pc=pc, tb=tb, rc=rc: e.matmul(ps[4 + g][:, 0:256], pc[:, g * 256:g * 256 + 128], tb[:, 0, 0:256], start=(rc == 0), stop=False),
                                  reads=[pcb, tbb], writes=[psb[4 + g]], inc=False)
                            kb.op("pe", lambda e, g=g, pc=pc, tb=tb, rc=rc: e.matmul(ps[4 + g][:, 0:256], pc[:, g * 256 + 128:g * 256 + 256], tb[:, 1, 0:256], start=False, stop=(rc == 1)),
                                  reads=[pcb, tbb], writes=[psb[4 + g]], inc=(g == 3))
                    for g in range(4):
                        t0 = TS + sq_ * 256
                        kb.op("act", lambda e, g=g, t0=t0: e.activation(out=mT[:, g, t0:t0 + 256], in_=ps[4 + g][:, 0:256], func=AF.Identity, scale=SC_P),
                              reads=[psb[4 + g]], writes=[mTb])
                ar.release(mkg)
                if debug == "G":
                    raise _Stop()

                srtr = Ring(ar, kb, "srt", 1, 8192, BF16, "p (c t) -> p c t", c=8)
                ogt = ar.alloc(8 * 512 * 2, BF16, "p (c b t) -> p c b t", c=8, b=4)
                ogtb = ar.buf()
                d_ogt = kb.dsem("ogt")
                ypre = ar.alloc(8 * 512 * 2, BF16, "p (m t) -> p m t", m=8)
                ypb = [ar.buf() for _ in range(8)]
                gtr = Ring(ar, kb, "gt", 4, 1024, BF16)
                t1r = Ring(ar, kb, "t1", 2, 2048, F32, dma=False)
                t2r = Ring(ar, kb, "t2", 2, 2048, F32, dma=False)
                hb_ = [0, 0, 0]
                for n in range(NT):
                    for bl in range(4):
                        kb.dma("sp", ogt[:, :, bl, :], ogsp[n * 4 + bl], d_ogt, reads=[ogsp_b], writes=[ogtb])
                    ogv = ogt.rearrange("p c b t -> p c (b t)")
                    srt, srtb, srtd = srtr.next()
                    kb.dma("sp", srt, gsp[0:8, :, n * 512:(n + 1) * 512].rearrange("c p t -> p c t"), srtd, reads=[gsp_b], writes=[srtb])
                    kb.op("dve", lambda e, o=ogv, g_=srt: e.tensor_tensor(out=o, in0=o, in1=g_, op=ALU.mult), reads=[ogtb, srtb], writes=[ogtb])
                    for m in range(8):
                        ba = hb_[0] % 2
                        hb_[0] += 1
                        bb = 2 + hb_[1] % 2
                        hb_[1] += 1
                        for g in range(4):
                            kb.op("pe", lambda e, ba=ba, g=g, m=m, n=n: e.matmul(ps[ba][:, :], wpf[:, g, m * 128:(m + 1) * 128], mT[:, g, n * 512:(n + 1) * 512], start=(g == 0), stop=(g == 3)),
                                  reads=[wpb, mTb], writes=[psb[ba]], inc=(g == 3))
                        for c8 in range(8):
                            kb.op("pe", lambda e, bb=bb, c8=c8, m=m: e.matmul(ps[bb][:, :], wpg[:, c8, m * 128:(m + 1) * 128], ogv[:, c8, :], start=(c8 == 0), stop=(c8 == 7)),
                                  reads=[wpb, ogtb], writes=[psb[bb]], inc=(c8 == 7))
                        ga, gab, gad = gtr.next()
                        kb.dma("sp", ga, gsp[8 + m, :, n * 512:(n + 1) * 512], gad, reads=[gsp_b], writes=[gab])
                        gb_, gbb, gbd = gtr.next()
                        kb.dma("sp", gb_, gsp[16 + m, :, n * 512:(n + 1) * 512], gbd, reads=[gsp_b], writes=[gbb])
                        t1, t1b, _ = t1r.next()
                        t2, t2b, _ = t2r.next()
                        kb.op("dve", lambda e, o=t1, ba=ba, ga=ga: e.tensor_tensor(out=o, in0=ps[ba][:, :], in1=ga, op=ALU.mult),
                              reads=[psb[ba], gab], writes=[t1b])
                        kb.op("dve", lambda e, o=t2, bb=bb, gb_=gb_: e.tensor_tensor(out=o, in0=ps[bb][:, :], in1=gb_, op=ALU.mult),
                              reads=[psb[bb], gbb], writes=[t2b])
                        kb.op("dve", lambda e, m=m, t1=t1, t2=t2: e.tensor_tensor(out=ypre[:, m, :], in0=t1, in1=t2, op=ALU.add),
                              reads=[t1b, t2b], writes=[ypb[m]])
                    for m2 in range(8):
                        bk = 4 + hb_[2] % 3
                        hb_[2] += 1
                        for m in range(8):
                            kb.op("pe", lambda e, bk=bk, m=m, m2=m2: e.matmul(ps[bk][:, :], wo[:, m, m2 * 128:(m2 + 1) * 128], ypre[:, m, :], start=(m == 0), stop=(m == 7)),
                                  reads=[wpb, ypb[m]], writes=[psb[bk]], inc=(m == 7))
                        xs = x[:, m2, n * 512:(n + 1) * 512]
                        kb.op("dve", lambda e, o=xs, bk=bk, s=scal(5, tsel(n), m2): e.scalar_tensor_tensor(out=o, in0=ps[bk][:, :], scalar=s, in1=o, op0=ALU.mult, op1=ALU.add),
                              reads=[psb[bk], scb], writes=xbufs(m2, n))
                ar.release(mk)
                dump_x("mix")

                if debug != "mix":
                    ffn(2, wg2, wu2, wd2)
                    dump_x("ffn2")

                    mk = ar.mark()
                    sqr = Ring(ar, kb, "sq", 2, 1024, BF16, dma=False)
                    rsr = Ring(ar, kb, "rs", 2, 2048, F32, dma=False)
                    xnr = Ring(ar, kb, "xn", 2, 8 * 2048, F32, "p (m t) -> p m t", dma=False, m=8)
                    osg = Ring(ar, kb, "osg", 3, 4096, F32)
                    yb_ = Buf()
                    tb_ = [0]
                    def fin_prepare(n):
                        rs, rsb = rms_stats(n, sqr, rsr, 7)
                        xn, xnb, _ = xnr.next()
                        for m in range(KC):
                            kb.op("dve", lambda e, o=xn[:, m, :], a=x[:, m, n * 512:(n + 1) * 512], s=fv[:, FV_NF + m:FV_NF + m + 1], r=rs:
                                  e.scalar_tensor_tensor(out=o, in0=a, scalar=s, in1=r, op0=ALU.mult, op1=ALU.mult),
                                  reads=xbufs(m, n) + [rsb, cbuf], writes=[xnb])
                        return xn, xnb

                    def fin_emit(n, xn, xnb):
                        for bl in range(4):
                            og_, ogb_, ogd_ = osg.next()
                            for hh in range(2):
                                bank = tb_[0] % 4
                                tb_[0] += 1
                                for i in range(4):
                                    m = hh * 4 + i
                                    kb.op("pe", lambda e, bank=bank, i=i, m=m, xn=xn, bl=bl: e.transpose(ps[bank][:, i * 128:(i + 1) * 128], xn[:, m, bl * 128:(bl + 1) * 128], ident),
                                          reads=[xnb, cbuf], writes=[psb[bank]], inc=(i == 3))
                                if hh == 0:
                                    kb.op("act", lambda e, o=og_[:, 0:512], bank=bank: e.activation(out=o, in_=ps[bank][:, :], func=AF.Copy),
                                          reads=[psb[bank]], writes=[ogb_])
                                else:
                                    kb.op("dve", lambda e, o=og_[:, 512:1024], bank=bank: e.tensor_copy(out=o, in_=ps[bank][:, :]),
                                          reads=[psb[bank]], writes=[ogb_])
                            r0 = (n * 4 + bl) * 128
                            kb.dma("sp", yout[r0:r0 + 128, :], og_, ogd_, reads=[ogb_], writes=[yb_])

                    cur_f = fin_prepare(0)
                    for n in range(NT):
                        nxt_f = fin_prepare(n + 1) if n + 1 < NT else None
                        fin_emit(n, *cur_f)
                        cur_f = nxt_f
                    ar.release(mk)

        try:
            _mixer_and_rest()
        except _Stop:
            dump_x(debug)

        kb.wait_all("sp")
        kb.replay(block)
    return nc


def _bf16(a):
    return np.asarray(a, dtype=np.float32).astype(ml_dtypes.bfloat16)


def _grid_pos():
    rows = 4096 // 64
    row = np.repeat(np.arange(rows, dtype=np.float32), 64)
    col = np.tile(np.arange(64, dtype=np.float32), rows)
    n_freq = D // 4
    omega = (np.float32(10000.0) ** (-np.arange(n_freq, dtype=np.float32) / np.float32(n_freq))).astype(np.float32)
    ra = row[:, None] * omega
    ca = col[:, None] * omega
    return np.concatenate([np.sin(ra), np.cos(ra), np.sin(ca), np.cos(ca)], axis=-1).astype(np.float32)


def _consts():
    ident = np.eye(128, dtype=np.float32)
    i = np.arange(128)
    L1 = (i[:, None] <= i[None, :]).astype(np.float32)
    U1 = (i[:, None] > i[None, :]).astype(np.float32)
    L2 = (i[:, None] >= i[None, :]).astype(np.float32)
    U2 = (i[:, None] < i[None, :]).astype(np.float32)
    tri = np.stack([L1, U1, L2, U2], 1) * np.float32(-1.0 / 16.0)
    mask = np.stack([np.tile(L1, (1, 4)), np.tile(L2, (1, 4))], 1)
    c = np.arange(128)
    ang = 2 * np.pi * ((c[:, None] * c[None, :]) % 128) / 128.0
    cs128 = np.concatenate([np.cos(ang), -np.sin(ang)], 1)
    return ident, tri.astype(np.float32), _bf16(mask), _bf16(cs128)


def _tables(flip):
    rc = np.arange(32)[:, None]
    p = np.arange(128)[None, :]
    local = (rc // 8) * 512 + (rc % 4) * 128 + p
    rpos = np.where(((rc % 8) // 4) == 0, local, 4095 - local).reshape(4096)
    j = np.arange(2048)
    cpos = (4095 - j) if flip else j
    ang = 2 * np.pi * ((rpos[:, None].astype(np.int64) * cpos[None, :]) % 4096) / 4096.0
    tabs = np.stack([np.cos(ang), np.sin(ang)], 1)
    tabs = np.ascontiguousarray(tabs.reshape(32, 128, 2, 4, 512).transpose(0, 3, 1, 2, 4))
    rp = np.arange(256)
    ppos = (255 - rp) if flip else rp
    angp = 2 * np.pi * ((ppos[:, None] * ppos[None, :]) % 256) / 256.0
    tabp = np.stack([np.cos(angp), np.sin(angp)], 1).reshape(2, 128, 2, 256)
    return _bf16(tabs), _bf16(tabp)


def _fm(vec):
    return np.ascontiguousarray(np.asarray(vec, np.float32).reshape(-1, 128).T)


def make_in_maps(inp):
    f32 = lambda a: np.ascontiguousarray(np.asarray(a, dtype=np.float32))
    pos = _grid_pos()
    ident, tri, mask, cs128 = _consts()
    tabs = [_tables(False), _tables(True)]
    shared = {
        "w_ada": f32(inp["w_ada"][0]),
        "w_ffn1_gate": f32(inp["w_ffn1_gate"][0]), "w_ffn1_up": f32(inp["w_ffn1_up"][0]), "w_ffn1_down": f32(inp["w_ffn1_down"][0]),
        "w_ffn2_gate": f32(inp["w_ffn2_gate"][0]), "w_ffn2_up": f32(inp["w_ffn2_up"][0]), "w_ffn2_down": f32(inp["w_ffn2_down"][0]),
        "w_in": f32(inp["w_in"][0]),
        "w_proj_fourier": f32(inp["w_proj_fourier"][0]), "w_proj_gla": f32(inp["w_proj_gla"][0]), "w_out": f32(inp["w_out"][0]),
        "ident": ident, "tri": tri, "maskc": mask, "cs128": cs128,
    }
    w_in = shared["w_in"]
    alr_f, alr_b = w_in[:, 3584:3600], w_in[:, 3600:3616]
    wa_f = np.concatenate([f32(inp["w_alpha_fwd"][0]), f32(inp["b_alpha_fwd"][0])[None]], 0)
    wa_b = np.concatenate([f32(inp["w_alpha_bwd"][0]), f32(inp["b_alpha_bwd"][0])[None]], 0)
    b_ada = _fm(inp["b_ada"][0])
    maps = []
    for c in range(8):
        b, half = c // 2, c % 2
        flip = half == 1
        sl = slice(half * TS, (half + 1) * TS)
        xs = f32(inp["x_sample"][b, sl])
        ps_ = pos[sl]
        xp = [f32(inp["x_prompt"][2 * c]), f32(inp["x_prompt"][2 * c + 1])]
        if flip:
            xs, ps_ = xs[::-1], ps_[::-1]
            xp = [a[::-1] for a in xp]
        xin = np.ascontiguousarray(np.concatenate([xs] + xp, 0))
        posT = np.ascontiguousarray(ps_.reshape(16, 128, KC, 128).transpose(0, 3, 2, 1))
        st = inp["state_gla_bwd"] if flip else inp["state_gla_fwd"]
        sinit = np.ascontiguousarray(f32(st[b, 0]).transpose(1, 0, 2))
        cc = np.stack([f32(inp["c"][b]), f32(inp["c_ctx"])], 0)
        cT = np.ascontiguousarray(cc.reshape(2, KC, 128).transpose(2, 1, 0))
        fvec = np.zeros((128, FV_N), np.float32)
        fvec[:, FV_BADA:FV_BADA + 144] = np.repeat(b_ada, 2, axis=1)
        fvec[:, FV_N1:FV_N1 + 8] = _fm(inp["norm_ffn1"][0])
        fvec[:, FV_N2:FV_N2 + 8] = _fm(inp["norm_mix"][0])
        fvec[:, FV_N3:FV_N3 + 8] = _fm(inp["norm_ffn2"][0])
        fvec[:, FV_NF:FV_NF + 8] = _fm(inp["final_norm"])
        fvec[:, FV_GN:FV_GN + 8] = _fm(inp["gla_norm"][0])
        fvec[:, FV_SEL:FV_SEL + 2] = np.array([1.0, 0.0] if flip else [0.0, 1.0], np.float32)
        m = dict(shared)
        m.update({
            "xin": xin, "posT": posT, "sinit": sinit, "cT": cT, "fvec": fvec,
            "w_alr": np.ascontiguousarray(np.concatenate([alr_b, alr_f] if flip else [alr_f, alr_b], 1)),
            "walpha": np.ascontiguousarray(np.stack([wa_b, wa_f] if flip else [wa_f, wa_b], 0)),
            "tabs": tabs[half][0], "tabp": tabs[half][1],
        })
        maps.append(m)
    return maps


def assemble(results):
    y_prompt = np.zeros((16, 256, D), np.float32)
    y_sample = np.zeros((4, 4096, D), np.float32)
    nsf = np.zeros((16, 1, 4, 128, 256), np.float32)
    nsb = np.zeros((16, 1, 4, 128, 256), np.float32)
    for c in range(8):
        r = results[c]
        b, half = c // 2, c % 2
        flip = half == 1
        y = r["yout"]
        ys, yp = y[:TS], [y[TS:TS + 256], y[TS + 256:]]
        if flip:
            ys = ys[::-1]
            yp = [a[::-1] for a in yp]
        y_sample[b, half * TS:(half + 1) * TS] = ys
        for s in range(2):
            y_prompt[2 * c + s] = yp[s]
            a1 = r["st1"][s].transpose(1, 0, 2)
            a2 = r["st2"][s].transpose(1, 0, 2)
            if flip:
                a1, a2 = a2, a1
            nsf[2 * c + s, 0] = a1
            nsb[2 * c + s, 0] = a2
    return y_prompt, y_sample, nsf, nsb


def kernel(**inputs):
    nc = build_nc()
    in_maps = make_in_maps(inputs)
    res = run_bass_kernel_spmd(nc, in_maps, core_ids=list(range(8)))
    return assemble(res.results)
```
